# Optimizing a Trainium2 kernel written in Bass

```python
import jax, jax.numpy as jnp
from jax import lax
import numpy as np

D_MODEL = 2048
BATCH = 16
SEQ = 2048
DEPTH = 4

GLA_HEADS = 4
GLA_DK = D_MODEL // 2
GLA_DV = D_MODEL
GLA_HEAD_K = GLA_DK // GLA_HEADS
GLA_HEAD_V = GLA_DV // GLA_HEADS
GLA_GATE_RANK = 16
GLA_GATE_NORMALIZER = 16.0
GLA_CHUNK = 64
GLA_SUBCHUNK = 16
GLA_SPLITS = (GLA_DK, 2 * GLA_DK, 2 * GLA_DK + GLA_DV, 2 * GLA_DK + 2 * GLA_DV)
GLA_IN = 2 * GLA_DK + 2 * GLA_DV + GLA_GATE_RANK

DIL_PATTERNS = ((128, 1), (512, 4), (2048, 16))
DIL_GROUPS = len(DIL_PATTERNS)
DIL_HEADS = 8
DIL_HEAD_DIM = 128
DIL_WIDTH = DIL_HEADS * DIL_HEAD_DIM
DIL_BLOCK = 128
DIL_IN = 3 * DIL_GROUPS * DIL_WIDTH

D_FF = 5504
CONV_WIDTH = 3

N_GLA_LAYERS = (DEPTH + 1) // 2
N_DIL_LAYERS = DEPTH // 2

DEEPNORM_ALPHA = (2 * DEPTH) ** 0.25
DEEPNORM_BETA = (8 * DEPTH) ** -0.25
LN_EPS = 1e-5
RMS_EPS = 1e-6

kernel_name = "hybrid_gla_dilated_convffn_deepnorm"


def layer_norm(x, g, b):
    xf = x.astype(jnp.float32)
    mu = xf.mean(-1, keepdims=True)
    var = jnp.square(xf - mu).mean(-1, keepdims=True)
    return ((xf - mu) * lax.rsqrt(var + LN_EPS)).astype(x.dtype) * g + b


def gla_chunk_step(state, inp):
    q, k, v, g = inp
    Bb, H, C, dk = q.shape
    c = GLA_SUBCHUNK
    n = C // c
    b = jnp.cumsum(g, axis=2)
    b_last = b[:, :, -1:, :]
    o_inter = jnp.einsum('bhcd,bhde->bhce', q * jnp.exp(b), state)
    qs = q.reshape(Bb, H, n, c, dk)
    ks = k.reshape(Bb, H, n, c, dk)
    vs = v.reshape(Bb, H, n, c, -1)
    bs = b.reshape(Bb, H, n, c, dk)
    b_ref = jnp.concatenate([jnp.zeros_like(bs[:, :, :1, -1]), bs[:, :, :-1, -1]], axis=2)
    q_ref = qs * jnp.exp(bs - b_ref[:, :, :, None, :])
    earlier = jnp.arange(C)[None, :] < (jnp.arange(n) * c)[:, None]
    k_exp = jnp.where(earlier[None, None, :, :, None],
                      b_ref[:, :, :, None, :] - b[:, :, None, :, :], -jnp.inf)
    k_ref = k[:, :, None] * jnp.exp(k_exp)
    a_inter = jnp.einsum('bhsid,bhsjd->bhsij', q_ref, k_ref)
    causal = jnp.tril(jnp.ones((c, c), dtype=bool))
    d_exp = jnp.where(causal[:, :, None],
                      bs[:, :, :, :, None, :] - bs[:, :, :, None, :, :], -jnp.inf)
    a_intra = jnp.einsum('bhsid,bhsjd,bhsijd->bhsij', qs, ks, jnp.exp(d_exp))
    o_intra = (jnp.einsum('bhsij,bhje->bhsie', a_inter, v)
               + jnp.einsum('bhsij,bhsje->bhsie', a_intra, vs)).reshape(Bb, H, C, -1)
    new_state = (jnp.exp(b_last[:, :, 0, :])[..., None] * state
                 + jnp.einsum('bhcd,bhce->bhde', k * jnp.exp(b_last - b), v))
    return new_state, o_inter + o_intra


def gla_mixer(x, w_in, w_gate_up, gate_bias, norm_g, w_out):
    B, S, _ = x.shape
    H, dk, dv, C = GLA_HEADS, GLA_HEAD_K, GLA_HEAD_V, GLA_CHUNK
    proj = x @ w_in
    q, k, v, r, g_low = jnp.split(proj, list(GLA_SPLITS), axis=-1)
    log_gate = jax.nn.log_sigmoid((g_low @ w_gate_up + gate_bias).astype(jnp.float32)) / GLA_GATE_NORMALIZER
    q = q.astype(jnp.float32) * dk ** -0.5

    def to_chunks(t, d):
        return t.astype(jnp.float32).reshape(B, S // C, C, H, d).transpose(1, 0, 3, 2, 4)

    xs = (to_chunks(q, dk), to_chunks(k, dk), to_chunks(v, dv), to_chunks(log_gate, dk))
    state0 = jnp.zeros((B, H, dk, dv), jnp.float32)
    _, o = lax.scan(gla_chunk_step, state0, xs)
    o = o.transpose(1, 0, 3, 2, 4).reshape(B, S, H, dv)
    o = o * lax.rsqrt(jnp.mean(jnp.square(o), -1, keepdims=True) + RMS_EPS) * norm_g
    o = o.reshape(B, S, GLA_DV).astype(x.dtype) * jax.nn.silu(r)
    return o @ w_out


def banded_attention(q, k, v, steps):
    N, L, H, dh = q.shape
    P = DIL_BLOCK
    nb = -(-L // P)
    Lp = nb * P
    pad = ((0, 0), (0, Lp - L), (0, 0), (0, 0))
    q, k, v = (jnp.pad(t, pad).reshape(N, nb, P, H, dh) for t in (q, k, v))

    def with_prev(t):
        prev = jnp.concatenate([jnp.zeros_like(t[:, :1]), t[:, :-1]], axis=1)
        return jnp.concatenate([prev, t], axis=2)

    kw, vw = with_prev(k), with_prev(v)
    s = jnp.einsum('nbqhd,nbkhd->nbhqk', q, kw).astype(jnp.float32) * dh ** -0.5
    qi = jnp.arange(P)[:, None] + P
    kj = jnp.arange(2 * P)[None, :]
    dist = qi - kj
    band = (dist >= 0) & (dist <= steps)
    real_key = (jnp.arange(nb)[:, None, None] > 0) | (kj[None] >= P)
    mask = band[None] & real_key
    s = jnp.where(mask[None, :, None], s, -jnp.inf)
    m = s.max(-1, keepdims=True)
    p = jnp.exp(s - m)
    l = p.sum(-1, keepdims=True)
    o = jnp.einsum('nbhqk,nbkhd->nbqhd', (p / l).astype(v.dtype), vw)
    lse = (m + jnp.log(l))[..., 0]
    o = o.reshape(N, Lp, H, dh)[:, :L]
    lse = lse.transpose(0, 1, 3, 2).reshape(N, Lp, H)[:, :L]
    return o, lse


def dilated_group(q, k, v, window, dilation):
    B, S, H, dh = q.shape
    L = S // dilation

    def to_strided(t):
        return t.reshape(B, L, dilation, H, dh).transpose(0, 2, 1, 3, 4).reshape(B * dilation, L, H, dh)

    o, lse = banded_attention(to_strided(q), to_strided(k), to_strided(v), window // dilation)
    o = o.reshape(B, dilation, L, H, dh).transpose(0, 2, 1, 3, 4).reshape(B, S, H, dh)
    lse = lse.reshape(B, dilation, L, H).transpose(0, 2, 1, 3).reshape(B, S, H)
    return o, lse


def dilated_mixer(x, w_in, w_out):
    B, S, _ = x.shape
    proj = (x @ w_in).reshape(B, S, DIL_GROUPS, 3, DIL_HEADS, DIL_HEAD_DIM)
    outs, lses = [], []
    for gi, (window, dilation) in enumerate(DIL_PATTERNS):
        o, lse = dilated_group(proj[:, :, gi, 0], proj[:, :, gi, 1], proj[:, :, gi, 2], window, dilation)
        outs.append(o)
        lses.append(lse)
    wts = jax.nn.softmax(jnp.stack(lses, 0), axis=0)
    o = jnp.einsum('gbsh,gbshd->bshd', wts.astype(x.dtype), jnp.stack(outs, 0))
    return o.reshape(B, S, DIL_WIDTH) @ w_out


def conv_ffn(x, w_up, conv_w, conv_b, w_down):
    S = x.shape[1]
    h = x @ w_up
    hp = jnp.pad(h, ((0, 0), (CONV_WIDTH - 1, 0), (0, 0)))
    h = sum((conv_w[j] * hp[:, j:j + S] for j in range(CONV_WIDTH)), conv_b)
    gate, up = jnp.split(h, 2, axis=-1)
    return (jax.nn.silu(gate) * up) @ w_down


def setup_inputs(seed: int = 0) -> dict:
    key = jax.random.key(seed)
    ks = jax.random.split(key, 16)
    nrm = lambda k, shape, scale: jax.random.normal(k, shape, jnp.float32) * scale
    return {
        "x": nrm(ks[0], (BATCH, SEQ, D_MODEL), 1.0),
        "gla_w_in": nrm(ks[1], (N_GLA_LAYERS, D_MODEL, GLA_IN), D_MODEL ** -0.5),
        "gla_w_gate_up": nrm(ks[2], (N_GLA_LAYERS, GLA_GATE_RANK, GLA_DK), GLA_GATE_RANK ** -0.5),
        "gla_gate_bias": nrm(ks[3], (N_GLA_LAYERS, GLA_DK), 0.1),
        "gla_norm_g": 1.0 + nrm(ks[4], (N_GLA_LAYERS, GLA_HEAD_V), 0.02),
        "gla_w_out": nrm(ks[5], (N_GLA_LAYERS, GLA_DV, D_MODEL), GLA_DV ** -0.5 * DEEPNORM_BETA),
        "dil_w_in": nrm(ks[6], (N_DIL_LAYERS, D_MODEL, DIL_IN), D_MODEL ** -0.5),
        "dil_w_out": nrm(ks[7], (N_DIL_LAYERS, DIL_WIDTH, D_MODEL), DIL_WIDTH ** -0.5 * DEEPNORM_BETA),
        "ffn_w_up": nrm(ks[8], (DEPTH, D_MODEL, 2 * D_FF), D_MODEL ** -0.5),
        "ffn_conv_w": nrm(ks[9], (DEPTH, CONV_WIDTH, 2 * D_FF), CONV_WIDTH ** -0.5),
        "ffn_conv_b": nrm(ks[10], (DEPTH, 2 * D_FF), 0.02),
        "ffn_w_down": nrm(ks[11], (DEPTH, D_FF, D_MODEL), D_FF ** -0.5 * DEEPNORM_BETA),
        "ln_g": 1.0 + nrm(ks[12], (DEPTH, 2, D_MODEL), 0.02),
        "ln_b": nrm(ks[13], (DEPTH, 2, D_MODEL), 0.02),
    }


def reference(x, gla_w_in, gla_w_gate_up, gla_gate_bias, gla_norm_g, gla_w_out,
              dil_w_in, dil_w_out, ffn_w_up, ffn_conv_w, ffn_conv_b, ffn_w_down, ln_g, ln_b):
    for i in range(DEPTH):
        j = i // 2
        if i % 2 == 0:
            mix = gla_mixer(x, gla_w_in[j], gla_w_gate_up[j], gla_gate_bias[j], gla_norm_g[j], gla_w_out[j])
        else:
            mix = dilated_mixer(x, dil_w_in[j], dil_w_out[j])
        x = layer_norm(DEEPNORM_ALPHA * x + mix, ln_g[i, 0], ln_b[i, 0])
        ffn = conv_ffn(x, ffn_w_up[i], ffn_conv_w[i], ffn_conv_b[i], ffn_w_down[i])
        x = layer_norm(DEEPNORM_ALPHA * x + ffn, ln_g[i, 1], ln_b[i, 1])
    return x
```

```python
import contextlib
from collections import deque
import numpy as np
import concourse.bass as bass
import concourse.mybir as mybir
from concourse.bass_utils import run_bass_kernel_spmd

F32 = mybir.dt.float32
BF16 = mybir.dt.bfloat16
AF = mybir.ActivationFunctionType
ALU = mybir.AluOpType

D = 2048
SEQ = 2048
DFF = 5504
NFC = 43
ALPHA = float(8 ** 0.25)
NCORES = 8
NSEQ_CORE = 2

ENGS = ["pe", "act", "dve", "pool", "sp"]
NDMASEM = 16
EPOCH = 30000
NWB = 6
ARENA_BYTES = 211968

NCV = 1640
C_LNG, C_LNB, C_CW, C_CB, C_NG = 0, 128, 256, 1288, 1632


def K(*a):
    return "_".join(map(str, a))


class Sched:
    def __init__(self, nc):
        self.nc = nc
        self.ops = []
        self.last_w = {}
        self.readers = {}
        self.last_comp = {}
        self.recent_dma = {e: deque(maxlen=NDMASEM) for e in ENGS}
        self.bar_deps = set()
        self.bar_pending = set()

    def barrier(self):
        deps = set()
        for e in ENGS:
            if e in self.last_comp:
                deps.add(self.last_comp[e])
            deps.update(self.recent_dma[e])
        self.bar_deps = deps
        self.bar_pending = set(ENGS)
        self.last_w = {}
        self.readers = {}

    def op(self, eng, fn, reads=(), writes=(), dma=False):
        i = len(self.ops)
        deps = set()
        for k in reads:
            w = self.last_w.get(k)
            if w is not None:
                deps.add(w)
        for k in writes:
            w = self.last_w.get(k)
            if w is not None:
                deps.add(w)
            rs = self.readers.get(k)
            if rs:
                deps.update(rs)
        for k in reads:
            self.readers.setdefault(k, []).append(i)
        for k in writes:
            self.last_w[k] = i
            self.readers[k] = []
        if eng in self.bar_pending:
            deps |= self.bar_deps
            self.bar_pending.discard(eng)
        deps.discard(i)
        self.ops.append((eng, fn, deps, dma))
        if dma:
            self.recent_dma[eng].append(i)
        else:
            self.last_comp[eng] = i
        return i

    def pe(self, fn, reads=(), writes=()):
        return self.op("pe", fn, reads, writes)

    def act(self, fn, reads=(), writes=()):
        return self.op("act", fn, reads, writes)

    def dve(self, fn, reads=(), writes=()):
        return self.op("dve", fn, reads, writes)

    def dma(self, q, fn, reads=(), writes=()):
        return self.op(q, fn, reads, writes, dma=True)

    def emit(self):
        nc = self.nc
        ops = self.ops
        n = len(ops)
        need_inc = [False] * n
        for i, (eng, fn, deps, dma) in enumerate(ops):
            for d in deps:
                deng, _, _, ddma = ops[d]
                if ddma:
                    continue
                if deng == "pe" and eng == "pe" and not dma:
                    continue
                need_inc[d] = True
        cnt = {e: 0 for e in ENGS}
        dcnt = {e: 0 for e in ENGS}
        sig = [None] * n
        for i, (eng, fn, deps, dma) in enumerate(ops):
            if dma:
                k = dcnt[eng]
                dcnt[eng] += 1
                sig[i] = (("d", eng, k % NDMASEM), 16 * (k // NDMASEM + 1))
            elif need_inc[i]:
                c = cnt[eng]
                cnt[eng] += 1
                sig[i] = (("c", eng, c // EPOCH), (c % EPOCH) + 1)
        semkeys = sorted({s[0] for s in sig if s is not None})
        with contextlib.ExitStack() as st:
            sems = {}
            for k in semkeys:
                sems[k] = st.enter_context(nc.semaphore("s_" + "_".join(map(str, k))))
            block = st.enter_context(nc.Block())

            def run_engine(ename, e):
                waited = {}
                for i, (eng, fn, deps, dma) in enumerate(ops):
                    if eng != ename:
                        continue
                    wl = {}
                    for d in deps:
                        deng, _, _, ddma = ops[d]
                        if (not ddma) and deng == "pe" and eng == "pe" and not dma:
                            continue
                        sk, v = sig[d]
                        if waited.get(sk, 0) >= v:
                            continue
                        if wl.get(sk, 0) < v:
                            wl[sk] = v
                    if dma:
                        sk, v = sig[i]
                        if v > 16 and waited.get(sk, 0) < v - 16 and wl.get(sk, 0) < v - 16:
                            wl[sk] = v - 16
                    for sk, v in wl.items():
                        e.wait_ge(sems[sk], v)
                        waited[sk] = v
                    ins = fn(e)
                    if sig[i] is not None:
                        ins.then_inc(sems[sig[i][0]], 16 if dma else 1)

            @block.tensor
            def _(e):
                run_engine("pe", e)

            @block.scalar
            def _(e):
                run_engine("act", e)

            @block.vector
            def _(e):
                run_engine("dve", e)

            @block.gpsimd
            def _(e):
                run_engine("pool", e)

            @block.sync
            def _(e):
                run_engine("sp", e)


class Alloc:
    def __init__(self, arena, base=0):
        self.a = arena
        self.off = base

    def get(self, shape, dt, parts=128):
        n = int(np.prod(shape))
        nb = n * (4 if dt == F32 else 2)
        start = self.off
        self.off += (nb + 63) // 64 * 64
        assert self.off <= ARENA_BYTES, ("arena overflow", self.off)
        ap = self.a[0:parts, start // 2:(start + nb) // 2]
        if dt == F32:
            ap = ap.bitcast(F32)
        if len(shape) == 2:
            ap = ap.rearrange("p (a b) -> p a b", b=shape[1])
        elif len(shape) == 3:
            ap = ap.rearrange("p (a b c) -> p a b c", b=shape[1], c=shape[2])
        return ap


def build(NSEQ=NSEQ_CORE, NSUB=8):
    NT = NSEQ * SEQ
    nc = bass.Bass("TRN2", target_bir_lowering=False)

    def din(name, shape):
        return nc.dram_tensor(name, list(shape), F32, kind="ExternalInput").ap()

    xT = din("xT", [D, NT])
    gla_w_in = din("gla_w_in", [2, D, 6160])
    gla_w_out = din("gla_w_out", [2, D, D])
    dil_w_in = din("dil_w_in", [2, D, 9216])
    dil_w_out = din("dil_w_out", [2, 1024, D])
    ffn_w_up = din("ffn_w_up", [4, D, 11008])
    ffn_w_down = din("ffn_w_down", [4, DFF, D])
    cvec_d = din("cvec", [128, NCV])
    cmat_d = din("cmat", [128, 896])
    wg_d = din("wg_aug", [2, 17, 1024])
    outT = nc.dram_tensor("outT", [D, NT], F32, kind="ExternalOutput").ap()
    X32 = nc.dram_tensor("X32", [D, NT], F32).ap()
    XB = [nc.dram_tensor(f"XB{i}", [D, NT], BF16).ap() for i in range(2)]

    with contextlib.ExitStack() as st:
        arena = st.enter_context(nc.sbuf_tensor("arena", [128, ARENA_BYTES // 2], BF16))
        ps = [st.enter_context(nc.psum_tensor(f"ps{i}", [128, 512], F32)) for i in range(8)]
        S = Sched(nc)

        cm = Alloc(arena, 0)
        cvec = cm.get([NCV], F32)
        cm32 = cm.get([3, 128], F32)
        tri = cm32[:, 0, :]
        masks = cm32[:, 1:3, :]
        mcur = cm32[:, 2, :]
        cb16 = cm.get([4, 128], BF16)
        ident, onesD, ones512, ones1 = cb16[:, 0, :], cb16[:, 1, :], cb16[:, 2, :], cb16[:, 3, :]
        wpool = [cm.get([2048], BF16) for _ in range(NWB)]
        XR = [cm.get([512], F32) for _ in range(3)]
        ybf = cm.get([512], BF16)
        ysq = cm.get([512], BF16)
        mean_t = cm.get([512], F32)
        tmp_t = cm.get([512], F32)
        rstd_t = [cm.get([512], F32) for _ in range(2)]
        nmr_t = [cm.get([512], F32) for _ in range(2)]
        YL = [cm.get([512], F32) for _ in range(3)]
        OBF = [cm.get([512], BF16) for _ in range(2)]
        PHASE_BASE = cm.off

        S.dma("sp", lambda e: e.dma_start(out=cvec, in_=cvec_d), writes=["cvec"])
        S.dma("sp", lambda e: e.dma_start(out=cm32, in_=cmat_d[:, 0:384].rearrange("p (a b) -> p a b", b=128)), writes=["cm32"])
        S.dma("pool", lambda e: e.dma_start(out=cb16, in_=cmat_d[:, 384:896].rearrange("p (a b) -> p a b", b=128)), writes=["cb16"])

        ctr = {"wb": 0, "bank": 0, "xr": 0, "yl": 0, "obf": 0}

        def wload(Wd, k0, nk, col0, ncols):
            i = ctr["wb"] % NWB
            ctr["wb"] += 1
            view = wpool[i][:, 0:nk * ncols].rearrange("p (a b) -> p a b", b=ncols)
            src = Wd[k0 * 128:(k0 + nk) * 128, col0:col0 + ncols].rearrange("(a p) m -> p a m", p=128)
            key = f"wb{i}"
            S.dma("pool", lambda e: e.dma_start(out=view, in_=src), writes=[key])
            return view, key

        def mainbank(nb=4):
            b = ctr["bank"] % nb
            ctr["bank"] += 1
            return b, f"ps{b}"

        def load_xs(xs, src, tok0, T):
            q = "pool" if src.dtype == F32 else "sp"
            for g in range(4):
                v = src[g * 512:(g + 1) * 512, tok0:tok0 + T].rearrange("(a p) t -> p a t", p=128)
                rk = [K("xb", c, t) for c in range(4 * g, 4 * g + 4) for t in range(tok0, tok0 + T, 512)]
                S.dma(q, lambda e, g=g, v=v: e.dma_start(out=xs[:, 4 * g:4 * g + 4, :], in_=v), reads=rk, writes=[f"xs{g}"])

        def outproj_ln(src_fn, KC, Wd, tok0, NTL, xres, x32_out, xb_out, lncol, okey):
            pend = [None]
            for oc in range(16):
                pieces = []
                for p0 in range(0, KC, 16):
                    n = min(16, KC - p0)
                    pieces.append((p0, n) + wload(Wd, p0, n, oc * 128, 128))
                for nt in range(NTL):
                    t0 = tok0 + nt * 512
                    b, bk = mainbank()
                    xi = ctr["xr"] % 3
                    ctr["xr"] += 1
                    xr, xk = XR[xi], f"xr{xi}"
                    xsrc = xres[oc * 128:(oc + 1) * 128, t0:t0 + 512]
                    S.dma("sp", lambda e, xr=xr, xsrc=xsrc: e.dma_start(out=xr, in_=xsrc), reads=[K("x32", oc, t0)], writes=[xk])
                    for (p0, n, wv, wk) in pieces:
                        for kk in range(n):
                            kc = p0 + kk
                            rhs, rk = src_fn(kc, nt)
                            S.pe(lambda e, b=b, wv=wv, kk=kk, rhs=rhs, kc=kc: e.matmul(ps[b][:], wv[:, kk, :], rhs, start=(kc == 0), stop=(kc == KC - 1)),
                                 reads=[wk, rk], writes=[bk])
                    if pend[0]:
                        pend[0]()
                        pend[0] = None
                    S.dve(lambda e, xr=xr, b=b: e.scalar_tensor_tensor(out=xr, in0=xr, scalar=ALPHA, in1=ps[b][:], op0=ALU.mult, op1=ALU.add),
                          reads=[xk, bk], writes=[xk])
                    S.act(lambda e, xr=xr: e.activation(out=ybf, in_=xr, func=AF.Copy), reads=[xk], writes=["ybf"])
                    S.act(lambda e, xr=xr: e.activation(out=ysq, in_=xr, func=AF.Square), reads=[xk], writes=["ysq"])
                    ydst = X32[oc * 128:(oc + 1) * 128, t0:t0 + 512]
                    S.dma("pool", lambda e, xr=xr, ydst=ydst: e.dma_start(out=ydst, in_=xr), reads=[xk], writes=[K("x32", oc, t0)])

                    def mk(oc=oc, nt=nt):
                        pm, pq = 4 + 2 * nt, 5 + 2 * nt
                        S.pe(lambda e: e.matmul(ps[pm][:], onesD, ybf, start=(oc == 0), stop=(oc == 15)), reads=["ybf", "cb16"], writes=[f"ps{pm}"])
                        S.pe(lambda e: e.matmul(ps[pq][:], onesD, ysq, start=(oc == 0), stop=(oc == 15)), reads=["ysq", "cb16"], writes=[f"ps{pq}"])
                    pend[0] = mk
            pend[0]()
            for nt in range(NTL):
                t0 = tok0 + nt * 512
                pm, pq = 4 + 2 * nt, 5 + 2 * nt
                rstd, nmr = rstd_t[nt], nmr_t[nt]
                S.act(lambda e, pm=pm: e.activation(out=mean_t, in_=ps[pm][:], func=AF.Copy), reads=[f"ps{pm}"], writes=["mean"])
                S.dve(lambda e: e.tensor_tensor(out=tmp_t, in0=mean_t, in1=mean_t, op=ALU.mult), reads=["mean"], writes=["tmp"])
                S.dve(lambda e, pq=pq: e.tensor_tensor(out=tmp_t, in0=ps[pq][:], in1=tmp_t, op=ALU.subtract), reads=[f"ps{pq}", "tmp"], writes=["tmp"])
                S.act(lambda e: e.activation(out=tmp_t, in_=tmp_t, func=AF.Sqrt, bias=1e-5, scale=1.0), reads=["tmp"], writes=["tmp"])
                S.dve(lambda e, rstd=rstd: e.reciprocal(out=rstd, in_=tmp_t), reads=["tmp"], writes=[f"rstd{nt}"])
                S.dve(lambda e, rstd=rstd, nmr=nmr: e.scalar_tensor_tensor(out=nmr, in0=mean_t, scalar=-1.0, in1=rstd, op0=ALU.mult, op1=ALU.mult),
                      reads=["mean", f"rstd{nt}"], writes=[f"nmr{nt}"])
                for oc in range(16):
                    yi = ctr["yl"] % 3
                    ctr["yl"] += 1
                    yl, yk = YL[yi], f"yl{yi}"
                    oi = ctr["obf"] % 2
                    ctr["obf"] += 1
                    ob, obk = OBF[oi], f"obf{oi}"
                    ysrc = X32[oc * 128:(oc + 1) * 128, t0:t0 + 512]
                    S.dma("sp", lambda e, yl=yl, ysrc=ysrc: e.dma_start(out=yl, in_=ysrc), reads=[K("x32", oc, t0)], writes=[yk])
                    S.dve(lambda e, yl=yl, rstd=rstd: e.tensor_tensor(out=yl, in0=yl, in1=rstd, op=ALU.mult), reads=[yk, f"rstd{nt}"], writes=[yk])
                    S.dve(lambda e, yl=yl, nmr=nmr: e.tensor_tensor(out=yl, in0=yl, in1=nmr, op=ALU.add), reads=[yk, f"nmr{nt}"], writes=[yk])
                    gcol = cvec[:, C_LNG + lncol + oc:C_LNG + lncol + oc + 1]
                    bcol = cvec[:, C_LNB + lncol + oc:C_LNB + lncol + oc + 1]
                    S.act(lambda e, yl=yl, ob=ob, gcol=gcol, bcol=bcol: e.activation(out=ob, in_=yl, func=AF.Identity, bias=bcol, scale=gcol),
                          reads=[yk, "cvec"], writes=[obk])
                    S.act(lambda e, yl=yl, gcol=gcol, bcol=bcol: e.activation(out=yl, in_=yl, func=AF.Identity, bias=bcol, scale=gcol),
                          reads=[yk, "cvec"], writes=[yk])
                    odst = x32_out[oc * 128:(oc + 1) * 128, t0:t0 + 512]
                    S.dma("pool", lambda e, yl=yl, odst=odst: e.dma_start(out=odst, in_=yl), reads=[yk], writes=[K(okey, oc, t0)])
                    bdst = xb_out[oc * 128:(oc + 1) * 128, t0:t0 + 512]
                    S.dma("pool", lambda e, ob=ob, bdst=bdst: e.dma_start(out=bdst, in_=ob), reads=[obk], writes=[K("xb", oc, t0)])

        def ffn_phase(l, xb_in, xres, x32_out, xb_out, okey):
            S.barrier()
            al = Alloc(arena, PHASE_BASE)
            xs = al.get([16, 1024], BF16)
            actT = al.get([NFC, 1024], BF16)
            hraw = [[al.get([514], F32) for _ in range(2)] for _ in range(2)]
            tt = [[al.get([512], F32) for _ in range(2)] for _ in range(2)]
            sg = [al.get([512], F32) for _ in range(2)]
            halo = al.get([86, 2], F32)
            Wup = ffn_w_up[l]
            Wdn = ffn_w_down[l]
            for tile in range(NT // 1024):
                tok0 = tile * 1024
                seq_start = (tok0 % SEQ == 0)
                load_xs(xs, xb_in, tok0, 1024)
                for j in range(NFC):
                    for half in range(2):
                        c = j + NFC * half
                        wv, wk = wload(Wup, 0, 16, c * 128, 128)
                        cw = [cvec[:, C_CW + (l * 3 + tap) * 86 + c:C_CW + (l * 3 + tap) * 86 + c + 1] for tap in range(3)]
                        cbc = cvec[:, C_CB + l * 86 + c:C_CB + l * 86 + c + 1]
                        for nt in range(2):
                            b, bk = mainbank()
                            for kc in range(16):
                                S.pe(lambda e, b=b, wv=wv, kc=kc, nt=nt: e.matmul(ps[b][:], wv[:, kc, :], xs[:, kc, nt * 512:(nt + 1) * 512],
                                                                                  start=(kc == 0), stop=(kc == 15)),
                                     reads=[wk, f"xs{kc // 4}"], writes=[bk])
                            hr, hk = hraw[half][nt], K("hr", half, nt)
                            t_, tk = tt[half][nt], K("tt", half, nt)
                            if nt == 0 and seq_start:
                                S.act(lambda e, hr=hr: e.memzero(hr[:, 0:2]), writes=[hk])
                            else:
                                S.act(lambda e, hr=hr, c=c: e.copy(out=hr[:, 0:2], in_=halo[:, c, :]), reads=[K("halo", c)], writes=[hk])
                            S.act(lambda e, hr=hr, b=b: e.copy(out=hr[:, 2:514], in_=ps[b][:]), reads=[bk], writes=[hk])
                            S.act(lambda e, hr=hr, c=c: e.copy(out=halo[:, c, :], in_=hr[:, 512:514]), reads=[hk], writes=[K("halo", c)])
                            S.act(lambda e, t_=t_, b=b, cw=cw, cbc=cbc: e.activation(out=t_, in_=ps[b][:], func=AF.Identity, bias=cbc, scale=cw[2]),
                                  reads=[bk, "cvec"], writes=[tk])
                            S.dve(lambda e, t_=t_, hr=hr, cw=cw: e.scalar_tensor_tensor(out=t_, in0=hr[:, 1:513], scalar=cw[1], in1=t_, op0=ALU.mult, op1=ALU.add),
                                  reads=[hk, tk, "cvec"], writes=[tk])
                            S.dve(lambda e, t_=t_, hr=hr, cw=cw: e.scalar_tensor_tensor(out=t_, in0=hr[:, 0:512], scalar=cw[0], in1=t_, op0=ALU.mult, op1=ALU.add),
                                  reads=[hk, tk, "cvec"], writes=[tk])
                            if half == 0:
                                S.act(lambda e, t_=t_, nt=nt: e.activation(out=sg[nt], in_=t_, func=AF.Silu), reads=[tk], writes=[K("sg", nt)])
                            else:
                                S.dve(lambda e, t_=t_, nt=nt, j=j: e.tensor_tensor(out=actT[:, j, nt * 512:(nt + 1) * 512], in0=t_, in1=sg[nt], op=ALU.mult),
                                      reads=[tk, K("sg", nt)], writes=[K("act", j, nt)])
                outproj_ln(lambda kc, nt: (actT[:, kc, nt * 512:(nt + 1) * 512], K("act", kc, nt)),
                           NFC, Wdn, tok0, 2, xres, x32_out, xb_out, (l * 2 + 1) * 16, okey)

        def gla_phase(l, xb_in, xres, x32_out, xb_out, okey):
            j = l // 2
            S.barrier()
            al = Alloc(arena, PHASE_BASE)
            xs = al.get([16, 512], BF16)
            og = xs
            qT = al.get([8, 512], BF16)
            kT = al.get([8, 512], BF16)
            vsb = al.get([4, 2048], BF16)
            rs = al.get([16, 512], BF16)
            glow = al.get([512], F32, parts=17)
            wg = al.get([1024], F32, parts=17)
            S32 = al.get([8, 512], F32)
            Sbf = al.get([8, 512], BF16)
            u = al.get([1024], F32)
            EB = al.get([8, 128], F32)
            ENB = al.get([8, 128], F32)
            qs = al.get([8, 128], BF16)
            ks = al.get([8, 128], BF16)
            kh = al.get([8, 128], BF16)
            khat = al.get([1024], BF16)
            PT = al.get([4, 128], BF16)
            sq = al.get([16, 128], BF16)
            rsq = al.get([4, 128], F32)
            t1 = al.get([16, 128], F32)
            Win = gla_w_in[j]
            Wout = gla_w_out[j]
            S.dma("sp", lambda e: e.dma_start(out=wg, in_=wg_d[j]), writes=["wg"])
            S.dve(lambda e: e.memset(glow, 1.0), writes=["glow"])
            ps5bf = ps[5][:].bitcast(BF16)
            XS_KEYS = [f"xs{g}" for g in range(4)]
            for tile in range(NT // 512):
                tok0 = tile * 512
                seq_start = (tok0 % SEQ == 0)
                load_xs(xs, xb_in, tok0, 512)
                if seq_start:
                    S.dve(lambda e: e.memset(S32, 0.0), writes=[K("S32", h) for h in range(8)])
                    S.dve(lambda e: e.memset(Sbf, 0.0), writes=[K("Sbf", h) for h in range(8)])

                def fm_chunk(col0, evac):
                    wv, wk = wload(Win, 0, 16, col0, 128)
                    b, bk = mainbank()
                    for kc in range(16):
                        S.pe(lambda e, b=b, wv=wv, kc=kc: e.matmul(ps[b][:], wv[:, kc, :], xs[:, kc, :], start=(kc == 0), stop=(kc == 15)),
                             reads=[wk, f"xs{kc // 4}"], writes=[bk])
                    evac(b, bk)
                for c in range(8):
                    fm_chunk(c * 128, lambda b, bk, c=c: S.act(lambda e: e.activation(out=qT[:, c, :], in_=ps[b][:], func=AF.Identity, scale=0.0625),
                                                                reads=[bk], writes=[K("qT", c)]))
                for c in range(8):
                    fm_chunk(1024 + c * 128, lambda b, bk, c=c: S.dve(lambda e: e.tensor_copy(out=kT[:, c, :], in_=ps[b][:]),
                                                                       reads=[bk], writes=[K("kT", c)]))
                for c in range(16):
                    fm_chunk(4096 + c * 128, lambda b, bk, c=c: S.act(lambda e: e.activation(out=rs[:, c, :], in_=ps[b][:], func=AF.Silu),
                                                                       reads=[bk], writes=[K("rs", c)]))
                wv, wk = wload(Win, 0, 16, 6144, 16)
                b, bk = mainbank()
                for kc in range(16):
                    S.pe(lambda e, b=b, wv=wv, kc=kc: e.matmul(ps[b][0:16, :], wv[:, kc, :], xs[:, kc, :], start=(kc == 0), stop=(kc == 15)),
                         reads=[wk, f"xs{kc // 4}"], writes=[bk])
                S.act(lambda e, b=b: e.copy(out=glow[0:16, :], in_=ps[b][0:16, :]), reads=[bk], writes=["glow"])
                for cg in range(16):
                    wv, wk = wload(Win, 0, 16, 2048 + cg * 128, 128)
                    b, bk = mainbank()
                    for tc in range(4):
                        for kc in range(16):
                            S.pe(lambda e, b=b, wv=wv, kc=kc, tc=tc: e.matmul(ps[b][:, tc * 128:(tc + 1) * 128], xs[:, kc, tc * 128:(tc + 1) * 128], wv[:, kc, :],
                                                                              start=(kc == 0), stop=(kc == 15)),
                                 reads=[wk, f"xs{kc // 4}"], writes=[bk])
                    S.act(lambda e, b=b, cg=cg: e.copy(out=vsb[:, :, cg * 128:(cg + 1) * 128], in_=ps[b][:].rearrange("p (a b) -> p a b", b=128)),
                          reads=[bk], writes=[K("v", cg)])
                VK = [K("v", cg) for cg in range(16)]
                for tc in range(4):
                    tsl = slice(tc * 128, (tc + 1) * 128)
                    for hf in range(2):
                        S.pe(lambda e, hf=hf, tsl=tsl: e.matmul(ps[hf][:], glow[0:17, tsl], wg[0:17, hf * 512:(hf + 1) * 512], start=True, stop=True),
                             reads=["glow", "wg"], writes=[f"ps{hf}"])
                        S.act(lambda e, hf=hf: e.activation(out=u[:, hf * 512:(hf + 1) * 512], in_=ps[hf][:], func=AF.Exp, scale=-1.0),
                              reads=[f"ps{hf}"], writes=[K("u", hf)])
                        S.act(lambda e, hf=hf: e.activation(out=u[:, hf * 512:(hf + 1) * 512], in_=u[:, hf * 512:(hf + 1) * 512], func=AF.Ln, bias=1.0, scale=1.0),
                              reads=[K("u", hf)], writes=[K("u", hf)])
                    for dc in range(8):
                        pb = 2 + dc // 4
                        S.pe(lambda e, dc=dc, pb=pb: e.matmul(ps[pb][:, (dc % 4) * 128:(dc % 4 + 1) * 128], u[:, dc * 128:(dc + 1) * 128], tri, start=True, stop=True),
                             reads=[K("u", dc // 4), "cm32"], writes=[f"ps{pb}"])
                    for hf in range(2):
                        pv = ps[2 + hf][:].rearrange("p (a b) -> p a b", b=128)
                        S.act(lambda e, hf=hf, pv=pv: e.activation(out=EB[:, hf * 4:(hf + 1) * 4, :], in_=pv, func=AF.Exp), reads=[f"ps{2 + hf}"], writes=[K("EB", hf)])
                        S.act(lambda e, hf=hf, pv=pv: e.activation(out=ENB[:, hf * 4:(hf + 1) * 4, :], in_=pv, func=AF.Exp, scale=-1.0), reads=[f"ps{2 + hf}"], writes=[K("ENB", hf)])
                    EBK = [K("EB", 0), K("EB", 1)]
                    ENBK = [K("ENB", 0), K("ENB", 1)]
                    S.dve(lambda e, tsl=tsl: e.tensor_tensor(out=qs, in0=qT[:, :, tsl], in1=EB, op=ALU.mult), reads=[K("qT", c) for c in range(8)] + EBK, writes=["qs"])
                    S.dve(lambda e, tsl=tsl: e.tensor_tensor(out=ks, in0=kT[:, :, tsl], in1=ENB, op=ALU.mult), reads=[K("kT", c) for c in range(8)] + ENBK, writes=["ks"])
                    S.dve(lambda e: e.tensor_tensor(out=kh, in0=ks, in1=EB[:, :, 127:128].broadcast_to([128, 8, 128]), op=ALU.mult), reads=["ks"] + EBK, writes=["kh"])
                    for h in range(4):
                        for dl in range(2):
                            S.pe(lambda e, h=h, dl=dl: e.matmul(ps[4][:, h * 128:(h + 1) * 128], ks[:, 2 * h + dl, :], qs[:, 2 * h + dl, :], start=(dl == 0), stop=(dl == 1)),
                                 reads=["ks", "qs"], writes=["ps4"])
                    S.dve(lambda e: e.tensor_tensor(out=PT, in0=ps[4][:].rearrange("p (a b) -> p a b", b=128),
                                                    in1=mcur.unsqueeze(1).broadcast_to([128, 4, 128]), op=ALU.mult), reads=["ps4", "cm32"], writes=["PT"])
                    for dc in range(8):
                        S.pe(lambda e, dc=dc: e.transpose(ps5bf[:, dc * 128:(dc + 1) * 128], kh[:, dc, :], ident), reads=["kh", "cb16"], writes=["ps5"])
                    S.act(lambda e: e.copy(out=khat, in_=ps5bf), reads=["ps5"], writes=["khat"])
                    for h in range(4):
                        for ec in range(4):
                            o_ = ps[h][:, ec * 128:(ec + 1) * 128]
                            S.pe(lambda e, h=h, ec=ec, o_=o_, tc=tc: e.matmul(o_, vsb[:, tc, h * 512 + ec * 128:h * 512 + (ec + 1) * 128], PT[:, h, :], start=True, stop=False),
                                 reads=VK + ["PT"], writes=[f"ps{h}"])
                            for dl in range(2):
                                S.pe(lambda e, h=h, ec=ec, o_=o_, dl=dl: e.matmul(o_, Sbf[:, 2 * h + dl, ec * 128:(ec + 1) * 128], qs[:, 2 * h + dl, :], start=False, stop=(dl == 1)),
                                     reads=[K("Sbf", 2 * h + dl), "qs"], writes=[f"ps{h}"])
                    for h in range(4):
                        S.act(lambda e, h=h: e.activation(out=sq[:, h * 4:(h + 1) * 4, :], in_=ps[h][:].rearrange("p (a b) -> p a b", b=128), func=AF.Square),
                              reads=[f"ps{h}"], writes=[K("sq", h)])
                    for h in range(4):
                        for ec in range(4):
                            S.pe(lambda e, h=h, ec=ec: e.matmul(ps[6][:, h * 128:(h + 1) * 128], ones512, sq[:, h * 4 + ec, :], start=(ec == 0), stop=(ec == 3)),
                                 reads=[K("sq", h), "cb16"], writes=["ps6"])
                    S.act(lambda e: e.activation(out=rsq, in_=ps[6][:].rearrange("p (a b) -> p a b", b=128), func=AF.Sqrt, bias=1e-6, scale=1.0), reads=["ps6"], writes=["rsq"])
                    S.dve(lambda e: e.reciprocal(out=rsq, in_=rsq), reads=["rsq"], writes=["rsq"])
                    for h in range(4):
                        S.dve(lambda e, h=h: e.tensor_tensor(out=t1[:, h * 4:(h + 1) * 4, :], in0=ps[h][:].rearrange("p (a b) -> p a b", b=128),
                                                             in1=rsq[:, h:h + 1, :].broadcast_to([128, 4, 128]), op=ALU.mult),
                              reads=[f"ps{h}", "rsq"], writes=[K("t1", h)])
                    for ec in range(4):
                        ngc = cvec[:, C_NG + j * 4 + ec:C_NG + j * 4 + ec + 1]
                        S.dve(lambda e, ec=ec, ngc=ngc, tsl=tsl: e.scalar_tensor_tensor(out=og[:, ec::4, tsl], in0=t1[:, ec::4, :], scalar=ngc, in1=rs[:, ec::4, tsl],
                                                                                       op0=ALU.mult, op1=ALU.mult),
                              reads=[K("t1", h) for h in range(4)] + [K("rs", c) for c in range(16)] + ["cvec"], writes=XS_KEYS + [K("og", tc)])
                    for hd in range(8):
                        h = hd // 2
                        pb = (4, 5, 7)[hd % 3]
                        S.pe(lambda e, hd=hd, h=h, pb=pb, tc=tc: e.matmul(ps[pb][:], khat[:, hd * 128:(hd + 1) * 128], vsb[:, tc, h * 512:(h + 1) * 512], start=True, stop=True),
                             reads=["khat"] + VK, writes=[f"ps{pb}"])
                        S.dve(lambda e, hd=hd, pb=pb: e.scalar_tensor_tensor(out=S32[:, hd, :], in0=S32[:, hd, :], scalar=EB[:, hd, 127:128], in1=ps[pb][:],
                                                                             op0=ALU.mult, op1=ALU.add),
                              reads=[K("S32", hd), f"ps{pb}"] + EBK, writes=[K("S32", hd)])
                        S.act(lambda e, hd=hd: e.copy(out=Sbf[:, hd, :], in_=S32[:, hd, :]), reads=[K("S32", hd)], writes=[K("Sbf", hd)])
                outproj_ln(lambda kc, nt: (og[:, kc, :], K("og", 0)), 16, Wout, tok0, 1, xres, x32_out, xb_out, (l * 2) * 16, okey)

        def dil_phase(l, xb_in, xres, x32_out, xb_out, okey):
            j = l // 2
            S.barrier()
            al = Alloc(arena, PHASE_BASE)
            xs = al.get([16, 2048], BF16)
            og = al.get([8, 2048], BF16)
            acc2 = al.get([2, 2048], F32)
            qTb = [al.get([2048], BF16) for _ in range(2)]
            kTb = [al.get([2048], BF16) for _ in range(2)]
            vT = al.get([2048], BF16)
            vtokb = [al.get([16, 128], BF16) for _ in range(2)]
            pexp = [al.get([2, 128], F32) for _ in range(2)]
            PTb = [al.get([2, 128], BF16) for _ in range(2)]
            Win = dil_w_in[j]
            Wout = dil_w_out[j]
            ps3bf = ps[3][:].bitcast(BF16)
            cnt = {"hg": 0, "blk": 0}
            for s in range(NSEQ):
                tokS = s * SEQ
                load_xs(xs, xb_in, tokS, 2048)
                for h in range(8):
                    for gi, dd in enumerate((1, 4, 16)):
                        hg = cnt["hg"] % 2
                        cnt["hg"] += 1
                        qT, kT, vtok = qTb[hg], kTb[hg], vtokb[hg]
                        qk_, kk_, vk_, vtk = K("q", hg), K("k", hg), "vT", K("vtok", hg)
                        for t in range(3):
                            wv, wk = wload(Win, 0, 16, ((gi * 3 + t) * 8 + h) * 128, 128)
                            for nt in range(4):
                                b, bk = mainbank(3)
                                for kc in range(16):
                                    S.pe(lambda e, b=b, wv=wv, kc=kc, nt=nt: e.matmul(ps[b][:], wv[:, kc, :], xs[:, kc, nt * 512:(nt + 1) * 512], start=(kc == 0), stop=(kc == 15)),
                                         reads=[wk, f"xs{kc // 4}"], writes=[bk])
                                nsl = slice(nt * 512, (nt + 1) * 512)
                                if t == 0:
                                    S.act(lambda e, b=b, qT=qT, nsl=nsl: e.activation(out=qT[:, nsl], in_=ps[b][:], func=AF.Identity, scale=float(128 ** -0.5)),
                                          reads=[bk], writes=[qk_])
                                elif t == 1:
                                    S.dve(lambda e, b=b, kT=kT, nsl=nsl: e.tensor_copy(out=kT[:, nsl], in_=ps[b][:]), reads=[bk], writes=[kk_])
                                else:
                                    S.act(lambda e, b=b, nsl=nsl: e.copy(out=vT[:, nsl], in_=ps[b][:]), reads=[bk], writes=[vk_])
                        nb = 16 // dd

                        def tsl(blk):
                            r, bb = blk // nb, blk % nb
                            return slice(r + dd * 128 * bb, r + dd * 128 * bb + dd * 127 + 1, dd)
                        for half in range(2):
                            for bi in range(8):
                                blk = half * 8 + bi
                                vsl = tsl(blk)
                                S.pe(lambda e, bi=bi, vsl=vsl: e.transpose(ps3bf[:, bi * 128:(bi + 1) * 128], vT[:, vsl], ident), reads=[vk_, "cb16"], writes=["ps3"])
                            S.dve(lambda e, half=half, vtok=vtok: e.tensor_copy(out=vtok[:, half * 8:(half + 1) * 8, :], in_=ps3bf.rearrange("p (a b) -> p a b", b=128)),
                                  reads=["ps3"], writes=[vtk])
                        for blk in range(16):
                            bb = blk % nb
                            j0 = 0 if bb > 0 else 1
                            ci = cnt["blk"] % 2
                            cnt["blk"] += 1
                            psS = ps[4 + ci][:].rearrange("p (a b) -> p a b", b=128)
                            psO = ps[6 + ci][:].rearrange("p (a b) -> p a b", b=128)
                            pe_, PT = pexp[ci], PTb[ci]
                            cs = tsl(blk)
                            if bb > 0:
                                psl = tsl(blk - 1)
                                S.pe(lambda e, psS=psS, kT=kT, qT=qT, psl=psl, cs=cs: e.matmul(psS[:, 0, :], kT[:, psl], qT[:, cs], start=True, stop=True),
                                     reads=[kk_, qk_], writes=[f"ps{4 + ci}"])
                            S.pe(lambda e, psS=psS, kT=kT, qT=qT, cs=cs: e.matmul(psS[:, 1, :], kT[:, cs], qT[:, cs], start=True, stop=True),
                                 reads=[kk_, qk_], writes=[f"ps{4 + ci}"])
                            S.act(lambda e, psS=psS, pe_=pe_, j0=j0: e.activation(out=pe_[:, j0:2, :], in_=psS[:, j0:2, :], func=AF.Exp), reads=[f"ps{4 + ci}"], writes=[K("pexp", ci)])
                            S.dve(lambda e, pe_=pe_, PT=PT, j0=j0: e.tensor_tensor(out=PT[:, j0:2, :], in0=pe_[:, j0:2, :], in1=masks[:, j0:2, :], op=ALU.mult),
                                  reads=[K("pexp", ci), "cm32"], writes=[K("PT", ci)])
                            for jj in range(j0, 2):
                                bk_ = blk - 1 if jj == 0 else blk
                                S.pe(lambda e, psO=psO, vtok=vtok, PT=PT, jj=jj, bk_=bk_, j0=j0: e.matmul(psO[:, 0, :], vtok[:, bk_, :], PT[:, jj, :], start=(jj == j0), stop=(jj == 1)),
                                     reads=[vtk, K("PT", ci)], writes=[f"ps{6 + ci}"])
                            for jj in range(j0, 2):
                                S.pe(lambda e, psO=psO, PT=PT, jj=jj, j0=j0: e.matmul(psO[:, 1, :], ones1, PT[:, jj, :], start=(jj == j0), stop=(jj == 1)),
                                     reads=["cb16", K("PT", ci)], writes=[f"ps{6 + ci}"])
                            if gi == 0:
                                S.dve(lambda e, psO=psO, cs=cs: e.tensor_copy(out=acc2[:, :, cs], in_=psO[:, 0:2, :]), reads=[f"ps{6 + ci}"], writes=["acc2"])
                            else:
                                S.dve(lambda e, psO=psO, cs=cs: e.tensor_tensor(out=acc2[:, :, cs], in0=acc2[:, :, cs], in1=psO[:, 0:2, :], op=ALU.add),
                                      reads=[f"ps{6 + ci}", "acc2"], writes=["acc2"])
                    S.dve(lambda e: e.reciprocal(out=acc2[:, 1, :], in_=acc2[:, 1, :]), reads=["acc2"], writes=["acc2"])
                    S.dve(lambda e, h=h: e.tensor_tensor(out=og[:, h, :], in0=acc2[:, 0, :], in1=acc2[:, 1, :], op=ALU.mult), reads=["acc2"], writes=[K("og", h)])
                for half in range(2):
                    outproj_ln(lambda kc, nt, half=half: (og[:, kc, half * 1024 + nt * 512:half * 1024 + (nt + 1) * 512], K("og", kc)),
                               8, Wout, tokS + half * 1024, 2, xres, x32_out, xb_out, (l * 2) * 16, okey)

        cur = None
        for sidx in range(NSUB):
            l = sidx // 2
            last = (sidx == NSUB - 1)
            xres = xT if sidx == 0 else X32
            x32_out = outT if last else X32
            okey = "out" if last else "x32"
            xb_in = xT if cur is None else XB[cur]
            nxt = 0 if cur is None else 1 - cur
            xb_out = XB[nxt]
            if sidx % 2 == 1:
                ffn_phase(l, xb_in, xres, x32_out, xb_out, okey)
            elif l % 2 == 0:
                gla_phase(l, xb_in, xres, x32_out, xb_out, okey)
            else:
                dil_phase(l, xb_in, xres, x32_out, xb_out, okey)
            cur = nxt
        S.barrier()
        S.op("sp", lambda e: e.nop())
        S.emit()
    return nc


def host_consts(inputs):
    f = np.float32
    cvec = np.zeros((128, NCV), f)
    cvec[:, C_LNG:C_LNG + 128] = np.asarray(inputs["ln_g"], f).reshape(8, 16, 128).transpose(2, 0, 1).reshape(128, 128)
    cvec[:, C_LNB:C_LNB + 128] = np.asarray(inputs["ln_b"], f).reshape(8, 16, 128).transpose(2, 0, 1).reshape(128, 128)
    cvec[:, C_CW:C_CW + 1032] = np.asarray(inputs["ffn_conv_w"], f).reshape(4, 3, 86, 128).transpose(3, 0, 1, 2).reshape(128, 1032)
    cvec[:, C_CB:C_CB + 344] = np.asarray(inputs["ffn_conv_b"], f).reshape(4, 86, 128).transpose(2, 0, 1).reshape(128, 344)
    cvec[:, C_NG:C_NG + 8] = np.asarray(inputs["gla_norm_g"], f).reshape(2, 4, 128).transpose(2, 0, 1).reshape(128, 8)
    cmat = np.zeros((128, 896), f)
    ii = np.arange(128)
    le = (ii[:, None] <= ii[None, :]).astype(f)
    ge = (ii[:, None] >= ii[None, :]).astype(f)
    cmat[:, 0:128] = le * f(-1.0 / 16.0)
    cmat[:, 128:256] = ge
    cmat[:, 256:384] = le
    cmat[:, 384:512] = np.eye(128, dtype=f)
    cmat[:, 512:640] = f(1.0 / 2048.0)
    cmat[:, 640:768] = f(1.0 / 512.0)
    cmat[:, 768:896] = f(1.0)
    wg = np.concatenate([np.asarray(inputs["gla_w_gate_up"], f), np.asarray(inputs["gla_gate_bias"], f)[:, None, :]], axis=1)
    return cvec, cmat, np.ascontiguousarray(wg)


_PROG = {}


def run(inputs, ncores=NCORES, nseq=NSEQ_CORE, nsub=8, trace=False):
    key = (nseq, nsub)
    if key not in _PROG:
        _PROG[key] = build(nseq, nsub)
    nc = _PROG[key]
    cvec, cmat, wg = host_consts(inputs)
    x = np.asarray(inputs["x"], np.float32)
    shared = {k: np.ascontiguousarray(np.asarray(inputs[k], np.float32)) for k in
              ("gla_w_in", "gla_w_out", "dil_w_in", "dil_w_out", "ffn_w_up", "ffn_w_down")}
    in_maps = []
    for c in range(ncores):
        xc = x[c * nseq:(c + 1) * nseq].reshape(nseq * SEQ, D)
        m = dict(shared)
        m.update(xT=np.ascontiguousarray(xc.T), cvec=cvec, cmat=cmat, wg_aug=wg)
        in_maps.append(m)
    res = run_bass_kernel_spmd(nc, in_maps, core_ids=list(range(ncores)), **({"trace": True} if trace else {}))
    out = np.empty((ncores * nseq, SEQ, D), np.float32)
    for c in range(ncores):
        oT = np.asarray(res.results[c]["outT"])
        out[c * nseq:(c + 1) * nseq] = oT.T.reshape(nseq, SEQ, D)
    return out, res


def kernel(**inputs):
    out, _ = run(inputs)
    return out
```

```python
import contextlib
from collections import deque
import numpy as np
import concourse.bass as bass
import concourse.mybir as mybir
from concourse.bass_utils import run_bass_kernel_spmd

F32 = mybir.dt.float32
BF16 = mybir.dt.bfloat16
AF = mybir.ActivationFunctionType
ALU = mybir.AluOpType

D = 2048
SEQ = 2048
DFF = 5504
NFC = 43
ALPHA = float(8 ** 0.25)
NCORES = 8
NSEQ_CORE = 2

ENGS = ["pe", "act", "dve", "pool", "sp"]
NDMASEM = 16
EPOCH = 30000
NWB = 6
ARENA_BYTES = 211968

NCV = 1640
C_LNG, C_LNB, C_CW, C_CB, C_NG = 0, 128, 256, 1288, 1632


def K(*a):
    return "_".join(map(str, a))


class Sched:
    def __init__(self, nc):
        self.nc = nc
        self.ops = []
        self.last_w = {}
        self.readers = {}
        self.last_comp = {}
        self.recent_dma = {e: deque(maxlen=NDMASEM) for e in ENGS}
        self.bar_deps = set()
        self.bar_pending = set()

    def barrier(self):
        deps = set()
        for e in ENGS:
            if e in self.last_comp:
                deps.add(self.last_comp[e])
            deps.update(self.recent_dma[e])
        self.bar_deps = deps
        self.bar_pending = set(ENGS)
        self.last_w = {}
        self.readers = {}

    def op(self, eng, fn, reads=(), writes=(), dma=False):
        i = len(self.ops)
        deps = set()
        for k in reads:
            w = self.last_w.get(k)
            if w is not None:
                deps.add(w)
        for k in writes:
            w = self.last_w.get(k)
            if w is not None:
                deps.add(w)
            rs = self.readers.get(k)
            if rs:
                deps.update(rs)
        for k in reads:
            self.readers.setdefault(k, []).append(i)
        for k in writes:
            self.last_w[k] = i
            self.readers[k] = []
        if eng in self.bar_pending:
            deps |= self.bar_deps
            self.bar_pending.discard(eng)
        deps.discard(i)
        self.ops.append((eng, fn, deps, dma))
        if dma:
            self.recent_dma[eng].append(i)
        else:
            self.last_comp[eng] = i
        return i

    def pe(self, fn, reads=(), writes=()):
        return self.op("pe", fn, reads, writes)

    def act(self, fn, reads=(), writes=()):
        return self.op("act", fn, reads, writes)

    def dve(self, fn, reads=(), writes=()):
        return self.op("dve", fn, reads, writes)

    def dma(self, q, fn, reads=(), writes=()):
        return self.op(q, fn, reads, writes, dma=True)

    def emit(self):
        nc = self.nc
        ops = self.ops
        n = len(ops)
        need_inc = [False] * n
        for i, (eng, fn, deps, dma) in enumerate(ops):
            for d in deps:
                deng, _, _, ddma = ops[d]
                if ddma:
                    continue
                if deng == "pe" and eng == "pe" and not dma:
                    continue
                need_inc[d] = True
        cnt = {e: 0 for e in ENGS}
        dcnt = {e: 0 for e in ENGS}
        sig = [None] * n
        for i, (eng, fn, deps, dma) in enumerate(ops):
            if dma:
                k = dcnt[eng]
                dcnt[eng] += 1
                sig[i] = (("d", eng, k % NDMASEM), 16 * (k // NDMASEM + 1))
            elif need_inc[i]:
                c = cnt[eng]
                cnt[eng] += 1
                sig[i] = (("c", eng, c // EPOCH), (c % EPOCH) + 1)
        semkeys = sorted({s[0] for s in sig if s is not None})
        with contextlib.ExitStack() as st:
            sems = {}
            for k in semkeys:
                sems[k] = st.enter_context(nc.semaphore("s_" + "_".join(map(str, k))))
            block = st.enter_context(nc.Block())

            def run_engine(ename, e):
                waited = {}
                for i, (eng, fn, deps, dma) in enumerate(ops):
                    if eng != ename:
                        continue
                    wl = {}
                    for d in deps:
                        deng, _, _, ddma = ops[d]
                        if (not ddma) and deng == "pe" and eng == "pe" and not dma:
                            continue
                        sk, v = sig[d]
                        if waited.get(sk, 0) >= v:
                            continue
                        if wl.get(sk, 0) < v:
                            wl[sk] = v
                    if dma:
                        sk, v = sig[i]
                        if v > 16 and waited.get(sk, 0) < v - 16 and wl.get(sk, 0) < v - 16:
                            wl[sk] = v - 16
                    for sk, v in wl.items():
                        e.wait_ge(sems[sk], v)
                        waited[sk] = v
                    ins = fn(e)
                    if sig[i] is not None:
                        ins.then_inc(sems[sig[i][0]], 16 if dma else 1)

            @block.tensor
            def _(e):
                run_engine("pe", e)

            @block.scalar
            def _(e):
                run_engine("act", e)

            @block.vector
            def _(e):
                run_engine("dve", e)

            @block.gpsimd
            def _(e):
                run_engine("pool", e)

            @block.sync
            def _(e):
                run_engine("sp", e)


class Alloc:
    def __init__(self, arena, base=0):
        self.a = arena
        self.off = base

    def get(self, shape, dt, parts=128):
        n = int(np.prod(shape))
        nb = n * (4 if dt == F32 else 2)
        start = self.off
        self.off += (nb + 63) // 64 * 64
        assert self.off <= ARENA_BYTES, ("arena overflow", self.off)
        ap = self.a[0:parts, start // 2:(start + nb) // 2]
        if dt == F32:
            ap = ap.bitcast(F32)
        if len(shape) == 2:
            ap = ap.rearrange("p (a b) -> p a b", b=shape[1])
        elif len(shape) == 3:
            ap = ap.rearrange("p (a b c) -> p a b c", b=shape[1], c=shape[2])
        return ap


def build(NSEQ=NSEQ_CORE, NSUB=8):
    NT = NSEQ * SEQ
    nc = bass.Bass("TRN2", target_bir_lowering=False)

    def din(name, shape):
        return nc.dram_tensor(name, list(shape), F32, kind="ExternalInput").ap()

    xT = din("xT", [D, NT])
    gla_w_in = din("gla_w_in", [2, D, 6160])
    gla_w_out = din("gla_w_out", [2, D, D])
    dil_w_in = din("dil_w_in", [2, D, 9216])
    dil_w_out = din("dil_w_out", [2, 1024, D])
    ffn_w_up = din("ffn_w_up", [4, D, 11008])
    ffn_w_down = din("ffn_w_down", [4, DFF, D])
    cvec_d = din("cvec", [128, NCV])
    cmat_d = din("cmat", [128, 896])
    wg_d = din("wg_aug", [2, 17, 1024])
    outT = nc.dram_tensor("outT", [D, NT], F32, kind="ExternalOutput").ap()
    X32 = nc.dram_tensor("X32", [D, NT], F32).ap()
    XB = [nc.dram_tensor(f"XB{i}", [D, NT], BF16).ap() for i in range(2)]
    WS = nc.dram_tensor("WS", [2, 64, 128, 2048], BF16).ap()

    with contextlib.ExitStack() as st:
        arena = st.enter_context(nc.sbuf_tensor("arena", [128, ARENA_BYTES // 2], BF16))
        ps = [st.enter_context(nc.psum_tensor(f"ps{i}", [128, 512], F32)) for i in range(8)]
        S = Sched(nc)

        cm = Alloc(arena, 0)
        cvec = cm.get([NCV], F32)
        cm32 = cm.get([3, 128], F32)
        tri = cm32[:, 0, :]
        masks = cm32[:, 1:3, :]
        mcur = cm32[:, 2, :]
        cb16 = cm.get([4, 128], BF16)
        ident, onesD, ones512, ones1 = cb16[:, 0, :], cb16[:, 1, :], cb16[:, 2, :], cb16[:, 3, :]
        wpool = [cm.get([2048], BF16) for _ in range(NWB)]
        XR = [cm.get([512], F32) for _ in range(3)]
        ybf = cm.get([512], BF16)
        ysq = cm.get([512], BF16)
        mean_t = cm.get([512], F32)
        tmp_t = cm.get([512], F32)
        rstd_t = [cm.get([512], F32) for _ in range(2)]
        nmr_t = [cm.get([512], F32) for _ in range(2)]
        YL = [cm.get([512], F32) for _ in range(3)]
        OBF = [cm.get([512], BF16) for _ in range(2)]
        PHASE_BASE = cm.off

        S.dma("sp", lambda e: e.dma_start(out=cvec, in_=cvec_d), writes=["cvec"])
        S.dma("sp", lambda e: e.dma_start(out=cm32, in_=cmat_d[:, 0:384].rearrange("p (a b) -> p a b", b=128)), writes=["cm32"])
        S.dma("pool", lambda e: e.dma_start(out=cb16, in_=cmat_d[:, 384:896].rearrange("p (a b) -> p a b", b=128)), writes=["cb16"])

        ctr = {"wb": 0, "bank": 0, "xr": 0, "yl": 0, "obf": 0}

        def wload(Wd, k0, nk, col0, ncols):
            i = ctr["wb"] % NWB
            ctr["wb"] += 1
            view = wpool[i][:, 0:nk * ncols].rearrange("p (a b) -> p a b", b=ncols)
            src = Wd[k0 * 128:(k0 + nk) * 128, col0:col0 + ncols].rearrange("(a p) m -> p a m", p=128)
            key = f"wb{i}"
            S.dma("pool", lambda e: e.dma_start(out=view, in_=src), writes=[key])
            return view, key

        def ws_src(j, cid):
            if cid < 48:
                col0 = (cid * 128 if cid < 8 else 1024 + (cid - 8) * 128 if cid < 16 else 4096 + (cid - 16) * 128 if cid < 32 else 2048 + (cid - 32) * 128)
                return gla_w_in[j], col0
            return gla_w_out[j], (cid - 48) * 128

        def gla_conv_ops(j):
            lst = []
            for cid in range(64):
                def f(cid=cid):
                    Wd, col0 = ws_src(j, cid)
                    view, key = wload(Wd, 0, 16, col0, 128)
                    dst = WS[j, cid].rearrange("p (a b) -> p a b", b=128)
                    S.dma("sp", lambda e: e.dma_start(out=dst, in_=view), reads=[key], writes=[K("ws", j, cid)])
                lst.append(f)
            return lst

        def wload_ws(j, cid):
            i = ctr["wb"] % NWB
            ctr["wb"] += 1
            view = wpool[i].rearrange("p (a b) -> p a b", b=128)
            src = WS[j, cid].rearrange("p (a b) -> p a b", b=128)
            key = f"wb{i}"
            S.dma("pool", lambda e: e.dma_start(out=view, in_=src), reads=[K("ws", j, cid)], writes=[key])
            return view, key

        pending_work = deque()

        def mainbank(nb=4):
            b = ctr["bank"] % nb
            ctr["bank"] += 1
            return b, f"ps{b}"

        def load_xs(xs, src, tok0, T):
            q = "pool" if src.dtype == F32 else "sp"
            for g in range(4):
                v = src[g * 512:(g + 1) * 512, tok0:tok0 + T].rearrange("(a p) t -> p a t", p=128)
                rk = [K("xb", c, t) for c in range(4 * g, 4 * g + 4) for t in range(tok0, tok0 + T, 512)]
                S.dma(q, lambda e, g=g, v=v: e.dma_start(out=xs[:, 4 * g:4 * g + 4, :], in_=v), reads=rk, writes=[f"xs{g}"])

        def outproj_ln(src_fn, KC, Wd, tok0, NTL, xres, x32_out, xb_out, lncol, okey, wfn=None):
            pend = [None]
            groups = [(oc, nt) for oc in range(16) for nt in range(NTL)]
            wts = {}
            xrs = {}

            def load_w(oc):
                if wfn is not None:
                    wv, wk = wfn(oc)
                    wts[oc] = [(0, KC, wv, wk)]
                    return
                pieces = []
                for p0 in range(0, KC, 16):
                    n = min(16, KC - p0)
                    pieces.append((p0, n) + wload(Wd, p0, n, oc * 128, 128))
                wts[oc] = pieces

            def load_xr(g):
                oc, nt = groups[g]
                t0 = tok0 + nt * 512
                xi = ctr["xr"] % 3
                ctr["xr"] += 1
                xr, xk = XR[xi], f"xr{xi}"
                xsrc = xres[oc * 128:(oc + 1) * 128, t0:t0 + 512]
                S.dma("sp", lambda e, xr=xr, xsrc=xsrc: e.dma_start(out=xr, in_=xsrc), reads=[K("x32", oc, t0)], writes=[xk])
                xrs[g] = (xr, xk)

            load_w(0)
            for g in range(min(2, len(groups))):
                load_xr(g)
            for gi, (oc, nt) in enumerate(groups):
                if True:
                    if nt == 0 and oc + 1 < 16:
                        load_w(oc + 1)
                    if gi + 2 < len(groups):
                        load_xr(gi + 2)
                    pieces = wts[oc]
                    t0 = tok0 + nt * 512
                    b, bk = mainbank()
                    xr, xk = xrs.pop(gi)
                    for (p0, n, wv, wk) in pieces:
                        for kk in range(n):
                            kc = p0 + kk
                            rhs, rk = src_fn(kc, nt)
                            S.pe(lambda e, b=b, wv=wv, kk=kk, rhs=rhs, kc=kc: e.matmul(ps[b][:], wv[:, kk, :], rhs, start=(kc == 0), stop=(kc == KC - 1)),
                                 reads=[wk, rk], writes=[bk])
                    if pend[0]:
                        pend[0]()
                        pend[0] = None
                    S.dve(lambda e, xr=xr, b=b: e.scalar_tensor_tensor(out=xr, in0=xr, scalar=ALPHA, in1=ps[b][:], op0=ALU.mult, op1=ALU.add),
                          reads=[xk, bk], writes=[xk])
                    S.act(lambda e, xr=xr: e.activation(out=ybf, in_=xr, func=AF.Copy), reads=[xk], writes=["ybf"])
                    S.act(lambda e, xr=xr: e.activation(out=ysq, in_=xr, func=AF.Square), reads=[xk], writes=["ysq"])
                    ydst = X32[oc * 128:(oc + 1) * 128, t0:t0 + 512]
                    S.dma("pool", lambda e, xr=xr, ydst=ydst: e.dma_start(out=ydst, in_=xr), reads=[xk], writes=[K("x32", oc, t0)])

                    def mk(oc=oc, nt=nt):
                        pm, pq = 4 + 2 * nt, 5 + 2 * nt
                        S.pe(lambda e: e.matmul(ps[pm][:], onesD, ybf, start=(oc == 0), stop=(oc == 15)), reads=["ybf", "cb16"], writes=[f"ps{pm}"])
                        S.pe(lambda e: e.matmul(ps[pq][:], onesD, ysq, start=(oc == 0), stop=(oc == 15)), reads=["ysq", "cb16"], writes=[f"ps{pq}"])
                    pend[0] = mk
            pend[0]()
            for nt in range(NTL):
                t0 = tok0 + nt * 512
                pm, pq = 4 + 2 * nt, 5 + 2 * nt
                rstd, nmr = rstd_t[nt], nmr_t[nt]
                S.act(lambda e, pm=pm: e.activation(out=mean_t, in_=ps[pm][:], func=AF.Copy), reads=[f"ps{pm}"], writes=["mean"])
                S.dve(lambda e: e.tensor_tensor(out=tmp_t, in0=mean_t, in1=mean_t, op=ALU.mult), reads=["mean"], writes=["tmp"])
                S.dve(lambda e, pq=pq: e.tensor_tensor(out=tmp_t, in0=ps[pq][:], in1=tmp_t, op=ALU.subtract), reads=[f"ps{pq}", "tmp"], writes=["tmp"])
                S.act(lambda e: e.activation(out=tmp_t, in_=tmp_t, func=AF.Sqrt, bias=1e-5, scale=1.0), reads=["tmp"], writes=["tmp"])
                S.dve(lambda e, rstd=rstd: e.reciprocal(out=rstd, in_=tmp_t), reads=["tmp"], writes=[f"rstd{nt}"])
                S.dve(lambda e, rstd=rstd, nmr=nmr: e.scalar_tensor_tensor(out=nmr, in0=mean_t, scalar=-1.0, in1=rstd, op0=ALU.mult, op1=ALU.mult),
                      reads=["mean", f"rstd{nt}"], writes=[f"nmr{nt}"])
                for oc in range(16):
                    yi = ctr["yl"] % 3
                    ctr["yl"] += 1
                    yl, yk = YL[yi], f"yl{yi}"
                    oi = ctr["obf"] % 2
                    ctr["obf"] += 1
                    ob, obk = OBF[oi], f"obf{oi}"
                    ysrc = X32[oc * 128:(oc + 1) * 128, t0:t0 + 512]
                    S.dma("sp", lambda e, yl=yl, ysrc=ysrc: e.dma_start(out=yl, in_=ysrc), reads=[K("x32", oc, t0)], writes=[yk])
                    S.dve(lambda e, yl=yl, rstd=rstd: e.tensor_tensor(out=yl, in0=yl, in1=rstd, op=ALU.mult), reads=[yk, f"rstd{nt}"], writes=[yk])
                    S.dve(lambda e, yl=yl, nmr=nmr: e.tensor_tensor(out=yl, in0=yl, in1=nmr, op=ALU.add), reads=[yk, f"nmr{nt}"], writes=[yk])
                    gcol = cvec[:, C_LNG + lncol + oc:C_LNG + lncol + oc + 1]
                    bcol = cvec[:, C_LNB + lncol + oc:C_LNB + lncol + oc + 1]
                    S.act(lambda e, yl=yl, ob=ob, gcol=gcol, bcol=bcol: e.activation(out=ob, in_=yl, func=AF.Identity, bias=bcol, scale=gcol),
                          reads=[yk, "cvec"], writes=[obk])
                    S.act(lambda e, yl=yl, gcol=gcol, bcol=bcol: e.activation(out=yl, in_=yl, func=AF.Identity, bias=bcol, scale=gcol),
                          reads=[yk, "cvec"], writes=[yk])
                    odst = x32_out[oc * 128:(oc + 1) * 128, t0:t0 + 512]
                    S.dma("pool", lambda e, yl=yl, odst=odst: e.dma_start(out=odst, in_=yl), reads=[yk], writes=[K(okey, oc, t0)])
                    bdst = xb_out[oc * 128:(oc + 1) * 128, t0:t0 + 512]
                    S.dma("pool", lambda e, ob=ob, bdst=bdst: e.dma_start(out=bdst, in_=ob), reads=[obk], writes=[K("xb", oc, t0)])

        def ffn_phase(l, xb_in, xres, x32_out, xb_out, okey):
            S.barrier()
            al = Alloc(arena, PHASE_BASE)
            xs = al.get([16, 1024], BF16)
            actT = al.get([NFC, 1024], BF16)
            hraw = [[al.get([514], F32) for _ in range(2)] for _ in range(2)]
            tt = [[al.get([512], F32) for _ in range(2)] for _ in range(2)]
            sg = [al.get([512], F32) for _ in range(2)]
            halo = al.get([86, 2], F32)
            Wup = ffn_w_up[l]
            Wdn = ffn_w_down[l]
            for tile in range(NT // 1024):
                tok0 = tile * 1024
                seq_start = (tok0 % SEQ == 0)
                load_xs(xs, xb_in, tok0, 1024)
                for j in range(NFC):
                    for half in range(2):
                        c = j + NFC * half
                        wv, wk = wload(Wup, 0, 16, c * 128, 128)
                        cw = [cvec[:, C_CW + (l * 3 + tap) * 86 + c:C_CW + (l * 3 + tap) * 86 + c + 1] for tap in range(3)]
                        cbc = cvec[:, C_CB + l * 86 + c:C_CB + l * 86 + c + 1]
                        for nt in range(2):
                            b, bk = mainbank()
                            for kc in range(16):
                                S.pe(lambda e, b=b, wv=wv, kc=kc, nt=nt: e.matmul(ps[b][:], wv[:, kc, :], xs[:, kc, nt * 512:(nt + 1) * 512],
                                                                                  start=(kc == 0), stop=(kc == 15)),
                                     reads=[wk, f"xs{kc // 4}"], writes=[bk])
                            hr, hk = hraw[half][nt], K("hr", half, nt)
                            t_, tk = tt[half][nt], K("tt", half, nt)
                            if nt == 0 and seq_start:
                                S.act(lambda e, hr=hr: e.memzero(hr[:, 0:2]), writes=[hk])
                            else:
                                S.act(lambda e, hr=hr, c=c: e.copy(out=hr[:, 0:2], in_=halo[:, c, :]), reads=[K("halo", c)], writes=[hk])
                            S.act(lambda e, hr=hr, b=b: e.copy(out=hr[:, 2:514], in_=ps[b][:]), reads=[bk], writes=[hk])
                            S.act(lambda e, hr=hr, c=c: e.copy(out=halo[:, c, :], in_=hr[:, 512:514]), reads=[hk], writes=[K("halo", c)])
                            S.act(lambda e, t_=t_, b=b, cw=cw, cbc=cbc: e.activation(out=t_, in_=ps[b][:], func=AF.Identity, bias=cbc, scale=cw[2]),
                                  reads=[bk, "cvec"], writes=[tk])
                            S.dve(lambda e, t_=t_, hr=hr, cw=cw: e.scalar_tensor_tensor(out=t_, in0=hr[:, 1:513], scalar=cw[1], in1=t_, op0=ALU.mult, op1=ALU.add),
                                  reads=[hk, tk, "cvec"], writes=[tk])
                            S.dve(lambda e, t_=t_, hr=hr, cw=cw: e.scalar_tensor_tensor(out=t_, in0=hr[:, 0:512], scalar=cw[0], in1=t_, op0=ALU.mult, op1=ALU.add),
                                  reads=[hk, tk, "cvec"], writes=[tk])
                            if half == 0:
                                S.act(lambda e, t_=t_, nt=nt: e.activation(out=sg[nt], in_=t_, func=AF.Silu), reads=[tk], writes=[K("sg", nt)])
                            else:
                                S.dve(lambda e, t_=t_, nt=nt, j=j: e.tensor_tensor(out=actT[:, j, nt * 512:(nt + 1) * 512], in0=t_, in1=sg[nt], op=ALU.mult),
                                      reads=[tk, K("sg", nt)], writes=[K("act", j, nt)])
                outproj_ln(lambda kc, nt: (actT[:, kc, nt * 512:(nt + 1) * 512], K("act", kc, nt)),
                           NFC, Wdn, tok0, 2, xres, x32_out, xb_out, (l * 2 + 1) * 16, okey)

        def gla_phase(l, xb_in, xres, x32_out, xb_out, okey):
            j = l // 2
            S.barrier()
            al = Alloc(arena, PHASE_BASE)
            xs = al.get([16, 512], BF16)
            og = xs
            qT = al.get([8, 512], BF16)
            kT = al.get([8, 512], BF16)
            vsb = al.get([4, 2048], BF16)
            rs = al.get([16, 512], BF16)
            glow = al.get([512], F32, parts=17)
            wg = al.get([1024], F32, parts=17)
            S32 = al.get([8, 512], F32)
            Sbf = al.get([8, 512], BF16)
            u = al.get([1024], F32)
            EB = al.get([8, 128], F32)
            ENB = al.get([8, 128], F32)
            qs = al.get([8, 128], BF16)
            ks = al.get([8, 128], BF16)
            kh = al.get([8, 128], BF16)
            khat = al.get([1024], BF16)
            PT = al.get([4, 128], BF16)
            sq = al.get([16, 128], BF16)
            rsq = al.get([4, 128], F32)
            t1 = al.get([16, 128], F32)
            Win = gla_w_in[j]
            Wout = gla_w_out[j]
            S.dma("sp", lambda e: e.dma_start(out=wg, in_=wg_d[j]), writes=["wg"])
            S.dve(lambda e: e.memset(glow, 1.0), writes=["glow"])
            ps5bf = ps[5][:].bitcast(BF16)
            XS_KEYS = [f"xs{g}" for g in range(4)]
            for tile in range(NT // 512):
                tok0 = tile * 512
                seq_start = (tok0 % SEQ == 0)
                load_xs(xs, xb_in, tok0, 512)
                if seq_start:
                    S.dve(lambda e: e.memset(S32, 0.0), writes=[K("S32", h) for h in range(8)])
                    S.dve(lambda e: e.memset(Sbf, 0.0), writes=[K("Sbf", h) for h in range(8)])

                def fm_chunk(cid, evac):
                    wv, wk = wload_ws(j, cid)
                    b, bk = mainbank()
                    for kc in range(16):
                        S.pe(lambda e, b=b, wv=wv, kc=kc: e.matmul(ps[b][:], wv[:, kc, :], xs[:, kc, :], start=(kc == 0), stop=(kc == 15)),
                             reads=[wk, f"xs{kc // 4}"], writes=[bk])
                    evac(b, bk)
                for c in range(8):
                    fm_chunk(c, lambda b, bk, c=c: S.act(lambda e: e.activation(out=qT[:, c, :], in_=ps[b][:], func=AF.Identity, scale=0.0625),
                                                                reads=[bk], writes=[K("qT", c)]))
                for c in range(8):
                    fm_chunk(8 + c, lambda b, bk, c=c: S.dve(lambda e: e.tensor_copy(out=kT[:, c, :], in_=ps[b][:]),
                                                                       reads=[bk], writes=[K("kT", c)]))
                for c in range(16):
                    fm_chunk(16 + c, lambda b, bk, c=c: S.act(lambda e: e.activation(out=rs[:, c, :], in_=ps[b][:], func=AF.Silu),
                                                                       reads=[bk], writes=[K("rs", c)]))
                wv, wk = wload(Win, 0, 16, 6144, 16)
                b, bk = mainbank()
                for kc in range(16):
                    S.pe(lambda e, b=b, wv=wv, kc=kc: e.matmul(ps[b][0:16, :], wv[:, kc, :], xs[:, kc, :], start=(kc == 0), stop=(kc == 15)),
                         reads=[wk, f"xs{kc // 4}"], writes=[bk])
                S.act(lambda e, b=b: e.copy(out=glow[0:16, :], in_=ps[b][0:16, :]), reads=[bk], writes=["glow"])
                for cg in range(16):
                    wv, wk = wload_ws(j, 32 + cg)
                    b, bk = mainbank()
                    for tc in range(4):
                        for kc in range(16):
                            S.pe(lambda e, b=b, wv=wv, kc=kc, tc=tc: e.matmul(ps[b][:, tc * 128:(tc + 1) * 128], xs[:, kc, tc * 128:(tc + 1) * 128], wv[:, kc, :],
                                                                              start=(kc == 0), stop=(kc == 15)),
                                 reads=[wk, f"xs{kc // 4}"], writes=[bk])
                    S.act(lambda e, b=b, cg=cg: e.copy(out=vsb[:, :, cg * 128:(cg + 1) * 128], in_=ps[b][:].rearrange("p (a b) -> p a b", b=128)),
                          reads=[bk], writes=[K("v", cg)])
                VK = [K("v", cg) for cg in range(16)]
                for tc in range(4):
                    tsl = slice(tc * 128, (tc + 1) * 128)
                    for hf in range(2):
                        S.pe(lambda e, hf=hf, tsl=tsl: e.matmul(ps[hf][:], glow[0:17, tsl], wg[0:17, hf * 512:(hf + 1) * 512], start=True, stop=True),
                             reads=["glow", "wg"], writes=[f"ps{hf}"])
                        S.act(lambda e, hf=hf: e.activation(out=u[:, hf * 512:(hf + 1) * 512], in_=ps[hf][:], func=AF.Exp, scale=-1.0),
                              reads=[f"ps{hf}"], writes=[K("u", hf)])
                        S.act(lambda e, hf=hf: e.activation(out=u[:, hf * 512:(hf + 1) * 512], in_=u[:, hf * 512:(hf + 1) * 512], func=AF.Ln, bias=1.0, scale=1.0),
                              reads=[K("u", hf)], writes=[K("u", hf)])
                    for dc in range(8):
                        pb = 2 + dc // 4
                        S.pe(lambda e, dc=dc, pb=pb: e.matmul(ps[pb][:, (dc % 4) * 128:(dc % 4 + 1) * 128], u[:, dc * 128:(dc + 1) * 128], tri, start=True, stop=True),
                             reads=[K("u", dc // 4), "cm32"], writes=[f"ps{pb}"])
                    for hf in range(2):
                        pv = ps[2 + hf][:].rearrange("p (a b) -> p a b", b=128)
                        S.act(lambda e, hf=hf, pv=pv: e.activation(out=EB[:, hf * 4:(hf + 1) * 4, :], in_=pv, func=AF.Exp), reads=[f"ps{2 + hf}"], writes=[K("EB", hf)])
                        S.act(lambda e, hf=hf, pv=pv: e.activation(out=ENB[:, hf * 4:(hf + 1) * 4, :], in_=pv, func=AF.Exp, scale=-1.0), reads=[f"ps{2 + hf}"], writes=[K("ENB", hf)])
                    EBK = [K("EB", 0), K("EB", 1)]
                    ENBK = [K("ENB", 0), K("ENB", 1)]
                    S.dve(lambda e, tsl=tsl: e.tensor_tensor(out=qs, in0=qT[:, :, tsl], in1=EB, op=ALU.mult), reads=[K("qT", c) for c in range(8)] + EBK, writes=["qs"])
                    S.dve(lambda e, tsl=tsl: e.tensor_tensor(out=ks, in0=kT[:, :, tsl], in1=ENB, op=ALU.mult), reads=[K("kT", c) for c in range(8)] + ENBK, writes=["ks"])
                    S.dve(lambda e: e.tensor_tensor(out=kh, in0=ks, in1=EB[:, :, 127:128].broadcast_to([128, 8, 128]), op=ALU.mult), reads=["ks"] + EBK, writes=["kh"])
                    for h in range(4):
                        for dl in range(2):
                            S.pe(lambda e, h=h, dl=dl: e.matmul(ps[4][:, h * 128:(h + 1) * 128], ks[:, 2 * h + dl, :], qs[:, 2 * h + dl, :], start=(dl == 0), stop=(dl == 1)),
                                 reads=["ks", "qs"], writes=["ps4"])
                    S.dve(lambda e: e.tensor_tensor(out=PT, in0=ps[4][:].rearrange("p (a b) -> p a b", b=128),
                                                    in1=mcur.unsqueeze(1).broadcast_to([128, 4, 128]), op=ALU.mult), reads=["ps4", "cm32"], writes=["PT"])
                    for dc in range(8):
                        S.pe(lambda e, dc=dc: e.transpose(ps5bf[:, dc * 128:(dc + 1) * 128], kh[:, dc, :], ident), reads=["kh", "cb16"], writes=["ps5"])
                    S.act(lambda e: e.copy(out=khat, in_=ps5bf), reads=["ps5"], writes=["khat"])
                    for h in range(4):
                        for ec in range(4):
                            o_ = ps[h][:, ec * 128:(ec + 1) * 128]
                            S.pe(lambda e, h=h, ec=ec, o_=o_, tc=tc: e.matmul(o_, vsb[:, tc, h * 512 + ec * 128:h * 512 + (ec + 1) * 128], PT[:, h, :], start=True, stop=False),
                                 reads=VK + ["PT"], writes=[f"ps{h}"])
                            for dl in range(2):
                                S.pe(lambda e, h=h, ec=ec, o_=o_, dl=dl: e.matmul(o_, Sbf[:, 2 * h + dl, ec * 128:(ec + 1) * 128], qs[:, 2 * h + dl, :], start=False, stop=(dl == 1)),
                                     reads=[K("Sbf", 2 * h + dl), "qs"], writes=[f"ps{h}"])
                    for h in range(4):
                        S.act(lambda e, h=h: e.activation(out=sq[:, h * 4:(h + 1) * 4, :], in_=ps[h][:].rearrange("p (a b) -> p a b", b=128), func=AF.Square),
                              reads=[f"ps{h}"], writes=[K("sq", h)])
                    for h in range(4):
                        for ec in range(4):
                            S.pe(lambda e, h=h, ec=ec: e.matmul(ps[6][:, h * 128:(h + 1) * 128], ones512, sq[:, h * 4 + ec, :], start=(ec == 0), stop=(ec == 3)),
                                 reads=[K("sq", h), "cb16"], writes=["ps6"])
                    S.act(lambda e: e.activation(out=rsq, in_=ps[6][:].rearrange("p (a b) -> p a b", b=128), func=AF.Sqrt, bias=1e-6, scale=1.0), reads=["ps6"], writes=["rsq"])
                    S.dve(lambda e: e.reciprocal(out=rsq, in_=rsq), reads=["rsq"], writes=["rsq"])
                    for h in range(4):
                        S.dve(lambda e, h=h: e.tensor_tensor(out=t1[:, h * 4:(h + 1) * 4, :], in0=ps[h][:].rearrange("p (a b) -> p a b", b=128),
                                                             in1=rsq[:, h:h + 1, :].broadcast_to([128, 4, 128]), op=ALU.mult),
                              reads=[f"ps{h}", "rsq"], writes=[K("t1", h)])
                    for ec in range(4):
                        ngc = cvec[:, C_NG + j * 4 + ec:C_NG + j * 4 + ec + 1]
                        S.dve(lambda e, ec=ec, ngc=ngc, tsl=tsl: e.scalar_tensor_tensor(out=og[:, ec::4, tsl], in0=t1[:, ec::4, :], scalar=ngc, in1=rs[:, ec::4, tsl],
                                                                                       op0=ALU.mult, op1=ALU.mult),
                              reads=[K("t1", h) for h in range(4)] + [K("rs", c) for c in range(16)] + ["cvec"], writes=XS_KEYS + [K("og", tc)])
                    for hd in range(8):
                        h = hd // 2
                        pb = (4, 5, 7)[hd % 3]
                        S.pe(lambda e, hd=hd, h=h, pb=pb, tc=tc: e.matmul(ps[pb][:], khat[:, hd * 128:(hd + 1) * 128], vsb[:, tc, h * 512:(h + 1) * 512], start=True, stop=True),
                             reads=["khat"] + VK, writes=[f"ps{pb}"])
                        S.dve(lambda e, hd=hd, pb=pb: e.scalar_tensor_tensor(out=S32[:, hd, :], in0=S32[:, hd, :], scalar=EB[:, hd, 127:128], in1=ps[pb][:],
                                                                             op0=ALU.mult, op1=ALU.add),
                              reads=[K("S32", hd), f"ps{pb}"] + EBK, writes=[K("S32", hd)])
                        S.act(lambda e, hd=hd: e.copy(out=Sbf[:, hd, :], in_=S32[:, hd, :]), reads=[K("S32", hd)], writes=[K("Sbf", hd)])
                outproj_ln(lambda kc, nt: (og[:, kc, :], K("og", 0)), 16, Wout, tok0, 1, xres, x32_out, xb_out, (l * 2) * 16, okey, wfn=lambda oc: wload_ws(j, 48 + oc))

        def dil_phase(l, xb_in, xres, x32_out, xb_out, okey):
            j = l // 2
            S.barrier()
            al = Alloc(arena, PHASE_BASE)
            xs = al.get([16, 2048], BF16)
            og = al.get([8, 2048], BF16)
            acc2 = al.get([2, 2048], F32)
            qTb = [al.get([2048], BF16) for _ in range(2)]
            kTb = [al.get([2048], BF16) for _ in range(2)]
            vT = al.get([2048], BF16)
            vtokb = [al.get([16, 128], BF16) for _ in range(2)]
            pexp = [al.get([2, 128], F32) for _ in range(2)]
            PTb = [al.get([2, 128], BF16) for _ in range(2)]
            Win = dil_w_in[j]
            Wout = dil_w_out[j]
            ps3bf = ps[3][:].bitcast(BF16)
            cnt = {"hg": 0, "blk": 0}
            for s in range(NSEQ):
                tokS = s * SEQ
                load_xs(xs, xb_in, tokS, 2048)
                for h in range(8):
                    for gi, dd in enumerate((1, 4, 16)):
                        hg = cnt["hg"] % 2
                        cnt["hg"] += 1
                        for _ in range(2):
                            if pending_work:
                                pending_work.popleft()()
                        qT, kT, vtok = qTb[hg], kTb[hg], vtokb[hg]
                        qk_, kk_, vk_, vtk = K("q", hg), K("k", hg), "vT", K("vtok", hg)
                        for t in range(3):
                            wv, wk = wload(Win, 0, 16, ((gi * 3 + t) * 8 + h) * 128, 128)
                            for nt in range(4):
                                b, bk = mainbank(3)
                                for kc in range(16):
                                    S.pe(lambda e, b=b, wv=wv, kc=kc, nt=nt: e.matmul(ps[b][:], wv[:, kc, :], xs[:, kc, nt * 512:(nt + 1) * 512], start=(kc == 0), stop=(kc == 15)),
                                         reads=[wk, f"xs{kc // 4}"], writes=[bk])
                                nsl = slice(nt * 512, (nt + 1) * 512)
                                if t == 0:
                                    S.act(lambda e, b=b, qT=qT, nsl=nsl: e.activation(out=qT[:, nsl], in_=ps[b][:], func=AF.Identity, scale=float(128 ** -0.5)),
                                          reads=[bk], writes=[qk_])
                                elif t == 1:
                                    S.dve(lambda e, b=b, kT=kT, nsl=nsl: e.tensor_copy(out=kT[:, nsl], in_=ps[b][:]), reads=[bk], writes=[kk_])
                                else:
                                    S.act(lambda e, b=b, nsl=nsl: e.copy(out=vT[:, nsl], in_=ps[b][:]), reads=[bk], writes=[vk_])
                        nb = 16 // dd

                        def tsl(blk):
                            r, bb = blk // nb, blk % nb
                            return slice(r + dd * 128 * bb, r + dd * 128 * bb + dd * 127 + 1, dd)
                        for half in range(2):
                            for bi in range(8):
                                blk = half * 8 + bi
                                vsl = tsl(blk)
                                S.pe(lambda e, bi=bi, vsl=vsl: e.transpose(ps3bf[:, bi * 128:(bi + 1) * 128], vT[:, vsl], ident), reads=[vk_, "cb16"], writes=["ps3"])
                            S.dve(lambda e, half=half, vtok=vtok: e.tensor_copy(out=vtok[:, half * 8:(half + 1) * 8, :], in_=ps3bf.rearrange("p (a b) -> p a b", b=128)),
                                  reads=["ps3"], writes=[vtk])
                        for blk in range(16):
                            bb = blk % nb
                            j0 = 0 if bb > 0 else 1
                            ci = cnt["blk"] % 2
                            cnt["blk"] += 1
                            psS = ps[4 + ci][:].rearrange("p (a b) -> p a b", b=128)
                            psO = ps[6 + ci][:].rearrange("p (a b) -> p a b", b=128)
                            pe_, PT = pexp[ci], PTb[ci]
                            cs = tsl(blk)
                            if bb > 0:
                                psl = tsl(blk - 1)
                                S.pe(lambda e, psS=psS, kT=kT, qT=qT, psl=psl, cs=cs: e.matmul(psS[:, 0, :], kT[:, psl], qT[:, cs], start=True, stop=True),
                                     reads=[kk_, qk_], writes=[f"ps{4 + ci}"])
                            S.pe(lambda e, psS=psS, kT=kT, qT=qT, cs=cs: e.matmul(psS[:, 1, :], kT[:, cs], qT[:, cs], start=True, stop=True),
                                 reads=[kk_, qk_], writes=[f"ps{4 + ci}"])
                            S.act(lambda e, psS=psS, pe_=pe_, j0=j0: e.activation(out=pe_[:, j0:2, :], in_=psS[:, j0:2, :], func=AF.Exp), reads=[f"ps{4 + ci}"], writes=[K("pexp", ci)])
                            S.dve(lambda e, pe_=pe_, PT=PT, j0=j0: e.tensor_tensor(out=PT[:, j0:2, :], in0=pe_[:, j0:2, :], in1=masks[:, j0:2, :], op=ALU.mult),
                                  reads=[K("pexp", ci), "cm32"], writes=[K("PT", ci)])
                            for jj in range(j0, 2):
                                bk_ = blk - 1 if jj == 0 else blk
                                S.pe(lambda e, psO=psO, vtok=vtok, PT=PT, jj=jj, bk_=bk_, j0=j0: e.matmul(psO[:, 0, :], vtok[:, bk_, :], PT[:, jj, :], start=(jj == j0), stop=(jj == 1)),
                                     reads=[vtk, K("PT", ci)], writes=[f"ps{6 + ci}"])
                            for jj in range(j0, 2):
                                S.pe(lambda e, psO=psO, PT=PT, jj=jj, j0=j0: e.matmul(psO[:, 1, :], ones1, PT[:, jj, :], start=(jj == j0), stop=(jj == 1)),
                                     reads=["cb16", K("PT", ci)], writes=[f"ps{6 + ci}"])
                            if gi == 0:
                                S.dve(lambda e, psO=psO, cs=cs: e.tensor_copy(out=acc2[:, :, cs], in_=psO[:, 0:2, :]), reads=[f"ps{6 + ci}"], writes=["acc2"])
                            else:
                                S.dve(lambda e, psO=psO, cs=cs: e.tensor_tensor(out=acc2[:, :, cs], in0=acc2[:, :, cs], in1=psO[:, 0:2, :], op=ALU.add),
                                      reads=[f"ps{6 + ci}", "acc2"], writes=["acc2"])
                    S.dve(lambda e: e.reciprocal(out=acc2[:, 1, :], in_=acc2[:, 1, :]), reads=["acc2"], writes=["acc2"])
                    S.dve(lambda e, h=h: e.tensor_tensor(out=og[:, h, :], in0=acc2[:, 0, :], in1=acc2[:, 1, :], op=ALU.mult), reads=["acc2"], writes=[K("og", h)])
                for half in range(2):
                    outproj_ln(lambda kc, nt, half=half: (og[:, kc, half * 1024 + nt * 512:half * 1024 + (nt + 1) * 512], K("og", kc)),
                               8, Wout, tokS + half * 1024, 2, xres, x32_out, xb_out, (l * 2) * 16, okey)

        cur = None
        for f in gla_conv_ops(0):
            f()
        for sidx in range(NSUB):
            l = sidx // 2
            if sidx == 2 and NSUB > 4:
                pending_work.extend(gla_conv_ops(1))
            if sidx == 3:
                while pending_work:
                    pending_work.popleft()()
            last = (sidx == NSUB - 1)
            xres = xT if sidx == 0 else X32
            x32_out = outT if last else X32
            okey = "out" if last else "x32"
            xb_in = xT if cur is None else XB[cur]
            nxt = 0 if cur is None else 1 - cur
            xb_out = XB[nxt]
            if sidx % 2 == 1:
                ffn_phase(l, xb_in, xres, x32_out, xb_out, okey)
            elif l % 2 == 0:
                gla_phase(l, xb_in, xres, x32_out, xb_out, okey)
            else:
                dil_phase(l, xb_in, xres, x32_out, xb_out, okey)
            cur = nxt
        S.barrier()
        S.op("sp", lambda e: e.nop())
        S.emit()
    return nc


def host_consts(inputs):
    f = np.float32
    cvec = np.zeros((128, NCV), f)
    cvec[:, C_LNG:C_LNG + 128] = np.asarray(inputs["ln_g"], f).reshape(8, 16, 128).transpose(2, 0, 1).reshape(128, 128)
    cvec[:, C_LNB:C_LNB + 128] = np.asarray(inputs["ln_b"], f).reshape(8, 16, 128).transpose(2, 0, 1).reshape(128, 128)
    cvec[:, C_CW:C_CW + 1032] = np.asarray(inputs["ffn_conv_w"], f).reshape(4, 3, 86, 128).transpose(3, 0, 1, 2).reshape(128, 1032)
    cvec[:, C_CB:C_CB + 344] = np.asarray(inputs["ffn_conv_b"], f).reshape(4, 86, 128).transpose(2, 0, 1).reshape(128, 344)
    cvec[:, C_NG:C_NG + 8] = np.asarray(inputs["gla_norm_g"], f).reshape(2, 4, 128).transpose(2, 0, 1).reshape(128, 8)
    cmat = np.zeros((128, 896), f)
    ii = np.arange(128)
    le = (ii[:, None] <= ii[None, :]).astype(f)
    ge = (ii[:, None] >= ii[None, :]).astype(f)
    cmat[:, 0:128] = le * f(-1.0 / 16.0)
    cmat[:, 128:256] = ge
    cmat[:, 256:384] = le
    cmat[:, 384:512] = np.eye(128, dtype=f)
    cmat[:, 512:640] = f(1.0 / 2048.0)
    cmat[:, 640:768] = f(1.0 / 512.0)
    cmat[:, 768:896] = f(1.0)
    wg = np.concatenate([np.asarray(inputs["gla_w_gate_up"], f), np.asarray(inputs["gla_gate_bias"], f)[:, None, :]], axis=1)
    return cvec, cmat, np.ascontiguousarray(wg)


_PROG = {}


def run(inputs, ncores=NCORES, nseq=NSEQ_CORE, nsub=8, trace=False):
    key = (nseq, nsub)
    if key not in _PROG:
        _PROG[key] = build(nseq, nsub)
    nc = _PROG[key]
    cvec, cmat, wg = host_consts(inputs)
    x = np.asarray(inputs["x"], np.float32)
    shared = {k: np.ascontiguousarray(np.asarray(inputs[k], np.float32)) for k in
              ("gla_w_in", "gla_w_out", "dil_w_in", "dil_w_out", "ffn_w_up", "ffn_w_down")}
    in_maps = []
    for c in range(ncores):
        xc = x[c * nseq:(c + 1) * nseq].reshape(nseq * SEQ, D)
        m = dict(shared)
        m.update(xT=np.ascontiguousarray(xc.T), cvec=cvec, cmat=cmat, wg_aug=wg)
        in_maps.append(m)
    res = run_bass_kernel_spmd(nc, in_maps, core_ids=list(range(ncores)), **({"trace": True} if trace else {}))
    out = np.empty((ncores * nseq, SEQ, D), np.float32)
    for c in range(ncores):
        oT = np.asarray(res.results[c]["outT"])
        out[c * nseq:(c + 1) * nseq] = oT.T.reshape(nseq, SEQ, D)
    return out, res


def kernel(**inputs):
    out, _ = run(inputs)
    return out
```

```python
import contextlib
from collections import deque
import numpy as np
import concourse.bass as bass
import concourse.mybir as mybir
from concourse.bass_utils import run_bass_kernel_spmd

F32 = mybir.dt.float32
BF16 = mybir.dt.bfloat16
AF = mybir.ActivationFunctionType
ALU = mybir.AluOpType

D = 2048
SEQ = 2048
DFF = 5504
NFC = 43
ALPHA = float(8 ** 0.25)
NCORES = 8
NSEQ_CORE = 2

ENGS = ["pe", "act", "dve", "pool", "sp"]
NDMASEM = 16
EPOCH = 30000
NWB = 6
ARENA_BYTES = 211968

NCV = 1640
C_LNG, C_LNB, C_CW, C_CB, C_NG = 0, 128, 256, 1288, 1632


def K(*a):
    return "_".join(map(str, a))


class Sched:
    def __init__(self, nc):
        self.nc = nc
        self.ops = []
        self.last_w = {}
        self.readers = {}
        self.last_comp = {}
        self.recent_dma = {e: deque(maxlen=NDMASEM) for e in ENGS}
        self.bar_deps = set()
        self.bar_pending = set()

    def barrier(self):
        deps = set()
        for e in ENGS:
            if e in self.last_comp:
                deps.add(self.last_comp[e])
            deps.update(self.recent_dma[e])
        self.bar_deps = deps
        self.bar_pending = set(ENGS)
        self.last_w = {}
        self.readers = {}

    def op(self, eng, fn, reads=(), writes=(), dma=False):
        i = len(self.ops)
        deps = set()
        for k in reads:
            w = self.last_w.get(k)
            if w is not None:
                deps.add(w)
        for k in writes:
            w = self.last_w.get(k)
            if w is not None:
                deps.add(w)
            rs = self.readers.get(k)
            if rs:
                deps.update(rs)
        for k in reads:
            self.readers.setdefault(k, []).append(i)
        for k in writes:
            self.last_w[k] = i
            self.readers[k] = []
        if eng in self.bar_pending:
            deps |= self.bar_deps
            self.bar_pending.discard(eng)
        deps.discard(i)
        self.ops.append((eng, fn, deps, dma))
        if dma:
            self.recent_dma[eng].append(i)
        else:
            self.last_comp[eng] = i
        return i

    def pe(self, fn, reads=(), writes=()):
        return self.op("pe", fn, reads, writes)

    def act(self, fn, reads=(), writes=()):
        return self.op("act", fn, reads, writes)

    def dve(self, fn, reads=(), writes=()):
        return self.op("dve", fn, reads, writes)

    def dma(self, q, fn, reads=(), writes=()):
        return self.op(q, fn, reads, writes, dma=True)

    def emit(self):
        nc = self.nc
        ops = self.ops
        n = len(ops)
        need_inc = [False] * n
        for i, (eng, fn, deps, dma) in enumerate(ops):
            for d in deps:
                deng, _, _, ddma = ops[d]
                if ddma:
                    continue
                if deng == "pe" and eng == "pe" and not dma:
                    continue
                need_inc[d] = True
        cnt = {e: 0 for e in ENGS}
        dcnt = {e: 0 for e in ENGS}
        sig = [None] * n
        for i, (eng, fn, deps, dma) in enumerate(ops):
            if dma:
                k = dcnt[eng]
                dcnt[eng] += 1
                sig[i] = (("d", eng, k % NDMASEM), 16 * (k // NDMASEM + 1))
            elif need_inc[i]:
                c = cnt[eng]
                cnt[eng] += 1
                sig[i] = (("c", eng, c // EPOCH), (c % EPOCH) + 1)
        semkeys = sorted({s[0] for s in sig if s is not None})
        with contextlib.ExitStack() as st:
            sems = {}
            for k in semkeys:
                sems[k] = st.enter_context(nc.semaphore("s_" + "_".join(map(str, k))))
            block = st.enter_context(nc.Block())

            def run_engine(ename, e):
                waited = {}
                for i, (eng, fn, deps, dma) in enumerate(ops):
                    if eng != ename:
                        continue
                    wl = {}
                    for d in deps:
                        deng, _, _, ddma = ops[d]
                        if (not ddma) and deng == "pe" and eng == "pe" and not dma:
                            continue
                        sk, v = sig[d]
                        if waited.get(sk, 0) >= v:
                            continue
                        if wl.get(sk, 0) < v:
                            wl[sk] = v
                    if dma:
                        sk, v = sig[i]
                        if v > 16 and waited.get(sk, 0) < v - 16 and wl.get(sk, 0) < v - 16:
                            wl[sk] = v - 16
                    for sk, v in wl.items():
                        e.wait_ge(sems[sk], v)
                        waited[sk] = v
                    ins = fn(e)
                    if sig[i] is not None:
                        ins.then_inc(sems[sig[i][0]], 16 if dma else 1)

            @block.tensor
            def _(e):
                run_engine("pe", e)

            @block.scalar
            def _(e):
                run_engine("act", e)

            @block.vector
            def _(e):
                run_engine("dve", e)

            @block.gpsimd
            def _(e):
                run_engine("pool", e)

            @block.sync
            def _(e):
                run_engine("sp", e)


class Alloc:
    def __init__(self, arena, base=0):
        self.a = arena
        self.off = base

    def get(self, shape, dt, parts=128):
        n = int(np.prod(shape))
        nb = n * (4 if dt == F32 else 2)
        start = self.off
        self.off += (nb + 63) // 64 * 64
        assert self.off <= ARENA_BYTES, ("arena overflow", self.off)
        ap = self.a[0:parts, start // 2:(start + nb) // 2]
        if dt == F32:
            ap = ap.bitcast(F32)
        if len(shape) == 2:
            ap = ap.rearrange("p (a b) -> p a b", b=shape[1])
        elif len(shape) == 3:
            ap = ap.rearrange("p (a b c) -> p a b c", b=shape[1], c=shape[2])
        return ap


def build(NSEQ=NSEQ_CORE, NSUB=8):
    NT = NSEQ * SEQ
    nc = bass.Bass("TRN2", target_bir_lowering=False)

    def din(name, shape):
        return nc.dram_tensor(name, list(shape), F32, kind="ExternalInput").ap()

    xT = din("xT", [D, NT])
    gla_w_in = din("gla_w_in", [2, D, 6160])
    gla_w_out = din("gla_w_out", [2, D, D])
    dil_w_in = din("dil_w_in", [2, D, 9216])
    dil_w_out = din("dil_w_out", [2, 1024, D])
    ffn_w_up = din("ffn_w_up", [4, D, 11008])
    ffn_w_down = din("ffn_w_down", [4, DFF, D])
    cvec_d = din("cvec", [128, NCV])
    cmat_d = din("cmat", [128, 1024])
    wg_d = din("wg_aug", [2, 17, 1024])
    outT = nc.dram_tensor("outT", [D, NT], F32, kind="ExternalOutput").ap()
    X32 = nc.dram_tensor("X32", [D, NT], F32).ap()
    XB = [nc.dram_tensor(f"XB{i}", [D, NT], BF16).ap() for i in range(2)]
    WS = nc.dram_tensor("WS", [2, 64, 128, 2048], BF16).ap()

    with contextlib.ExitStack() as st:
        arena = st.enter_context(nc.sbuf_tensor("arena", [128, ARENA_BYTES // 2], BF16))
        ps = [st.enter_context(nc.psum_tensor(f"ps{i}", [128, 512], F32)) for i in range(8)]
        S = Sched(nc)

        cm = Alloc(arena, 0)
        cvec = cm.get([NCV], F32)
        cm32 = cm.get([4, 128], F32)
        tri = cm32[:, 0, :]
        M3 = cm32[:, 1:4, :]
        mcur = cm32[:, 3, :]
        cb16 = cm.get([4, 128], BF16)
        ident, onesD, ones512, ones1 = cb16[:, 0, :], cb16[:, 1, :], cb16[:, 2, :], cb16[:, 3, :]
        wpool = [cm.get([2048], BF16) for _ in range(NWB)]
        XR = [cm.get([512], F32) for _ in range(3)]
        ybf = cm.get([512], BF16)
        ysq = cm.get([512], BF16)
        mean_t = cm.get([512], F32)
        tmp_t = cm.get([512], F32)
        rstd_t = [cm.get([512], F32) for _ in range(2)]
        nmr_t = [cm.get([512], F32) for _ in range(2)]
        YL = [cm.get([512], F32) for _ in range(3)]
        OBF = [cm.get([512], BF16) for _ in range(2)]
        PHASE_BASE = cm.off

        S.dma("sp", lambda e: e.dma_start(out=cvec, in_=cvec_d), writes=["cvec"])
        S.dma("sp", lambda e: e.dma_start(out=cm32, in_=cmat_d[:, 0:512].rearrange("p (a b) -> p a b", b=128)), writes=["cm32"])
        S.dma("pool", lambda e: e.dma_start(out=cb16, in_=cmat_d[:, 512:1024].rearrange("p (a b) -> p a b", b=128)), writes=["cb16"])

        ctr = {"wb": 0, "bank": 0, "xr": 0, "yl": 0, "obf": 0}

        def wload(Wd, k0, nk, col0, ncols):
            i = ctr["wb"] % NWB
            ctr["wb"] += 1
            view = wpool[i][:, 0:nk * ncols].rearrange("p (a b) -> p a b", b=ncols)
            src = Wd[k0 * 128:(k0 + nk) * 128, col0:col0 + ncols].rearrange("(a p) m -> p a m", p=128)
            key = f"wb{i}"
            S.dma("pool", lambda e: e.dma_start(out=view, in_=src), writes=[key])
            return view, key

        def ws_src(j, cid):
            if cid < 48:
                col0 = (cid * 128 if cid < 8 else 1024 + (cid - 8) * 128 if cid < 16 else 4096 + (cid - 16) * 128 if cid < 32 else 2048 + (cid - 32) * 128)
                return gla_w_in[j], col0
            return gla_w_out[j], (cid - 48) * 128

        def gla_conv_ops(j):
            lst = []
            for cid in range(64):
                def f(cid=cid):
                    Wd, col0 = ws_src(j, cid)
                    view, key = wload(Wd, 0, 16, col0, 128)
                    dst = WS[j, cid].rearrange("p (a b) -> p a b", b=128)
                    S.dma("sp", lambda e: e.dma_start(out=dst, in_=view), reads=[key], writes=[K("ws", j, cid)])
                lst.append(f)
            return lst

        def wload_ws(j, cid):
            i = ctr["wb"] % NWB
            ctr["wb"] += 1
            view = wpool[i].rearrange("p (a b) -> p a b", b=128)
            src = WS[j, cid].rearrange("p (a b) -> p a b", b=128)
            key = f"wb{i}"
            S.dma("pool", lambda e: e.dma_start(out=view, in_=src), reads=[K("ws", j, cid)], writes=[key])
            return view, key

        def pk(b):
            return [f"ps{b}", f"ps{b}_0", f"ps{b}_1"]

        pending_work = deque()

        def pop_work(n=1):
            for _ in range(n):
                if pending_work:
                    pending_work.popleft()()

        def drain_work():
            while pending_work:
                pending_work.popleft()()

        def mainbank(nb=4):
            b = ctr["bank"] % nb
            ctr["bank"] += 1
            return b, f"ps{b}"

        def load_xs(xs, src, tok0, T):
            q = "pool" if src.dtype == F32 else "sp"
            for g in range(4):
                v = src[g * 512:(g + 1) * 512, tok0:tok0 + T].rearrange("(a p) t -> p a t", p=128)
                rk = [K("xb", c, t) for c in range(4 * g, 4 * g + 4) for t in range(tok0, tok0 + T, 512)]
                S.dma(q, lambda e, g=g, v=v: e.dma_start(out=xs[:, 4 * g:4 * g + 4, :], in_=v), reads=rk, writes=[f"xs{g}"])

        def outproj_ln(src_fn, KC, Wd, tok0, NTL, xres, x32_out, xb_out, lncol, okey, wfn=None, defer=True):
            pend = [None]
            groups = [(oc, nt) for oc in range(16) for nt in range(NTL)]
            wts = {}
            xrs = {}

            def load_w(oc):
                if wfn is not None:
                    wv, wk = wfn(oc)
                    wts[oc] = [(0, KC, wv, wk)]
                    return
                pieces = []
                for p0 in range(0, KC, 16):
                    n = min(16, KC - p0)
                    pieces.append((p0, n) + wload(Wd, p0, n, oc * 128, 128))
                wts[oc] = pieces

            def load_xr(g):
                oc, nt = groups[g]
                t0 = tok0 + nt * 512
                xi = ctr["xr"] % 3
                ctr["xr"] += 1
                xr, xk = XR[xi], f"xr{xi}"
                xsrc = xres[oc * 128:(oc + 1) * 128, t0:t0 + 512]
                S.dma("sp", lambda e, xr=xr, xsrc=xsrc: e.dma_start(out=xr, in_=xsrc), reads=[K("x32", oc, t0)], writes=[xk])
                xrs[g] = (xr, xk)

            load_w(0)
            for g in range(min(2, len(groups))):
                load_xr(g)
            for gi, (oc, nt) in enumerate(groups):
                if True:
                    if nt == 0 and oc + 1 < 16:
                        load_w(oc + 1)
                    if gi + 2 < len(groups):
                        load_xr(gi + 2)
                    pieces = wts[oc]
                    t0 = tok0 + nt * 512
                    b, bk = mainbank()
                    xr, xk = xrs.pop(gi)
                    for (p0, n, wv, wk) in pieces:
                        for kk in range(n):
                            kc = p0 + kk
                            rhs, rk = src_fn(kc, nt)
                            S.pe(lambda e, b=b, wv=wv, kk=kk, rhs=rhs, kc=kc: e.matmul(ps[b][:], wv[:, kk, :], rhs, start=(kc == 0), stop=(kc == KC - 1)),
                                 reads=[wk, rk], writes=[bk])
                    if pend[0]:
                        pend[0]()
                        pend[0] = None
                    S.dve(lambda e, xr=xr, b=b: e.scalar_tensor_tensor(out=xr, in0=xr, scalar=ALPHA, in1=ps[b][:], op0=ALU.mult, op1=ALU.add),
                          reads=[xk, bk], writes=[xk])
                    S.act(lambda e, xr=xr: e.activation(out=ybf, in_=xr, func=AF.Copy), reads=[xk], writes=["ybf"])
                    S.act(lambda e, xr=xr: e.activation(out=ysq, in_=xr, func=AF.Square), reads=[xk], writes=["ysq"])
                    ydst = X32[oc * 128:(oc + 1) * 128, t0:t0 + 512]
                    S.dma("act", lambda e, xr=xr, ydst=ydst: e.dma_start(out=ydst, in_=xr), reads=[xk], writes=[K("x32", oc, t0)])

                    def mk(oc=oc, nt=nt):
                        pm, pq = 4 + 2 * nt, 5 + 2 * nt
                        S.pe(lambda e: e.matmul(ps[pm][:], onesD, ybf, start=(oc == 0), stop=(oc == 15)), reads=["ybf", "cb16"], writes=pk(pm))
                        S.pe(lambda e: e.matmul(ps[pq][:], onesD, ysq, start=(oc == 0), stop=(oc == 15)), reads=["ysq", "cb16"], writes=pk(pq))
                    pend[0] = mk
            pend[0]()
            units = []
            for nt in range(NTL):
                t0 = tok0 + nt * 512
                pm, pq = 4 + 2 * nt, 5 + 2 * nt
                rstd, nmr = rstd_t[nt], nmr_t[nt]
                S.act(lambda e, pm=pm: e.activation(out=mean_t, in_=ps[pm][:], func=AF.Copy), reads=pk(pm), writes=["mean"])
                S.dve(lambda e: e.tensor_tensor(out=tmp_t, in0=mean_t, in1=mean_t, op=ALU.mult), reads=["mean"], writes=["tmp"])
                S.dve(lambda e, pq=pq: e.tensor_tensor(out=tmp_t, in0=ps[pq][:], in1=tmp_t, op=ALU.subtract), reads=pk(pq) + ["tmp"], writes=["tmp"])
                S.act(lambda e: e.activation(out=tmp_t, in_=tmp_t, func=AF.Sqrt, bias=1e-5, scale=1.0), reads=["tmp"], writes=["tmp"])
                S.dve(lambda e, rstd=rstd: e.reciprocal(out=rstd, in_=tmp_t), reads=["tmp"], writes=[f"rstd{nt}"])
                S.dve(lambda e, rstd=rstd, nmr=nmr: e.scalar_tensor_tensor(out=nmr, in0=mean_t, scalar=-1.0, in1=rstd, op0=ALU.mult, op1=ALU.mult),
                      reads=["mean", f"rstd{nt}"], writes=[f"nmr{nt}"])
                for oc in range(16):
                    def ld(oc=oc, nt=nt, t0=t0):
                        yi = ctr["yl"] % 3
                        ctr["yl"] += 1
                        yl, yk = YL[yi], f"yl{yi}"
                        ysrc = X32[oc * 128:(oc + 1) * 128, t0:t0 + 512]
                        S.dma("sp", lambda e, yl=yl, ysrc=ysrc: e.dma_start(out=yl, in_=ysrc), reads=[K("x32", oc, t0)], writes=[yk])
                        return yl, yk

                    def cp(yl, yk, oc=oc, nt=nt, t0=t0, rstd=rstd, nmr=nmr):
                        oi = ctr["obf"] % 2
                        ctr["obf"] += 1
                        ob, obk = OBF[oi], f"obf{oi}"
                        S.dve(lambda e, yl=yl, rstd=rstd: e.tensor_tensor(out=yl, in0=yl, in1=rstd, op=ALU.mult), reads=[yk, f"rstd{nt}"], writes=[yk])
                        S.dve(lambda e, yl=yl, nmr=nmr: e.tensor_tensor(out=yl, in0=yl, in1=nmr, op=ALU.add), reads=[yk, f"nmr{nt}"], writes=[yk])
                        gcol = cvec[:, C_LNG + lncol + oc:C_LNG + lncol + oc + 1]
                        bcol = cvec[:, C_LNB + lncol + oc:C_LNB + lncol + oc + 1]
                        S.act(lambda e, yl=yl, ob=ob, gcol=gcol, bcol=bcol: e.activation(out=ob, in_=yl, func=AF.Identity, bias=bcol, scale=gcol),
                              reads=[yk, "cvec"], writes=[obk])
                        S.act(lambda e, yl=yl, gcol=gcol, bcol=bcol: e.activation(out=yl, in_=yl, func=AF.Identity, bias=bcol, scale=gcol),
                              reads=[yk, "cvec"], writes=[yk])
                        odst = x32_out[oc * 128:(oc + 1) * 128, t0:t0 + 512]
                        S.dma("act", lambda e, yl=yl, odst=odst: e.dma_start(out=odst, in_=yl), reads=[yk], writes=[K(okey, oc, t0)])
                        bdst = xb_out[oc * 128:(oc + 1) * 128, t0:t0 + 512]
                        S.dma("act", lambda e, ob=ob, bdst=bdst: e.dma_start(out=bdst, in_=ob), reads=[obk], writes=[K("xb", oc, t0)])
                    units.append((ld, cp))
            nun = len(units)
            st_ = {}
            for i in range(nun + 2):
                def item(i=i):
                    if i - 2 >= 0:
                        units[i - 2][1](*st_.pop(i - 2))
                    if i < nun:
                        st_[i] = units[i][0]()
                if defer:
                    pending_work.append(item)
                else:
                    item()

        def ffn_phase(l, xb_in, xres, x32_out, xb_out, okey):
            S.barrier()
            al = Alloc(arena, PHASE_BASE)
            xs = al.get([16, 1024], BF16)
            actT = al.get([NFC, 1024], BF16)
            hraw = [[al.get([514], F32) for _ in range(2)] for _ in range(2)]
            tt = [[al.get([512], F32) for _ in range(2)] for _ in range(2)]
            sg = [al.get([512], F32) for _ in range(2)]
            halo = al.get([86, 2], F32)
            Wup = ffn_w_up[l]
            Wdn = ffn_w_down[l]
            for tile in range(NT // 1024):
                tok0 = tile * 1024
                seq_start = (tok0 % SEQ == 0)
                load_xs(xs, xb_in, tok0, 1024)
                for j in range(NFC):
                    for half in range(2):
                        c = j + NFC * half
                        pop_work(1)
                        wv, wk = wload(Wup, 0, 16, c * 128, 128)
                        cw = [cvec[:, C_CW + (l * 3 + tap) * 86 + c:C_CW + (l * 3 + tap) * 86 + c + 1] for tap in range(3)]
                        cbc = cvec[:, C_CB + l * 86 + c:C_CB + l * 86 + c + 1]
                        for nt in range(2):
                            b, bk = mainbank()
                            for kc in range(16):
                                S.pe(lambda e, b=b, wv=wv, kc=kc, nt=nt: e.matmul(ps[b][:], wv[:, kc, :], xs[:, kc, nt * 512:(nt + 1) * 512],
                                                                                  start=(kc == 0), stop=(kc == 15)),
                                     reads=[wk, f"xs{kc // 4}"], writes=[bk])
                            hr, hk = hraw[half][nt], K("hr", half, nt)
                            t_, tk = tt[half][nt], K("tt", half, nt)
                            if nt == 0 and seq_start:
                                S.act(lambda e, hr=hr: e.memzero(hr[:, 0:2]), writes=[hk])
                            else:
                                S.act(lambda e, hr=hr, c=c: e.copy(out=hr[:, 0:2], in_=halo[:, c, :]), reads=[K("halo", c)], writes=[hk])
                            S.act(lambda e, hr=hr, b=b: e.copy(out=hr[:, 2:514], in_=ps[b][:]), reads=[bk], writes=[hk])
                            S.act(lambda e, hr=hr, c=c: e.copy(out=halo[:, c, :], in_=hr[:, 512:514]), reads=[hk], writes=[K("halo", c)])
                            S.act(lambda e, t_=t_, b=b, cw=cw, cbc=cbc: e.activation(out=t_, in_=ps[b][:], func=AF.Identity, bias=cbc, scale=cw[2]),
                                  reads=[bk, "cvec"], writes=[tk])
                            S.dve(lambda e, t_=t_, hr=hr, cw=cw: e.scalar_tensor_tensor(out=t_, in0=hr[:, 1:513], scalar=cw[1], in1=t_, op0=ALU.mult, op1=ALU.add),
                                  reads=[hk, tk, "cvec"], writes=[tk])
                            S.dve(lambda e, t_=t_, hr=hr, cw=cw: e.scalar_tensor_tensor(out=t_, in0=hr[:, 0:512], scalar=cw[0], in1=t_, op0=ALU.mult, op1=ALU.add),
                                  reads=[hk, tk, "cvec"], writes=[tk])
                            if half == 0:
                                S.act(lambda e, t_=t_, nt=nt: e.activation(out=sg[nt], in_=t_, func=AF.Silu), reads=[tk], writes=[K("sg", nt)])
                            else:
                                S.dve(lambda e, t_=t_, nt=nt, j=j: e.tensor_tensor(out=actT[:, j, nt * 512:(nt + 1) * 512], in0=t_, in1=sg[nt], op=ALU.mult),
                                      reads=[tk, K("sg", nt)], writes=[K("act", j, nt)])
                outproj_ln(lambda kc, nt: (actT[:, kc, nt * 512:(nt + 1) * 512], K("act", kc, nt)),
                           NFC, Wdn, tok0, 2, xres, x32_out, xb_out, (l * 2 + 1) * 16, okey)
            drain_work()

        def gla_phase(l, xb_in, xres, x32_out, xb_out, okey):
            j = l // 2
            S.barrier()
            al = Alloc(arena, PHASE_BASE)
            xs = al.get([16, 512], BF16)
            og = xs
            qT = al.get([8, 512], BF16)
            kT = al.get([8, 512], BF16)
            vsb = al.get([4, 2048], BF16)
            rs = al.get([16, 512], BF16)
            glow = al.get([512], F32, parts=17)
            wg = al.get([1024], F32, parts=17)
            S32 = al.get([8, 512], F32)
            Sbf = al.get([8, 512], BF16)
            u = al.get([1024], F32)
            EB = al.get([8, 128], F32)
            ENB = al.get([8, 128], F32)
            qs = al.get([8, 128], BF16)
            ks = al.get([8, 128], BF16)
            kh = al.get([8, 128], BF16)
            khat = al.get([1024], BF16)
            PT = al.get([4, 128], BF16)
            sq = al.get([16, 128], BF16)
            rsq = al.get([4, 128], F32)
            t1 = al.get([16, 128], F32)
            Win = gla_w_in[j]
            Wout = gla_w_out[j]
            S.dma("sp", lambda e: e.dma_start(out=wg, in_=wg_d[j]), writes=["wg"])
            S.dve(lambda e: e.memset(glow, 1.0), writes=["glow"])
            ps5bf = ps[5][:].bitcast(BF16)
            XS_KEYS = [f"xs{g}" for g in range(4)]
            for tile in range(NT // 512):
                tok0 = tile * 512
                seq_start = (tok0 % SEQ == 0)
                load_xs(xs, xb_in, tok0, 512)
                if seq_start:
                    S.dve(lambda e: e.memset(S32, 0.0), writes=[K("S32", h) for h in range(8)])
                    S.dve(lambda e: e.memset(Sbf, 0.0), writes=[K("Sbf", h) for h in range(8)])

                def fm_chunk(cid, evac):
                    pop_work(1)
                    wv, wk = wload_ws(j, cid)
                    b, bk = mainbank()
                    for kc in range(16):
                        S.pe(lambda e, b=b, wv=wv, kc=kc: e.matmul(ps[b][:], wv[:, kc, :], xs[:, kc, :], start=(kc == 0), stop=(kc == 15)),
                             reads=[wk, f"xs{kc // 4}"], writes=[bk])
                    evac(b, bk)
                for c in range(8):
                    fm_chunk(c, lambda b, bk, c=c: S.act(lambda e: e.activation(out=qT[:, c, :], in_=ps[b][:], func=AF.Identity, scale=0.0625),
                                                                reads=[bk], writes=[K("qT", c)]))
                for c in range(8):
                    fm_chunk(8 + c, lambda b, bk, c=c: S.dve(lambda e: e.tensor_copy(out=kT[:, c, :], in_=ps[b][:]),
                                                                       reads=[bk], writes=[K("kT", c)]))
                for c in range(16):
                    fm_chunk(16 + c, lambda b, bk, c=c: S.act(lambda e: e.activation(out=rs[:, c, :], in_=ps[b][:], func=AF.Silu),
                                                                       reads=[bk], writes=[K("rs", c)]))
                wv, wk = wload(Win, 0, 16, 6144, 16)
                b, bk = mainbank()
                for kc in range(16):
                    S.pe(lambda e, b=b, wv=wv, kc=kc: e.matmul(ps[b][0:16, :], wv[:, kc, :], xs[:, kc, :], start=(kc == 0), stop=(kc == 15)),
                         reads=[wk, f"xs{kc // 4}"], writes=[bk])
                S.act(lambda e, b=b: e.copy(out=glow[0:16, :], in_=ps[b][0:16, :]), reads=[bk], writes=["glow"])
                for cg in range(16):
                    wv, wk = wload_ws(j, 32 + cg)
                    b, bk = mainbank()
                    for tc in range(4):
                        for kc in range(16):
                            S.pe(lambda e, b=b, wv=wv, kc=kc, tc=tc: e.matmul(ps[b][:, tc * 128:(tc + 1) * 128], xs[:, kc, tc * 128:(tc + 1) * 128], wv[:, kc, :],
                                                                              start=(kc == 0), stop=(kc == 15)),
                                 reads=[wk, f"xs{kc // 4}"], writes=[bk])
                    S.act(lambda e, b=b, cg=cg: e.copy(out=vsb[:, :, cg * 128:(cg + 1) * 128], in_=ps[b][:].rearrange("p (a b) -> p a b", b=128)),
                          reads=[bk], writes=[K("v", cg)])
                VK = [K("v", cg) for cg in range(16)]
                for tc in range(4):
                    tsl = slice(tc * 128, (tc + 1) * 128)
                    for hf in range(2):
                        S.pe(lambda e, hf=hf, tsl=tsl: e.matmul(ps[hf][:], glow[0:17, tsl], wg[0:17, hf * 512:(hf + 1) * 512], start=True, stop=True),
                             reads=["glow", "wg"], writes=[f"ps{hf}"])
                        S.act(lambda e, hf=hf: e.activation(out=u[:, hf * 512:(hf + 1) * 512], in_=ps[hf][:], func=AF.Exp, scale=-1.0),
                              reads=[f"ps{hf}"], writes=[K("u", hf)])
                        S.act(lambda e, hf=hf: e.activation(out=u[:, hf * 512:(hf + 1) * 512], in_=u[:, hf * 512:(hf + 1) * 512], func=AF.Ln, bias=1.0, scale=1.0),
                              reads=[K("u", hf)], writes=[K("u", hf)])
                    for dc in range(8):
                        pb = 2 + dc // 4
                        S.pe(lambda e, dc=dc, pb=pb: e.matmul(ps[pb][:, (dc % 4) * 128:(dc % 4 + 1) * 128], u[:, dc * 128:(dc + 1) * 128], tri, start=True, stop=True),
                             reads=[K("u", dc // 4), "cm32"], writes=[f"ps{pb}"])
                    for hf in range(2):
                        pv = ps[2 + hf][:].rearrange("p (a b) -> p a b", b=128)
                        S.act(lambda e, hf=hf, pv=pv: e.activation(out=EB[:, hf * 4:(hf + 1) * 4, :], in_=pv, func=AF.Exp), reads=[f"ps{2 + hf}"], writes=[K("EB", hf)])
                        S.act(lambda e, hf=hf, pv=pv: e.activation(out=ENB[:, hf * 4:(hf + 1) * 4, :], in_=pv, func=AF.Exp, scale=-1.0), reads=[f"ps{2 + hf}"], writes=[K("ENB", hf)])
                    EBK = [K("EB", 0), K("EB", 1)]
                    ENBK = [K("ENB", 0), K("ENB", 1)]
                    S.dve(lambda e, tsl=tsl: e.tensor_tensor(out=qs, in0=qT[:, :, tsl], in1=EB, op=ALU.mult), reads=[K("qT", c) for c in range(8)] + EBK, writes=["qs"])
                    S.dve(lambda e, tsl=tsl: e.tensor_tensor(out=ks, in0=kT[:, :, tsl], in1=ENB, op=ALU.mult), reads=[K("kT", c) for c in range(8)] + ENBK, writes=["ks"])
                    S.dve(lambda e: e.tensor_tensor(out=kh, in0=ks, in1=EB[:, :, 127:128].broadcast_to([128, 8, 128]), op=ALU.mult), reads=["ks"] + EBK, writes=["kh"])
                    for h in range(4):
                        for dl in range(2):
                            S.pe(lambda e, h=h, dl=dl: e.matmul(ps[4][:, h * 128:(h + 1) * 128], ks[:, 2 * h + dl, :], qs[:, 2 * h + dl, :], start=(dl == 0), stop=(dl == 1)),
                                 reads=["ks", "qs"], writes=["ps4"])
                    S.dve(lambda e: e.tensor_tensor(out=PT, in0=ps[4][:].rearrange("p (a b) -> p a b", b=128),
                                                    in1=mcur.unsqueeze(1).broadcast_to([128, 4, 128]), op=ALU.mult), reads=["ps4", "cm32"], writes=["PT"])
                    for dc in range(8):
                        S.pe(lambda e, dc=dc: e.transpose(ps5bf[:, dc * 128:(dc + 1) * 128], kh[:, dc, :], ident), reads=["kh", "cb16"], writes=["ps5"])
                    S.act(lambda e: e.copy(out=khat, in_=ps5bf), reads=["ps5"], writes=["khat"])
                    for h in range(4):
                        for ec in range(4):
                            o_ = ps[h][:, ec * 128:(ec + 1) * 128]
                            S.pe(lambda e, h=h, ec=ec, o_=o_, tc=tc: e.matmul(o_, vsb[:, tc, h * 512 + ec * 128:h * 512 + (ec + 1) * 128], PT[:, h, :], start=True, stop=False),
                                 reads=VK + ["PT"], writes=[f"ps{h}"])
                            for dl in range(2):
                                S.pe(lambda e, h=h, ec=ec, o_=o_, dl=dl: e.matmul(o_, Sbf[:, 2 * h + dl, ec * 128:(ec + 1) * 128], qs[:, 2 * h + dl, :], start=False, stop=(dl == 1)),
                                     reads=[K("Sbf", 2 * h + dl), "qs"], writes=[f"ps{h}"])
                    for h in range(4):
                        S.act(lambda e, h=h: e.activation(out=sq[:, h * 4:(h + 1) * 4, :], in_=ps[h][:].rearrange("p (a b) -> p a b", b=128), func=AF.Square),
                              reads=[f"ps{h}"], writes=[K("sq", h)])
                    for h in range(4):
                        for ec in range(4):
                            S.pe(lambda e, h=h, ec=ec: e.matmul(ps[6][:, h * 128:(h + 1) * 128], ones512, sq[:, h * 4 + ec, :], start=(ec == 0), stop=(ec == 3)),
                                 reads=[K("sq", h), "cb16"], writes=["ps6"])
                    S.act(lambda e: e.activation(out=rsq, in_=ps[6][:].rearrange("p (a b) -> p a b", b=128), func=AF.Sqrt, bias=1e-6, scale=1.0), reads=["ps6"], writes=["rsq"])
                    S.dve(lambda e: e.reciprocal(out=rsq, in_=rsq), reads=["rsq"], writes=["rsq"])
                    for h in range(4):
                        S.dve(lambda e, h=h: e.tensor_tensor(out=t1[:, h * 4:(h + 1) * 4, :], in0=ps[h][:].rearrange("p (a b) -> p a b", b=128),
                                                             in1=rsq[:, h:h + 1, :].broadcast_to([128, 4, 128]), op=ALU.mult),
                              reads=[f"ps{h}", "rsq"], writes=[K("t1", h)])
                    for ec in range(4):
                        ngc = cvec[:, C_NG + j * 4 + ec:C_NG + j * 4 + ec + 1]
                        S.dve(lambda e, ec=ec, ngc=ngc, tsl=tsl: e.scalar_tensor_tensor(out=og[:, ec::4, tsl], in0=t1[:, ec::4, :], scalar=ngc, in1=rs[:, ec::4, tsl],
                                                                                       op0=ALU.mult, op1=ALU.mult),
                              reads=[K("t1", h) for h in range(4)] + [K("rs", c) for c in range(16)] + ["cvec"], writes=XS_KEYS + [K("og", tc)])
                    for hd in range(8):
                        h = hd // 2
                        pb = (4, 5, 7)[hd % 3]
                        S.pe(lambda e, hd=hd, h=h, pb=pb, tc=tc: e.matmul(ps[pb][:], khat[:, hd * 128:(hd + 1) * 128], vsb[:, tc, h * 512:(h + 1) * 512], start=True, stop=True),
                             reads=["khat"] + VK, writes=[f"ps{pb}"])
                        S.dve(lambda e, hd=hd, pb=pb: e.scalar_tensor_tensor(out=S32[:, hd, :], in0=S32[:, hd, :], scalar=EB[:, hd, 127:128], in1=ps[pb][:],
                                                                             op0=ALU.mult, op1=ALU.add),
                              reads=[K("S32", hd), f"ps{pb}"] + EBK, writes=[K("S32", hd)])
                        S.act(lambda e, hd=hd: e.copy(out=Sbf[:, hd, :], in_=S32[:, hd, :]), reads=[K("S32", hd)], writes=[K("Sbf", hd)])
                outproj_ln(lambda kc, nt: (og[:, kc, :], f"xs{kc // 4}"), 16, Wout, tok0, 1, xres, x32_out, xb_out, (l * 2) * 16, okey, wfn=lambda oc: wload_ws(j, 48 + oc))
            drain_work()

        def dil_phase(l, xb_in, xres, x32_out, xb_out, okey):
            j = l // 2
            S.barrier()
            al = Alloc(arena, PHASE_BASE)
            xs = al.get([16, 2048], BF16)
            og = al.get([8, 2048], BF16)
            acc2 = al.get([2, 2048], F32)
            qTb = [al.get([2048], BF16) for _ in range(2)]
            kTb = [al.get([2048], BF16) for _ in range(2)]
            vT = al.get([2048], BF16)
            vtokb = [al.get([16, 128], BF16) for _ in range(2)]
            pexp = [al.get([2, 2, 128], BF16) for _ in range(2)]
            PTb = [al.get([2, 2, 128], BF16) for _ in range(2)]
            Win = dil_w_in[j]
            Wout = dil_w_out[j]
            ps3bf = ps[3][:].bitcast(BF16)
            cnt = {"hg": 0, "blk": 0}
            for s in range(NSEQ):
                tokS = s * SEQ
                load_xs(xs, xb_in, tokS, 2048)
                def make_iter(h, gi, dd, hg):
                    qT, kT, vtok = qTb[hg], kTb[hg], vtokb[hg]
                    qk_, kk_, vk_, vtk = K("q", hg), K("k", hg), "vT", K("vtok", hg)
                    nb = 16 // dd
                    wst = {}

                    def tsl(blk):
                        r, bb = blk // nb, blk % nb
                        return slice(r + dd * 128 * bb, r + dd * 128 * bb + dd * 127 + 1, dd)

                    def grp(t, nt):
                        if t == 0 and nt == 0:
                            for tt_ in range(3):
                                wst[tt_] = wload(Win, 0, 16, ((gi * 3 + tt_) * 8 + h) * 128, 128)
                        wv, wk = wst[t]
                        b, bk = mainbank(3)
                        for kc in range(16):
                            S.pe(lambda e, b=b, wv=wv, kc=kc, nt=nt: e.matmul(ps[b][:], wv[:, kc, :], xs[:, kc, nt * 512:(nt + 1) * 512], start=(kc == 0), stop=(kc == 15)),
                                 reads=[wk, f"xs{kc // 4}"], writes=[bk])
                        nsl = slice(nt * 512, (nt + 1) * 512)
                        if t == 0:
                            S.act(lambda e, b=b, nsl=nsl: e.activation(out=qT[:, nsl], in_=ps[b][:], func=AF.Identity, scale=float(128 ** -0.5)),
                                  reads=[bk], writes=[qk_])
                        elif t == 1:
                            S.dve(lambda e, b=b, nsl=nsl: e.tensor_copy(out=kT[:, nsl], in_=ps[b][:]), reads=[bk], writes=[kk_])
                        else:
                            S.act(lambda e, b=b, nsl=nsl: e.copy(out=vT[:, nsl], in_=ps[b][:]), reads=[bk], writes=[vk_])
                    groups = [(lambda t=t, nt=nt: grp(t, nt)) for t in range(3) for nt in range(4)]

                    def trans():
                        for half in range(2):
                            for bi in range(8):
                                vsl = tsl(half * 8 + bi)
                                S.pe(lambda e, bi=bi, vsl=vsl: e.transpose(ps3bf[:, bi * 128:(bi + 1) * 128], vT[:, vsl], ident), reads=[vk_, "cb16"], writes=["ps3"])
                            S.dve(lambda e, half=half: e.tensor_copy(out=vtok[:, half * 8:(half + 1) * 8, :], in_=ps3bf.rearrange("p (a b) -> p a b", b=128)),
                                  reads=["ps3"], writes=[vtk])

                    def att_scores(pr):
                        ci = cnt["blk"] % 2
                        cnt["blk"] += 1
                        sb = 4 + ci
                        psS = ps[sb][:].rearrange("p (k a b) -> p k a b", a=2, b=128)
                        infos = []
                        for kq in range(2):
                            blk = 2 * pr + kq
                            bb = blk % nb
                            cs = tsl(blk)
                            psl = tsl(blk - 1) if bb > 0 else cs
                            S.pe(lambda e, psS=psS, psl=psl, cs=cs, kq=kq: e.matmul(psS[:, kq, 0, :], kT[:, psl], qT[:, cs], start=True, stop=True),
                                 reads=[kk_, qk_], writes=[f"ps{sb}"])
                            S.pe(lambda e, psS=psS, cs=cs, kq=kq: e.matmul(psS[:, kq, 1, :], kT[:, cs], qT[:, cs], start=True, stop=True),
                                 reads=[kk_, qk_], writes=[f"ps{sb}"])
                            infos.append((blk, bb, cs))
                        return (ci, infos)

                    def att_mid(ctx):
                        ci, infos = ctx
                        sb, ob_ = 4 + ci, 6 + ci
                        psS = ps[sb][:].rearrange("p (k a b) -> p k a b", a=2, b=128)
                        pe_, PT = pexp[ci], PTb[ci]
                        S.act(lambda e, psS=psS, pe_=pe_: e.activation(out=pe_, in_=psS, func=AF.Exp), reads=[f"ps{sb}"], writes=[K("pexp", ci)])
                        for kq, (blk, bb, cs) in enumerate(infos):
                            msk = M3[:, 1:3, :] if bb > 0 else M3[:, 0:3:2, :]
                            S.dve(lambda e, pe_=pe_, PT=PT, kq=kq, msk=msk: e.tensor_tensor(out=PT[:, kq, :, :], in0=pe_[:, kq, :, :], in1=msk, op=ALU.mult),
                                  reads=[K("pexp", ci), "cm32"], writes=[K("PT", ci, kq)])

                    def att_rest(ctx):
                        ci, infos = ctx
                        sb, ob_ = 4 + ci, 6 + ci
                        psO = ps[ob_][:].rearrange("p (k a b) -> p k a b", a=2, b=128)
                        pe_, PT = pexp[ci], PTb[ci]
                        for kq, (blk, bb, cs) in enumerate(infos):
                            for jj in range(2):
                                bk_ = blk - 1 if (jj == 0 and bb > 0) else blk
                                S.pe(lambda e, psO=psO, PT=PT, jj=jj, bk_=bk_, kq=kq: e.matmul(psO[:, kq, 0, :], vtok[:, bk_, :], PT[:, kq, jj, :], start=(jj == 0), stop=(jj == 1)),
                                     reads=[vtk, K("PT", ci, kq)], writes=[f"ps{ob_}"])
                            for jj in range(2):
                                S.pe(lambda e, psO=psO, PT=PT, jj=jj, kq=kq: e.matmul(psO[:, kq, 1, :], ones1, PT[:, kq, jj, :], start=(jj == 0), stop=(jj == 1)),
                                     reads=["cb16", K("PT", ci, kq)], writes=[f"ps{ob_}"])
                        for kq, (blk, bb, cs) in enumerate(infos):
                            if gi == 0:
                                S.dve(lambda e, psO=psO, cs=cs, kq=kq: e.tensor_copy(out=acc2[:, :, cs], in_=psO[:, kq, :, :]), reads=[f"ps{ob_}"], writes=["acc2"])
                            else:
                                S.dve(lambda e, psO=psO, cs=cs, kq=kq: e.tensor_tensor(out=acc2[:, :, cs], in0=acc2[:, :, cs], in1=psO[:, kq, :, :], op=ALU.add),
                                      reads=[f"ps{ob_}", "acc2"], writes=["acc2"])
                    actx = deque()

                    def att_step(pr):
                        if pr < 8:
                            actx.append(att_scores(pr))
                            att_mid(actx[-1])
                        if pr >= 1:
                            att_rest(actx.popleft())
                    att = [(lambda pr=pr: att_step(pr)) for pr in range(9)]

                    def fin():
                        if gi == 2:
                            S.dve(lambda e: e.reciprocal(out=acc2[:, 1, :], in_=acc2[:, 1, :]), reads=["acc2"], writes=["acc2"])
                            S.dve(lambda e: e.tensor_tensor(out=og[:, h, :], in0=acc2[:, 0, :], in1=acc2[:, 1, :], op=ALU.mult), reads=["acc2"], writes=[K("og", h)])
                    return groups, trans, att, fin

                iters = [(h, gi, dd) for h in range(8) for gi, dd in enumerate((1, 4, 16))]
                prev = None
                for it in range(len(iters) + 1):
                    cur_it = None
                    if it < len(iters):
                        cur_it = make_iter(*iters[it], cnt["hg"] % 2)
                        cnt["hg"] += 1
                    pop_work(4)
                    glist = cur_it[0] if cur_it else []
                    alist = prev[2] if prev else []
                    for i in range(max(len(glist), len(alist))):
                        if i < len(glist):
                            glist[i]()
                        if i < len(alist):
                            alist[i]()
                    if prev:
                        prev[3]()
                    if cur_it:
                        cur_it[1]()
                    prev = cur_it
                for half in range(2):
                    outproj_ln(lambda kc, nt, half=half: (og[:, kc, half * 1024 + nt * 512:half * 1024 + (nt + 1) * 512], K("og", kc)),
                               8, Wout, tokS + half * 1024, 2, xres, x32_out, xb_out, (l * 2) * 16, okey)
            drain_work()

        cur = None
        for f in gla_conv_ops(0):
            f()
        for sidx in range(NSUB):
            l = sidx // 2
            if sidx == 2 and NSUB > 4:
                pending_work.extend(gla_conv_ops(1))
            if sidx == 3:
                while pending_work:
                    pending_work.popleft()()
            last = (sidx == NSUB - 1)
            xres = xT if sidx == 0 else X32
            x32_out = outT if last else X32
            okey = "out" if last else "x32"
            xb_in = xT if cur is None else XB[cur]
            nxt = 0 if cur is None else 1 - cur
            xb_out = XB[nxt]
            if sidx % 2 == 1:
                ffn_phase(l, xb_in, xres, x32_out, xb_out, okey)
            elif l % 2 == 0:
                gla_phase(l, xb_in, xres, x32_out, xb_out, okey)
            else:
                dil_phase(l, xb_in, xres, x32_out, xb_out, okey)
            cur = nxt
        S.barrier()
        S.op("sp", lambda e: e.nop())
        S.emit()
    return nc


def host_consts(inputs):
    f = np.float32
    cvec = np.zeros((128, NCV), f)
    cvec[:, C_LNG:C_LNG + 128] = np.asarray(inputs["ln_g"], f).reshape(8, 16, 128).transpose(2, 0, 1).reshape(128, 128)
    cvec[:, C_LNB:C_LNB + 128] = np.asarray(inputs["ln_b"], f).reshape(8, 16, 128).transpose(2, 0, 1).reshape(128, 128)
    cvec[:, C_CW:C_CW + 1032] = np.asarray(inputs["ffn_conv_w"], f).reshape(4, 3, 86, 128).transpose(3, 0, 1, 2).reshape(128, 1032)
    cvec[:, C_CB:C_CB + 344] = np.asarray(inputs["ffn_conv_b"], f).reshape(4, 86, 128).transpose(2, 0, 1).reshape(128, 344)
    cvec[:, C_NG:C_NG + 8] = np.asarray(inputs["gla_norm_g"], f).reshape(2, 4, 128).transpose(2, 0, 1).reshape(128, 8)
    cmat = np.zeros((128, 1024), f)
    ii = np.arange(128)
    le = (ii[:, None] <= ii[None, :]).astype(f)
    ge = (ii[:, None] >= ii[None, :]).astype(f)
    cmat[:, 0:128] = le * f(-1.0 / 16.0)
    cmat[:, 256:384] = ge
    cmat[:, 384:512] = le
    cmat[:, 512:640] = np.eye(128, dtype=f)
    cmat[:, 640:768] = f(1.0 / 2048.0)
    cmat[:, 768:896] = f(1.0 / 512.0)
    cmat[:, 896:1024] = f(1.0)
    wg = np.concatenate([np.asarray(inputs["gla_w_gate_up"], f), np.asarray(inputs["gla_gate_bias"], f)[:, None, :]], axis=1)
    return cvec, cmat, np.ascontiguousarray(wg)


_PROG = {}


def run(inputs, ncores=NCORES, nseq=NSEQ_CORE, nsub=8, trace=False):
    key = (nseq, nsub)
    if key not in _PROG:
        _PROG[key] = build(nseq, nsub)
    nc = _PROG[key]
    cvec, cmat, wg = host_consts(inputs)
    x = np.asarray(inputs["x"], np.float32)
    shared = {k: np.ascontiguousarray(np.asarray(inputs[k], np.float32)) for k in
              ("gla_w_in", "gla_w_out", "dil_w_in", "dil_w_out", "ffn_w_up", "ffn_w_down")}
    in_maps = []
    for c in range(ncores):
        xc = x[c * nseq:(c + 1) * nseq].reshape(nseq * SEQ, D)
        m = dict(shared)
        m.update(xT=np.ascontiguousarray(xc.T), cvec=cvec, cmat=cmat, wg_aug=wg)
        in_maps.append(m)
    res = run_bass_kernel_spmd(nc, in_maps, core_ids=list(range(ncores)), **({"trace": True} if trace else {}))
    out = np.empty((ncores * nseq, SEQ, D), np.float32)
    for c in range(ncores):
        oT = np.asarray(res.results[c]["outT"])
        out[c * nseq:(c + 1) * nseq] = oT.T.reshape(nseq, SEQ, D)
    return out, res


def kernel(**inputs):
    out, _ = run(inputs)
    return out
```

```python
import contextlib
from collections import deque
import numpy as np
import concourse.bass as bass
import concourse.mybir as mybir
from concourse.bass_utils import run_bass_kernel_spmd

F32 = mybir.dt.float32
BF16 = mybir.dt.bfloat16
AF = mybir.ActivationFunctionType
ALU = mybir.AluOpType

D = 2048
SEQ = 2048
DFF = 5504
NFC = 43
ALPHA = float(8 ** 0.25)
NCORES = 8
NSEQ_CORE = 2

ENGS = ["pe", "act", "dve", "pool", "sp"]
NDMASEM = 16
EPOCH = 30000
NWB = 6
ARENA_BYTES = 212800

NCV = 1640
C_LNG, C_LNB, C_CW, C_CB, C_NG = 0, 128, 256, 1288, 1632


def K(*a):
    return "_".join(map(str, a))


class Sched:
    def __init__(self, nc):
        self.nc = nc
        self.ops = []
        self.last_w = {}
        self.readers = {}
        self.last_comp = {}
        self.recent_dma = {e: deque(maxlen=NDMASEM) for e in ENGS}
        self.bar_deps = set()
        self.bar_pending = set()

    def barrier(self):
        deps = set()
        for e in ENGS:
            if e in self.last_comp:
                deps.add(self.last_comp[e])
            deps.update(self.recent_dma[e])
        self.bar_deps = deps
        self.bar_pending = set(ENGS)
        self.last_w = {}
        self.readers = {}

    def op(self, eng, fn, reads=(), writes=(), dma=False):
        i = len(self.ops)
        deps = set()
        for k in reads:
            w = self.last_w.get(k)
            if w is not None:
                deps.add(w)
        for k in writes:
            w = self.last_w.get(k)
            if w is not None:
                deps.add(w)
            rs = self.readers.get(k)
            if rs:
                deps.update(rs)
        for k in reads:
            self.readers.setdefault(k, []).append(i)
        for k in writes:
            self.last_w[k] = i
            self.readers[k] = []
        if eng in self.bar_pending:
            deps |= self.bar_deps
            self.bar_pending.discard(eng)
        deps.discard(i)
        self.ops.append((eng, fn, deps, dma))
        if dma:
            self.recent_dma[eng].append(i)
        else:
            self.last_comp[eng] = i
        return i

    def pe(self, fn, reads=(), writes=()):
        return self.op("pe", fn, reads, writes)

    def act(self, fn, reads=(), writes=()):
        return self.op("act", fn, reads, writes)

    def dve(self, fn, reads=(), writes=()):
        return self.op("dve", fn, reads, writes)

    def dma(self, q, fn, reads=(), writes=()):
        return self.op(q, fn, reads, writes, dma=True)

    def emit(self):
        nc = self.nc
        ops = self.ops
        n = len(ops)
        need_inc = [False] * n
        for i, (eng, fn, deps, dma) in enumerate(ops):
            for d in deps:
                deng, _, _, ddma = ops[d]
                if ddma:
                    continue
                if deng == "pe" and eng == "pe" and not dma:
                    continue
                need_inc[d] = True
        cnt = {e: 0 for e in ENGS}
        dcnt = {e: 0 for e in ENGS}
        sig = [None] * n
        for i, (eng, fn, deps, dma) in enumerate(ops):
            if dma:
                k = dcnt[eng]
                dcnt[eng] += 1
                sig[i] = (("d", eng, k % NDMASEM), 16 * (k // NDMASEM + 1))
            elif need_inc[i]:
                c = cnt[eng]
                cnt[eng] += 1
                sig[i] = (("c", eng, c // EPOCH), (c % EPOCH) + 1)
        semkeys = sorted({s[0] for s in sig if s is not None})
        with contextlib.ExitStack() as st:
            sems = {}
            for k in semkeys:
                sems[k] = st.enter_context(nc.semaphore("s_" + "_".join(map(str, k))))
            block = st.enter_context(nc.Block())

            def run_engine(ename, e):
                waited = {}
                for i, (eng, fn, deps, dma) in enumerate(ops):
                    if eng != ename:
                        continue
                    wl = {}
                    for d in deps:
                        deng, _, _, ddma = ops[d]
                        if (not ddma) and deng == "pe" and eng == "pe" and not dma:
                            continue
                        sk, v = sig[d]
                        if waited.get(sk, 0) >= v:
                            continue
                        if wl.get(sk, 0) < v:
                            wl[sk] = v
                    if dma:
                        sk, v = sig[i]
                        if v > 16 and waited.get(sk, 0) < v - 16 and wl.get(sk, 0) < v - 16:
                            wl[sk] = v - 16
                    for sk, v in wl.items():
                        e.wait_ge(sems[sk], v)
                        waited[sk] = v
                    ins = fn(e)
                    if sig[i] is not None:
                        ins.then_inc(sems[sig[i][0]], 16 if dma else 1)

            @block.tensor
            def _(e):
                run_engine("pe", e)

            @block.scalar
            def _(e):
                run_engine("act", e)

            @block.vector
            def _(e):
                run_engine("dve", e)

            @block.gpsimd
            def _(e):
                run_engine("pool", e)

            @block.sync
            def _(e):
                run_engine("sp", e)


class Alloc:
    def __init__(self, arena, base=0):
        self.a = arena
        self.off = base

    def get(self, shape, dt, parts=128):
        n = int(np.prod(shape))
        nb = n * (4 if dt == F32 else 2)
        start = self.off
        self.off += (nb + 63) // 64 * 64
        assert self.off <= ARENA_BYTES, ("arena overflow", self.off)
        ap = self.a[0:parts, start // 2:(start + nb) // 2]
        if dt == F32:
            ap = ap.bitcast(F32)
        if len(shape) == 2:
            ap = ap.rearrange("p (a b) -> p a b", b=shape[1])
        elif len(shape) == 3:
            ap = ap.rearrange("p (a b c) -> p a b c", b=shape[1], c=shape[2])
        return ap


def build(NSEQ=NSEQ_CORE, NSUB=8):
    NT = NSEQ * SEQ
    nc = bass.Bass("TRN2", target_bir_lowering=False)

    def din(name, shape):
        return nc.dram_tensor(name, list(shape), F32, kind="ExternalInput").ap()

    xT = din("xT", [D, NT])
    gla_w_in = din("gla_w_in", [2, D, 6160])
    gla_w_out = din("gla_w_out", [2, D, D])
    dil_w_in = din("dil_w_in", [2, D, 9216])
    dil_w_out = din("dil_w_out", [2, 1024, D])
    ffn_w_up = din("ffn_w_up", [4, D, 11008])
    ffn_w_down = din("ffn_w_down", [4, DFF, D])
    cvec_d = din("cvec", [128, NCV])
    cmat_d = din("cmat", [128, 1024])
    wg_d = din("wg_aug", [2, 17, 1024])
    outT = nc.dram_tensor("outT", [D, NT], F32, kind="ExternalOutput").ap()
    X32 = nc.dram_tensor("X32", [D, NT], F32).ap()
    XB = [nc.dram_tensor(f"XB{i}", [D, NT], BF16).ap() for i in range(2)]
    WS = nc.dram_tensor("WS", [2, 64, 128, 2048], BF16).ap()

    with contextlib.ExitStack() as st:
        arena = st.enter_context(nc.sbuf_tensor("arena", [128, ARENA_BYTES // 2], BF16))
        ps = [st.enter_context(nc.psum_tensor(f"ps{i}", [128, 512], F32)) for i in range(8)]
        S = Sched(nc)

        cm = Alloc(arena, 0)
        cvec = cm.get([NCV], F32)
        cm32 = cm.get([4, 128], F32)
        tri = cm32[:, 0, :]
        M3 = cm32[:, 1:4, :]
        mcur = cm32[:, 3, :]
        cb16 = cm.get([4, 128], BF16)
        ident, onesD, ones512, ones1 = cb16[:, 0, :], cb16[:, 1, :], cb16[:, 2, :], cb16[:, 3, :]
        wpool = [cm.get([2048], BF16) for _ in range(NWB)]
        XR = [cm.get([512], F32) for _ in range(3)]
        ybf = cm.get([512], BF16)
        ysq = cm.get([512], BF16)
        mean_t = cm.get([512], F32)
        tmp_t = cm.get([512], F32)
        rstd_t = [cm.get([512], F32) for _ in range(2)]
        nmr_t = [cm.get([512], F32) for _ in range(2)]
        YL = [cm.get([512], F32) for _ in range(4)]
        OBF = [cm.get([512], BF16) for _ in range(2)]
        PHASE_BASE = cm.off

        S.dma("sp", lambda e: e.dma_start(out=cvec, in_=cvec_d), writes=["cvec"])
        S.dma("sp", lambda e: e.dma_start(out=cm32, in_=cmat_d[:, 0:512].rearrange("p (a b) -> p a b", b=128)), writes=["cm32"])
        S.dma("pool", lambda e: e.dma_start(out=cb16, in_=cmat_d[:, 512:1024].rearrange("p (a b) -> p a b", b=128)), writes=["cb16"])

        ctr = {"wb": 0, "bank": 0, "xr": 0, "yl": 0, "obf": 0}

        def wload(Wd, k0, nk, col0, ncols):
            i = ctr["wb"] % NWB
            ctr["wb"] += 1
            view = wpool[i][:, 0:nk * ncols].rearrange("p (a b) -> p a b", b=ncols)
            src = Wd[k0 * 128:(k0 + nk) * 128, col0:col0 + ncols].rearrange("(a p) m -> p a m", p=128)
            key = f"wb{i}"
            S.dma("pool", lambda e: e.dma_start(out=view, in_=src), writes=[key])
            return view, key

        def ws_src(j, cid):
            if cid < 48:
                col0 = (cid * 128 if cid < 8 else 1024 + (cid - 8) * 128 if cid < 16 else 4096 + (cid - 16) * 128 if cid < 32 else 2048 + (cid - 32) * 128)
                return gla_w_in[j], col0
            return gla_w_out[j], (cid - 48) * 128

        def gla_conv_ops(j):
            lst = []
            for cid in range(64):
                def f(cid=cid):
                    Wd, col0 = ws_src(j, cid)
                    view, key = wload(Wd, 0, 16, col0, 128)
                    dst = WS[j, cid].rearrange("p (a b) -> p a b", b=128)
                    S.dma("sp", lambda e: e.dma_start(out=dst, in_=view), reads=[key], writes=[K("ws", j, cid)])
                lst.append(f)
            return lst

        def wload_ws(j, cid):
            i = ctr["wb"] % NWB
            ctr["wb"] += 1
            view = wpool[i].rearrange("p (a b) -> p a b", b=128)
            src = WS[j, cid].rearrange("p (a b) -> p a b", b=128)
            key = f"wb{i}"
            S.dma("pool", lambda e: e.dma_start(out=view, in_=src), reads=[K("ws", j, cid)], writes=[key])
            return view, key

        def pk(b):
            return [f"ps{b}", f"ps{b}_0", f"ps{b}_1"]

        pending_work = deque()

        def pop_work(n=1):
            for _ in range(n):
                if pending_work:
                    pending_work.popleft()()

        def drain_work():
            while pending_work:
                pending_work.popleft()()

        def mainbank(nb=4):
            b = ctr["bank"] % nb
            ctr["bank"] += 1
            return b, f"ps{b}"

        def load_xs(xs, src, tok0, T):
            q = "pool" if src.dtype == F32 else "sp"
            for g in range(4):
                v = src[g * 512:(g + 1) * 512, tok0:tok0 + T].rearrange("(a p) t -> p a t", p=128)
                rk = [K("xb", c, t) for c in range(4 * g, 4 * g + 4) for t in range(tok0, tok0 + T, 512)]
                S.dma(q, lambda e, g=g, v=v: e.dma_start(out=xs[:, 4 * g:4 * g + 4, :], in_=v), reads=rk, writes=[f"xs{g}"])

        def outproj_ln(src_fn, KC, Wd, tok0, NTL, xres, x32_out, xb_out, lncol, okey, wfn=None, defer=True):
            pend = [None]
            groups = [(oc, nt) for oc in range(16) for nt in range(NTL)]
            wts = {}
            xrs = {}

            def load_w(oc):
                if wfn is not None:
                    wv, wk = wfn(oc)
                    wts[oc] = [(0, KC, wv, wk)]
                    return
                pieces = []
                for p0 in range(0, KC, 16):
                    n = min(16, KC - p0)
                    pieces.append((p0, n) + wload(Wd, p0, n, oc * 128, 128))
                wts[oc] = pieces

            def load_xr(g):
                oc, nt = groups[g]
                t0 = tok0 + nt * 512
                xi = ctr["xr"] % 3
                ctr["xr"] += 1
                xr, xk = XR[xi], f"xr{xi}"
                xsrc = xres[oc * 128:(oc + 1) * 128, t0:t0 + 512]
                S.dma("sp", lambda e, xr=xr, xsrc=xsrc: e.dma_start(out=xr, in_=xsrc), reads=[K("x32", oc, t0)], writes=[xk])
                xrs[g] = (xr, xk)

            load_w(0)
            for g in range(min(2, len(groups))):
                load_xr(g)
            for gi, (oc, nt) in enumerate(groups):
                if True:
                    if nt == 0 and oc + 1 < 16:
                        load_w(oc + 1)
                    if gi + 2 < len(groups):
                        load_xr(gi + 2)
                    pieces = wts[oc]
                    t0 = tok0 + nt * 512
                    b, bk = mainbank()
                    xr, xk = xrs.pop(gi)
                    for (p0, n, wv, wk) in pieces:
                        for kk in range(n):
                            kc = p0 + kk
                            rhs, rk = src_fn(kc, nt)
                            S.pe(lambda e, b=b, wv=wv, kk=kk, rhs=rhs, kc=kc: e.matmul(ps[b][:], wv[:, kk, :], rhs, start=(kc == 0), stop=(kc == KC - 1)),
                                 reads=[wk, rk], writes=[bk])
                    if pend[0]:
                        pend[0]()
                        pend[0] = None
                    S.dve(lambda e, xr=xr, b=b: e.scalar_tensor_tensor(out=xr, in0=xr, scalar=ALPHA, in1=ps[b][:], op0=ALU.mult, op1=ALU.add),
                          reads=[xk, bk], writes=[xk])
                    S.act(lambda e, xr=xr: e.activation(out=ybf, in_=xr, func=AF.Copy), reads=[xk], writes=["ybf"])
                    S.act(lambda e, xr=xr: e.activation(out=ysq, in_=xr, func=AF.Square), reads=[xk], writes=["ysq"])
                    ydst = X32[oc * 128:(oc + 1) * 128, t0:t0 + 512]
                    S.dma("act", lambda e, xr=xr, ydst=ydst: e.dma_start(out=ydst, in_=xr), reads=[xk], writes=[K("x32", oc, t0)])

                    def mk(oc=oc, nt=nt):
                        pm, pq = 4 + 2 * nt, 5 + 2 * nt
                        S.pe(lambda e: e.matmul(ps[pm][:], onesD, ybf, start=(oc == 0), stop=(oc == 15)), reads=["ybf", "cb16"], writes=pk(pm))
                        S.pe(lambda e: e.matmul(ps[pq][:], onesD, ysq, start=(oc == 0), stop=(oc == 15)), reads=["ysq", "cb16"], writes=pk(pq))
                    pend[0] = mk
            pend[0]()
            units = []
            for nt in range(NTL):
                t0 = tok0 + nt * 512
                pm, pq = 4 + 2 * nt, 5 + 2 * nt
                rstd, nmr = rstd_t[nt], nmr_t[nt]
                S.act(lambda e, pm=pm: e.activation(out=mean_t, in_=ps[pm][:], func=AF.Copy), reads=pk(pm), writes=["mean"])
                S.dve(lambda e: e.tensor_tensor(out=tmp_t, in0=mean_t, in1=mean_t, op=ALU.mult), reads=["mean"], writes=["tmp"])
                S.dve(lambda e, pq=pq: e.tensor_tensor(out=tmp_t, in0=ps[pq][:], in1=tmp_t, op=ALU.subtract), reads=pk(pq) + ["tmp"], writes=["tmp"])
                S.act(lambda e: e.activation(out=tmp_t, in_=tmp_t, func=AF.Sqrt, bias=1e-5, scale=1.0), reads=["tmp"], writes=["tmp"])
                S.dve(lambda e, rstd=rstd: e.reciprocal(out=rstd, in_=tmp_t), reads=["tmp"], writes=[f"rstd{nt}"])
                S.dve(lambda e, rstd=rstd, nmr=nmr: e.scalar_tensor_tensor(out=nmr, in0=mean_t, scalar=-1.0, in1=rstd, op0=ALU.mult, op1=ALU.mult),
                      reads=["mean", f"rstd{nt}"], writes=[f"nmr{nt}"])
                for oc in range(16):
                    def ld(oc=oc, nt=nt, t0=t0):
                        yi = ctr["yl"] % 4
                        ctr["yl"] += 1
                        yl, yk = YL[yi], f"yl{yi}"
                        ysrc = X32[oc * 128:(oc + 1) * 128, t0:t0 + 512]
                        S.dma("sp", lambda e, yl=yl, ysrc=ysrc: e.dma_start(out=yl, in_=ysrc), reads=[K("x32", oc, t0)], writes=[yk])
                        return yl, yk

                    def cpd(yl, yk, oc=oc, nt=nt, t0=t0, rstd=rstd, nmr=nmr):
                        S.dve(lambda e, yl=yl, rstd=rstd: e.tensor_tensor(out=yl, in0=yl, in1=rstd, op=ALU.mult), reads=[yk, f"rstd{nt}"], writes=[yk])
                        S.dve(lambda e, yl=yl, nmr=nmr: e.tensor_tensor(out=yl, in0=yl, in1=nmr, op=ALU.add), reads=[yk, f"nmr{nt}"], writes=[yk])

                    def cp(yl, yk, oc=oc, nt=nt, t0=t0, rstd=rstd, nmr=nmr):
                        oi = ctr["obf"] % 2
                        ctr["obf"] += 1
                        ob, obk = OBF[oi], f"obf{oi}"
                        gcol = cvec[:, C_LNG + lncol + oc:C_LNG + lncol + oc + 1]
                        bcol = cvec[:, C_LNB + lncol + oc:C_LNB + lncol + oc + 1]
                        S.act(lambda e, yl=yl, ob=ob, gcol=gcol, bcol=bcol: e.activation(out=ob, in_=yl, func=AF.Identity, bias=bcol, scale=gcol),
                              reads=[yk, "cvec"], writes=[obk])
                        S.act(lambda e, yl=yl, gcol=gcol, bcol=bcol: e.activation(out=yl, in_=yl, func=AF.Identity, bias=bcol, scale=gcol),
                              reads=[yk, "cvec"], writes=[yk])
                        odst = x32_out[oc * 128:(oc + 1) * 128, t0:t0 + 512]
                        S.dma("act", lambda e, yl=yl, odst=odst: e.dma_start(out=odst, in_=yl), reads=[yk], writes=[K(okey, oc, t0)])
                        bdst = xb_out[oc * 128:(oc + 1) * 128, t0:t0 + 512]
                        S.dma("act", lambda e, ob=ob, bdst=bdst: e.dma_start(out=bdst, in_=ob), reads=[obk], writes=[K("xb", oc, t0)])
                    units.append((ld, cp, cpd))
            nun = len(units)
            st_ = {}
            for i in range(nun + 3):
                def item(i=i):
                    if 0 <= i - 3 < nun:
                        units[i - 3][1](*st_.pop(i - 3))
                    if 0 <= i - 2 < nun:
                        units[i - 2][2](*st_[i - 2])
                    if i < nun:
                        st_[i] = units[i][0]()
                if defer:
                    pending_work.append(item)
                else:
                    item()

        def ffn_phase(l, xb_in, xres, x32_out, xb_out, okey):
            S.barrier()
            al = Alloc(arena, PHASE_BASE)
            xs = al.get([16, 1024], BF16)
            actT = al.get([NFC, 1024], BF16)
            hraw = [[al.get([514], F32) for _ in range(2)] for _ in range(2)]
            tt = [[al.get([512], F32) for _ in range(2)] for _ in range(2)]
            sg = [al.get([512], F32) for _ in range(2)]
            halo = al.get([86, 2], F32)
            Wup = ffn_w_up[l]
            Wdn = ffn_w_down[l]
            load_xs(xs, xb_in, 0, 1024)
            for tile in range(NT // 1024):
                tok0 = tile * 1024
                seq_start = (tok0 % SEQ == 0)
                for j in range(NFC):
                    for half in range(2):
                        c = j + NFC * half
                        if half == 0:
                            pop_work(1)
                        wv, wk = wload(Wup, 0, 16, c * 128, 128)
                        cw = [cvec[:, C_CW + (l * 3 + tap) * 86 + c:C_CW + (l * 3 + tap) * 86 + c + 1] for tap in range(3)]
                        cbc = cvec[:, C_CB + l * 86 + c:C_CB + l * 86 + c + 1]
                        for nt in range(2):
                            b, bk = mainbank()
                            for kc in range(16):
                                S.pe(lambda e, b=b, wv=wv, kc=kc, nt=nt: e.matmul(ps[b][:], wv[:, kc, :], xs[:, kc, nt * 512:(nt + 1) * 512],
                                                                                  start=(kc == 0), stop=(kc == 15)),
                                     reads=[wk, f"xs{kc // 4}"], writes=[bk])
                            hr, hk = hraw[half][nt], K("hr", half, nt)
                            t_, tk = tt[half][nt], K("tt", half, nt)
                            if nt == 0 and seq_start:
                                S.act(lambda e, hr=hr: e.memzero(hr[:, 0:2]), writes=[hk])
                            else:
                                S.act(lambda e, hr=hr, c=c: e.copy(out=hr[:, 0:2], in_=halo[:, c, :]), reads=[K("halo", c)], writes=[hk])
                            S.act(lambda e, hr=hr, b=b: e.copy(out=hr[:, 2:514], in_=ps[b][:]), reads=[bk], writes=[hk])
                            S.act(lambda e, hr=hr, c=c: e.copy(out=halo[:, c, :], in_=hr[:, 512:514]), reads=[hk], writes=[K("halo", c)])
                            S.act(lambda e, t_=t_, b=b, cw=cw, cbc=cbc: e.activation(out=t_, in_=ps[b][:], func=AF.Identity, bias=cbc, scale=cw[2]),
                                  reads=[bk, "cvec"], writes=[tk])
                            S.dve(lambda e, t_=t_, hr=hr, cw=cw: e.scalar_tensor_tensor(out=t_, in0=hr[:, 1:513], scalar=cw[1], in1=t_, op0=ALU.mult, op1=ALU.add),
                                  reads=[hk, tk, "cvec"], writes=[tk])
                            S.dve(lambda e, t_=t_, hr=hr, cw=cw: e.scalar_tensor_tensor(out=t_, in0=hr[:, 0:512], scalar=cw[0], in1=t_, op0=ALU.mult, op1=ALU.add),
                                  reads=[hk, tk, "cvec"], writes=[tk])
                            if half == 0:
                                S.act(lambda e, t_=t_, nt=nt: e.activation(out=sg[nt], in_=t_, func=AF.Silu), reads=[tk], writes=[K("sg", nt)])
                            else:
                                S.dve(lambda e, t_=t_, nt=nt, j=j: e.tensor_tensor(out=actT[:, j, nt * 512:(nt + 1) * 512], in0=t_, in1=sg[nt], op=ALU.mult),
                                      reads=[tk, K("sg", nt)], writes=[K("act", j, nt)])
                if tok0 + 1024 < NT:
                    load_xs(xs, xb_in, tok0 + 1024, 1024)
                outproj_ln(lambda kc, nt: (actT[:, kc, nt * 512:(nt + 1) * 512], K("act", kc, nt)),
                           NFC, Wdn, tok0, 2, xres, x32_out, xb_out, (l * 2 + 1) * 16, okey)
            drain_work()

        def gla_phase(l, xb_in, xres, x32_out, xb_out, okey):
            j = l // 2
            S.barrier()
            al = Alloc(arena, PHASE_BASE)
            xs = al.get([16, 512], BF16)
            og = xs
            qT = al.get([8, 512], BF16)
            kT = al.get([8, 512], BF16)
            vsb = al.get([4, 2048], BF16)
            rs = al.get([16, 512], BF16)
            glow = al.get([512], F32, parts=17)
            wg = al.get([1024], F32, parts=17)
            S32 = al.get([8, 512], F32)
            Sbf = al.get([8, 512], BF16)
            u = al.get([1024], F32)
            EB = al.get([8, 128], F32)
            ENB = al.get([8, 128], F32)
            qs = al.get([8, 128], BF16)
            ks = al.get([8, 128], BF16)
            kh = al.get([8, 128], BF16)
            khat = al.get([1024], BF16)
            PT = al.get([4, 128], BF16)
            sq = al.get([16, 128], BF16)
            rsq = al.get([4, 128], F32)
            t1 = al.get([16, 128], F32)
            Win = gla_w_in[j]
            Wout = gla_w_out[j]
            S.dma("sp", lambda e: e.dma_start(out=wg, in_=wg_d[j]), writes=["wg"])
            S.dve(lambda e: e.memset(glow, 1.0), writes=["glow"])
            ps5bf = ps[5][:].bitcast(BF16)
            XS_KEYS = [f"xs{g}" for g in range(4)]
            for tile in range(NT // 512):
                tok0 = tile * 512
                seq_start = (tok0 % SEQ == 0)
                load_xs(xs, xb_in, tok0, 512)
                if seq_start:
                    S.dve(lambda e: e.memset(S32, 0.0), writes=[K("S32", h) for h in range(8)])
                    S.dve(lambda e: e.memset(Sbf, 0.0), writes=[K("Sbf", h) for h in range(8)])

                def fm_chunk(cid, evac):
                    pop_work(1)
                    wv, wk = wload_ws(j, cid)
                    b, bk = mainbank()
                    for kc in range(16):
                        S.pe(lambda e, b=b, wv=wv, kc=kc: e.matmul(ps[b][:], wv[:, kc, :], xs[:, kc, :], start=(kc == 0), stop=(kc == 15)),
                             reads=[wk, f"xs{kc // 4}"], writes=[bk])
                    evac(b, bk)
                for c in range(8):
                    fm_chunk(c, lambda b, bk, c=c: S.act(lambda e: e.activation(out=qT[:, c, :], in_=ps[b][:], func=AF.Identity, scale=0.0625),
                                                                reads=[bk], writes=[K("qT", c)]))
                for c in range(8):
                    fm_chunk(8 + c, lambda b, bk, c=c: S.dve(lambda e: e.tensor_copy(out=kT[:, c, :], in_=ps[b][:]),
                                                                       reads=[bk], writes=[K("kT", c)]))
                for c in range(16):
                    fm_chunk(16 + c, lambda b, bk, c=c: S.act(lambda e: e.activation(out=rs[:, c, :], in_=ps[b][:], func=AF.Silu),
                                                                       reads=[bk], writes=[K("rs", c)]))
                wv, wk = wload(Win, 0, 16, 6144, 16)
                b, bk = mainbank()
                for kc in range(16):
                    S.pe(lambda e, b=b, wv=wv, kc=kc: e.matmul(ps[b][0:16, :], wv[:, kc, :], xs[:, kc, :], start=(kc == 0), stop=(kc == 15)),
                         reads=[wk, f"xs{kc // 4}"], writes=[bk])
                S.act(lambda e, b=b: e.copy(out=glow[0:16, :], in_=ps[b][0:16, :]), reads=[bk], writes=["glow"])
                for cg in range(16):
                    wv, wk = wload_ws(j, 32 + cg)
                    b, bk = mainbank()
                    for tc in range(4):
                        for kc in range(16):
                            S.pe(lambda e, b=b, wv=wv, kc=kc, tc=tc: e.matmul(ps[b][:, tc * 128:(tc + 1) * 128], xs[:, kc, tc * 128:(tc + 1) * 128], wv[:, kc, :],
                                                                              start=(kc == 0), stop=(kc == 15)),
                                 reads=[wk, f"xs{kc // 4}"], writes=[bk])
                    S.act(lambda e, b=b, cg=cg: e.copy(out=vsb[:, :, cg * 128:(cg + 1) * 128], in_=ps[b][:].rearrange("p (a b) -> p a b", b=128)),
                          reads=[bk], writes=[K("v", cg)])
                VK = [K("v", cg) for cg in range(16)]
                for tc in range(4):
                    tsl = slice(tc * 128, (tc + 1) * 128)
                    for hf in range(2):
                        S.pe(lambda e, hf=hf, tsl=tsl: e.matmul(ps[hf][:], glow[0:17, tsl], wg[0:17, hf * 512:(hf + 1) * 512], start=True, stop=True),
                             reads=["glow", "wg"], writes=[f"ps{hf}"])
                        S.act(lambda e, hf=hf: e.activation(out=u[:, hf * 512:(hf + 1) * 512], in_=ps[hf][:], func=AF.Exp, scale=-1.0),
                              reads=[f"ps{hf}"], writes=[K("u", hf)])
                        S.act(lambda e, hf=hf: e.activation(out=u[:, hf * 512:(hf + 1) * 512], in_=u[:, hf * 512:(hf + 1) * 512], func=AF.Ln, bias=1.0, scale=1.0),
                              reads=[K("u", hf)], writes=[K("u", hf)])
                    for dc in range(8):
                        pb = 2 + dc // 4
                        S.pe(lambda e, dc=dc, pb=pb: e.matmul(ps[pb][:, (dc % 4) * 128:(dc % 4 + 1) * 128], u[:, dc * 128:(dc + 1) * 128], tri, start=True, stop=True),
                             reads=[K("u", dc // 4), "cm32"], writes=[f"ps{pb}"])
                    for hf in range(2):
                        pv = ps[2 + hf][:].rearrange("p (a b) -> p a b", b=128)
                        S.act(lambda e, hf=hf, pv=pv: e.activation(out=EB[:, hf * 4:(hf + 1) * 4, :], in_=pv, func=AF.Exp), reads=[f"ps{2 + hf}"], writes=[K("EB", hf)])
                        S.act(lambda e, hf=hf, pv=pv: e.activation(out=ENB[:, hf * 4:(hf + 1) * 4, :], in_=pv, func=AF.Exp, scale=-1.0), reads=[f"ps{2 + hf}"], writes=[K("ENB", hf)])
                    EBK = [K("EB", 0), K("EB", 1)]
                    ENBK = [K("ENB", 0), K("ENB", 1)]
                    S.dve(lambda e, tsl=tsl: e.tensor_tensor(out=qs, in0=qT[:, :, tsl], in1=EB, op=ALU.mult), reads=[K("qT", c) for c in range(8)] + EBK, writes=["qs"])
                    S.dve(lambda e, tsl=tsl: e.tensor_tensor(out=ks, in0=kT[:, :, tsl], in1=ENB, op=ALU.mult), reads=[K("kT", c) for c in range(8)] + ENBK, writes=["ks"])
                    S.dve(lambda e: e.tensor_tensor(out=kh, in0=ks, in1=EB[:, :, 127:128].broadcast_to([128, 8, 128]), op=ALU.mult), reads=["ks"] + EBK, writes=["kh"])
                    for h in range(4):
                        for dl in range(2):
                            S.pe(lambda e, h=h, dl=dl: e.matmul(ps[4][:, h * 128:(h + 1) * 128], ks[:, 2 * h + dl, :], qs[:, 2 * h + dl, :], start=(dl == 0), stop=(dl == 1)),
                                 reads=["ks", "qs"], writes=["ps4"])
                    S.dve(lambda e: e.tensor_tensor(out=PT, in0=ps[4][:].rearrange("p (a b) -> p a b", b=128),
                                                    in1=mcur.unsqueeze(1).broadcast_to([128, 4, 128]), op=ALU.mult), reads=["ps4", "cm32"], writes=["PT"])
                    for dc in range(8):
                        S.pe(lambda e, dc=dc: e.transpose(ps5bf[:, dc * 128:(dc + 1) * 128], kh[:, dc, :], ident), reads=["kh", "cb16"], writes=["ps5"])
                    S.act(lambda e: e.copy(out=khat, in_=ps5bf), reads=["ps5"], writes=["khat"])
                    for h in range(4):
                        for ec in range(4):
                            o_ = ps[h][:, ec * 128:(ec + 1) * 128]
                            S.pe(lambda e, h=h, ec=ec, o_=o_, tc=tc: e.matmul(o_, vsb[:, tc, h * 512 + ec * 128:h * 512 + (ec + 1) * 128], PT[:, h, :], start=True, stop=False),
                                 reads=VK + ["PT"], writes=[f"ps{h}"])
                            for dl in range(2):
                                S.pe(lambda e, h=h, ec=ec, o_=o_, dl=dl: e.matmul(o_, Sbf[:, 2 * h + dl, ec * 128:(ec + 1) * 128], qs[:, 2 * h + dl, :], start=False, stop=(dl == 1)),
                                     reads=[K("Sbf", 2 * h + dl), "qs"], writes=[f"ps{h}"])
                    for h in range(4):
                        S.act(lambda e, h=h: e.activation(out=sq[:, h * 4:(h + 1) * 4, :], in_=ps[h][:].rearrange("p (a b) -> p a b", b=128), func=AF.Square),
                              reads=[f"ps{h}"], writes=[K("sq", h)])
                    for h in range(4):
                        for ec in range(4):
                            S.pe(lambda e, h=h, ec=ec: e.matmul(ps[6][:, h * 128:(h + 1) * 128], ones512, sq[:, h * 4 + ec, :], start=(ec == 0), stop=(ec == 3)),
                                 reads=[K("sq", h), "cb16"], writes=["ps6"])
                    S.act(lambda e: e.activation(out=rsq, in_=ps[6][:].rearrange("p (a b) -> p a b", b=128), func=AF.Sqrt, bias=1e-6, scale=1.0), reads=["ps6"], writes=["rsq"])
                    S.dve(lambda e: e.reciprocal(out=rsq, in_=rsq), reads=["rsq"], writes=["rsq"])
                    for h in range(4):
                        S.dve(lambda e, h=h: e.tensor_tensor(out=t1[:, h * 4:(h + 1) * 4, :], in0=ps[h][:].rearrange("p (a b) -> p a b", b=128),
                                                             in1=rsq[:, h:h + 1, :].broadcast_to([128, 4, 128]), op=ALU.mult),
                              reads=[f"ps{h}", "rsq"], writes=[K("t1", h)])
                    for ec in range(4):
                        ngc = cvec[:, C_NG + j * 4 + ec:C_NG + j * 4 + ec + 1]
                        S.dve(lambda e, ec=ec, ngc=ngc, tsl=tsl: e.scalar_tensor_tensor(out=og[:, ec::4, tsl], in0=t1[:, ec::4, :], scalar=ngc, in1=rs[:, ec::4, tsl],
                                                                                       op0=ALU.mult, op1=ALU.mult),
                              reads=[K("t1", h) for h in range(4)] + [K("rs", c) for c in range(16)] + ["cvec"], writes=XS_KEYS + [K("og", tc)])
                    for hd in range(8):
                        h = hd // 2
                        pb = (4, 5, 7)[hd % 3]
                        S.pe(lambda e, hd=hd, h=h, pb=pb, tc=tc: e.matmul(ps[pb][:], khat[:, hd * 128:(hd + 1) * 128], vsb[:, tc, h * 512:(h + 1) * 512], start=True, stop=True),
                             reads=["khat"] + VK, writes=[f"ps{pb}"])
                        S.dve(lambda e, hd=hd, pb=pb: e.scalar_tensor_tensor(out=S32[:, hd, :], in0=S32[:, hd, :], scalar=EB[:, hd, 127:128], in1=ps[pb][:],
                                                                             op0=ALU.mult, op1=ALU.add),
                              reads=[K("S32", hd), f"ps{pb}"] + EBK, writes=[K("S32", hd)])
                        S.act(lambda e, hd=hd: e.copy(out=Sbf[:, hd, :], in_=S32[:, hd, :]), reads=[K("S32", hd)], writes=[K("Sbf", hd)])
                outproj_ln(lambda kc, nt: (og[:, kc, :], f"xs{kc // 4}"), 16, Wout, tok0, 1, xres, x32_out, xb_out, (l * 2) * 16, okey, wfn=lambda oc: wload_ws(j, 48 + oc))
            drain_work()

        def dil_phase(l, xb_in, xres, x32_out, xb_out, okey):
            j = l // 2
            S.barrier()
            al = Alloc(arena, PHASE_BASE)
            xs = al.get([16, 2048], BF16)
            og = al.get([8, 2048], BF16)
            acc2 = al.get([2, 2048], F32)
            qTb = [al.get([2048], BF16) for _ in range(2)]
            kTb = [al.get([2048], BF16) for _ in range(2)]
            vT = al.get([2048], BF16)
            vtokb = [al.get([16, 128], BF16) for _ in range(2)]
            pexp = [al.get([2, 2, 128], BF16) for _ in range(2)]
            PTb = [al.get([2, 2, 128], BF16) for _ in range(2)]
            Win = dil_w_in[j]
            Wout = dil_w_out[j]
            ps3bf = ps[3][:].bitcast(BF16)
            cnt = {"hg": 0, "blk": 0}
            load_xs(xs, xb_in, 0, 2048)
            for s in range(NSEQ):
                tokS = s * SEQ
                def make_iter(h, gi, dd, hg):
                    qT, kT, vtok = qTb[hg], kTb[hg], vtokb[hg]
                    qk_, kk_, vk_, vtk = K("q", hg), K("k", hg), "vT", K("vtok", hg)
                    nb = 16 // dd
                    wst = {}

                    def tsl(blk):
                        r, bb = blk // nb, blk % nb
                        return slice(r + dd * 128 * bb, r + dd * 128 * bb + dd * 127 + 1, dd)

                    def grp(t, nt):
                        if t == 0 and nt == 0:
                            for tt_ in range(3):
                                wst[tt_] = wload(Win, 0, 16, ((gi * 3 + tt_) * 8 + h) * 128, 128)
                        wv, wk = wst[t]
                        b, bk = mainbank(3)
                        for kc in range(16):
                            S.pe(lambda e, b=b, wv=wv, kc=kc, nt=nt: e.matmul(ps[b][:], wv[:, kc, :], xs[:, kc, nt * 512:(nt + 1) * 512], start=(kc == 0), stop=(kc == 15)),
                                 reads=[wk, f"xs{kc // 4}"], writes=[bk])
                        nsl = slice(nt * 512, (nt + 1) * 512)
                        if t == 0:
                            S.act(lambda e, b=b, nsl=nsl: e.activation(out=qT[:, nsl], in_=ps[b][:], func=AF.Identity, scale=float(128 ** -0.5)),
                                  reads=[bk], writes=[qk_])
                        elif t == 1:
                            S.dve(lambda e, b=b, nsl=nsl: e.tensor_copy(out=kT[:, nsl], in_=ps[b][:]), reads=[bk], writes=[kk_])
                        else:
                            S.act(lambda e, b=b, nsl=nsl: e.copy(out=vT[:, nsl], in_=ps[b][:]), reads=[bk], writes=[vk_])
                    groups = [(lambda t=t, nt=nt: grp(t, nt)) for t in range(3) for nt in range(4)]

                    def trans():
                        for half in range(2):
                            for bi in range(8):
                                vsl = tsl(half * 8 + bi)
                                S.pe(lambda e, bi=bi, vsl=vsl: e.transpose(ps3bf[:, bi * 128:(bi + 1) * 128], vT[:, vsl], ident), reads=[vk_, "cb16"], writes=["ps3"])
                            S.dve(lambda e, half=half: e.tensor_copy(out=vtok[:, half * 8:(half + 1) * 8, :], in_=ps3bf.rearrange("p (a b) -> p a b", b=128)),
                                  reads=["ps3"], writes=[vtk])

                    def att_scores(pr):
                        ci = cnt["blk"] % 2
                        cnt["blk"] += 1
                        sb = 4 + ci
                        psS = ps[sb][:].rearrange("p (k a b) -> p k a b", a=2, b=128)
                        infos = []
                        for kq in range(2):
                            blk = 2 * pr + kq
                            bb = blk % nb
                            cs = tsl(blk)
                            psl = tsl(blk - 1) if bb > 0 else cs
                            S.pe(lambda e, psS=psS, psl=psl, cs=cs, kq=kq: e.matmul(psS[:, kq, 0, :], kT[:, psl], qT[:, cs], start=True, stop=True),
                                 reads=[kk_, qk_], writes=[f"ps{sb}"])
                            S.pe(lambda e, psS=psS, cs=cs, kq=kq: e.matmul(psS[:, kq, 1, :], kT[:, cs], qT[:, cs], start=True, stop=True),
                                 reads=[kk_, qk_], writes=[f"ps{sb}"])
                            infos.append((blk, bb, cs))
                        return (ci, infos)

                    def att_mid(ctx):
                        ci, infos = ctx
                        sb, ob_ = 4 + ci, 6 + ci
                        psS = ps[sb][:].rearrange("p (k a b) -> p k a b", a=2, b=128)
                        pe_, PT = pexp[ci], PTb[ci]
                        S.act(lambda e, psS=psS, pe_=pe_: e.activation(out=pe_, in_=psS, func=AF.Exp), reads=[f"ps{sb}"], writes=[K("pexp", ci)])
                        for kq, (blk, bb, cs) in enumerate(infos):
                            msk = M3[:, 1:3, :] if bb > 0 else M3[:, 0:3:2, :]
                            S.dve(lambda e, pe_=pe_, PT=PT, kq=kq, msk=msk: e.tensor_tensor(out=PT[:, kq, :, :], in0=pe_[:, kq, :, :], in1=msk, op=ALU.mult),
                                  reads=[K("pexp", ci), "cm32"], writes=[K("PT", ci, kq)])

                    def att_rest(ctx):
                        ci, infos = ctx
                        sb, ob_ = 4 + ci, 6 + ci
                        psO = ps[ob_][:].rearrange("p (k a b) -> p k a b", a=2, b=128)
                        pe_, PT = pexp[ci], PTb[ci]
                        for kq, (blk, bb, cs) in enumerate(infos):
                            for jj in range(2):
                                bk_ = blk - 1 if (jj == 0 and bb > 0) else blk
                                S.pe(lambda e, psO=psO, PT=PT, jj=jj, bk_=bk_, kq=kq: e.matmul(psO[:, kq, 0, :], vtok[:, bk_, :], PT[:, kq, jj, :], start=(jj == 0), stop=(jj == 1)),
                                     reads=[vtk, K("PT", ci, kq)], writes=[f"ps{ob_}"])
                            for jj in range(2):
                                S.pe(lambda e, psO=psO, PT=PT, jj=jj, kq=kq: e.matmul(psO[:, kq, 1, :], ones1, PT[:, kq, jj, :], start=(jj == 0), stop=(jj == 1)),
                                     reads=["cb16", K("PT", ci, kq)], writes=[f"ps{ob_}"])
                        for kq, (blk, bb, cs) in enumerate(infos):
                            if gi == 0:
                                S.dve(lambda e, psO=psO, cs=cs, kq=kq: e.tensor_copy(out=acc2[:, :, cs], in_=psO[:, kq, :, :]), reads=[f"ps{ob_}"], writes=["acc2"])
                            else:
                                S.dve(lambda e, psO=psO, cs=cs, kq=kq: e.tensor_tensor(out=acc2[:, :, cs], in0=acc2[:, :, cs], in1=psO[:, kq, :, :], op=ALU.add),
                                      reads=[f"ps{ob_}", "acc2"], writes=["acc2"])
                    actx = deque()

                    def att_step(pr):
                        if pr < 8:
                            actx.append(att_scores(pr))
                            att_mid(actx[-1])
                        if pr >= 1:
                            att_rest(actx.popleft())
                    att = [(lambda pr=pr: att_step(pr)) for pr in range(9)]

                    def fin():
                        if gi == 2:
                            S.dve(lambda e: e.reciprocal(out=acc2[:, 1, :], in_=acc2[:, 1, :]), reads=["acc2"], writes=["acc2"])
                            S.dve(lambda e: e.tensor_tensor(out=og[:, h, :], in0=acc2[:, 0, :], in1=acc2[:, 1, :], op=ALU.mult), reads=["acc2"], writes=[K("og", h)])
                    return groups, trans, att, fin

                iters = [(h, gi, dd) for h in range(8) for gi, dd in enumerate((1, 4, 16))]
                prev = None
                for it in range(len(iters) + 1):
                    cur_it = None
                    if it < len(iters):
                        cur_it = make_iter(*iters[it], cnt["hg"] % 2)
                        cnt["hg"] += 1
                    pop_work(4)
                    glist = cur_it[0] if cur_it else []
                    alist = prev[2] if prev else []
                    for i in range(max(len(glist), len(alist))):
                        if i < len(glist):
                            glist[i]()
                        if i < len(alist):
                            alist[i]()
                    if prev:
                        prev[3]()
                    if cur_it:
                        cur_it[1]()
                    prev = cur_it
                if s + 1 < NSEQ:
                    load_xs(xs, xb_in, tokS + SEQ, 2048)
                for half in range(2):
                    outproj_ln(lambda kc, nt, half=half: (og[:, kc, half * 1024 + nt * 512:half * 1024 + (nt + 1) * 512], K("og", kc)),
                               8, Wout, tokS + half * 1024, 2, xres, x32_out, xb_out, (l * 2) * 16, okey)
            drain_work()

        cur = None
        for f in gla_conv_ops(0):
            f()
        for sidx in range(NSUB):
            l = sidx // 2
            if sidx == 2 and NSUB > 4:
                pending_work.extend(gla_conv_ops(1))
            if sidx == 3:
                while pending_work:
                    pending_work.popleft()()
            last = (sidx == NSUB - 1)
            xres = xT if sidx == 0 else X32
            x32_out = outT if last else X32
            okey = "out" if last else "x32"
            xb_in = xT if cur is None else XB[cur]
            nxt = 0 if cur is None else 1 - cur
            xb_out = XB[nxt]
            if sidx % 2 == 1:
                ffn_phase(l, xb_in, xres, x32_out, xb_out, okey)
            elif l % 2 == 0:
                gla_phase(l, xb_in, xres, x32_out, xb_out, okey)
            else:
                dil_phase(l, xb_in, xres, x32_out, xb_out, okey)
            cur = nxt
        S.barrier()
        S.op("sp", lambda e: e.nop())
        S.emit()
    return nc


def host_consts(inputs):
    f = np.float32
    cvec = np.zeros((128, NCV), f)
    cvec[:, C_LNG:C_LNG + 128] = np.asarray(inputs["ln_g"], f).reshape(8, 16, 128).transpose(2, 0, 1).reshape(128, 128)
    cvec[:, C_LNB:C_LNB + 128] = np.asarray(inputs["ln_b"], f).reshape(8, 16, 128).transpose(2, 0, 1).reshape(128, 128)
    cvec[:, C_CW:C_CW + 1032] = np.asarray(inputs["ffn_conv_w"], f).reshape(4, 3, 86, 128).transpose(3, 0, 1, 2).reshape(128, 1032)
    cvec[:, C_CB:C_CB + 344] = np.asarray(inputs["ffn_conv_b"], f).reshape(4, 86, 128).transpose(2, 0, 1).reshape(128, 344)
    cvec[:, C_NG:C_NG + 8] = np.asarray(inputs["gla_norm_g"], f).reshape(2, 4, 128).transpose(2, 0, 1).reshape(128, 8)
    cmat = np.zeros((128, 1024), f)
    ii = np.arange(128)
    le = (ii[:, None] <= ii[None, :]).astype(f)
    ge = (ii[:, None] >= ii[None, :]).astype(f)
    cmat[:, 0:128] = le * f(-1.0 / 16.0)
    cmat[:, 256:384] = ge
    cmat[:, 384:512] = le
    cmat[:, 512:640] = np.eye(128, dtype=f)
    cmat[:, 640:768] = f(1.0 / 2048.0)
    cmat[:, 768:896] = f(1.0 / 512.0)
    cmat[:, 896:1024] = f(1.0)
    wg = np.concatenate([np.asarray(inputs["gla_w_gate_up"], f), np.asarray(inputs["gla_gate_bias"], f)[:, None, :]], axis=1)
    return cvec, cmat, np.ascontiguousarray(wg)


_PROG = {}


def run(inputs, ncores=NCORES, nseq=NSEQ_CORE, nsub=8, trace=False):
    key = (nseq, nsub)
    if key not in _PROG:
        _PROG[key] = build(nseq, nsub)
    nc = _PROG[key]
    cvec, cmat, wg = host_consts(inputs)
    x = np.asarray(inputs["x"], np.float32)
    shared = {k: np.ascontiguousarray(np.asarray(inputs[k], np.float32)) for k in
              ("gla_w_in", "gla_w_out", "dil_w_in", "dil_w_out", "ffn_w_up", "ffn_w_down")}
    in_maps = []
    for c in range(ncores):
        xc = x[c * nseq:(c + 1) * nseq].reshape(nseq * SEQ, D)
        m = dict(shared)
        m.update(xT=np.ascontiguousarray(xc.T), cvec=cvec, cmat=cmat, wg_aug=wg)
        in_maps.append(m)
    res = run_bass_kernel_spmd(nc, in_maps, core_ids=list(range(ncores)), **({"trace": True} if trace else {}))
    out = np.empty((ncores * nseq, SEQ, D), np.float32)
    for c in range(ncores):
        oT = np.asarray(res.results[c]["outT"])
        out[c * nseq:(c + 1) * nseq] = oT.T.reshape(nseq, SEQ, D)
    return out, res


def kernel(**inputs):
    out, _ = run(inputs)
    return out
```

```python
import contextlib
from collections import deque
import numpy as np
import concourse.bass as bass
import concourse.mybir as mybir
from concourse.bass_utils import run_bass_kernel_spmd

F32 = mybir.dt.float32
BF16 = mybir.dt.bfloat16
AF = mybir.ActivationFunctionType
ALU = mybir.AluOpType

D = 2048
SEQ = 2048
DFF = 5504
NFC = 43
ALPHA = float(8 ** 0.25)
NCORES = 8
NSEQ_CORE = 2

ENGS = ["pe", "act", "dve", "pool", "sp"]
NDMASEM = 16
EPOCH = 30000
NWB = 6
ARENA_BYTES = 212800

NCV = 1640
C_LNG, C_LNB, C_CW, C_CB, C_NG = 0, 128, 256, 1288, 1632


def K(*a):
    return "_".join(map(str, a))


class Sched:
    def __init__(self, nc):
        self.nc = nc
        self.ops = []
        self.last_w = {}
        self.readers = {}
        self.last_comp = {}
        self.recent_dma = {e: deque(maxlen=NDMASEM) for e in ENGS}
        self.bar_deps = set()
        self.bar_pending = set()

    def barrier(self):
        deps = set()
        for e in ENGS:
            if e in self.last_comp:
                deps.add(self.last_comp[e])
            deps.update(self.recent_dma[e])
        self.bar_deps = deps
        self.bar_pending = set(ENGS)
        self.last_w = {}
        self.readers = {}

    def op(self, eng, fn, reads=(), writes=(), dma=False):
        i = len(self.ops)
        deps = set()
        for k in reads:
            w = self.last_w.get(k)
            if w is not None:
                deps.add(w)
        for k in writes:
            w = self.last_w.get(k)
            if w is not None:
                deps.add(w)
            rs = self.readers.get(k)
            if rs:
                deps.update(rs)
        for k in reads:
            self.readers.setdefault(k, []).append(i)
        for k in writes:
            self.last_w[k] = i
            self.readers[k] = []
        if eng in self.bar_pending:
            deps |= self.bar_deps
            self.bar_pending.discard(eng)
        deps.discard(i)
        self.ops.append((eng, fn, deps, dma))
        if dma:
            self.recent_dma[eng].append(i)
        else:
            self.last_comp[eng] = i
        return i

    def pe(self, fn, reads=(), writes=()):
        return self.op("pe", fn, reads, writes)

    def act(self, fn, reads=(), writes=()):
        return self.op("act", fn, reads, writes)

    def dve(self, fn, reads=(), writes=()):
        return self.op("dve", fn, reads, writes)

    def dma(self, q, fn, reads=(), writes=()):
        return self.op(q, fn, reads, writes, dma=True)

    def emit(self):
        nc = self.nc
        ops = self.ops
        n = len(ops)
        need_inc = [False] * n
        for i, (eng, fn, deps, dma) in enumerate(ops):
            for d in deps:
                deng, _, _, ddma = ops[d]
                if ddma:
                    continue
                if deng == "pe" and eng == "pe" and not dma:
                    continue
                need_inc[d] = True
        cnt = {e: 0 for e in ENGS}
        dcnt = {e: 0 for e in ENGS}
        sig = [None] * n
        for i, (eng, fn, deps, dma) in enumerate(ops):
            if dma:
                k = dcnt[eng]
                dcnt[eng] += 1
                sig[i] = (("d", eng, k % NDMASEM), 16 * (k // NDMASEM + 1))
            elif need_inc[i]:
                c = cnt[eng]
                cnt[eng] += 1
                sig[i] = (("c", eng, c // EPOCH), (c % EPOCH) + 1)
        semkeys = sorted({s[0] for s in sig if s is not None})
        with contextlib.ExitStack() as st:
            sems = {}
            for k in semkeys:
                sems[k] = st.enter_context(nc.semaphore("s_" + "_".join(map(str, k))))
            block = st.enter_context(nc.Block())

            def run_engine(ename, e):
                waited = {}
                for i, (eng, fn, deps, dma) in enumerate(ops):
                    if eng != ename:
                        continue
                    wl = {}
                    for d in deps:
                        deng, _, _, ddma = ops[d]
                        if (not ddma) and deng == "pe" and eng == "pe" and not dma:
                            continue
                        sk, v = sig[d]
                        if waited.get(sk, 0) >= v:
                            continue
                        if wl.get(sk, 0) < v:
                            wl[sk] = v
                    if dma:
                        sk, v = sig[i]
                        if v > 16 and waited.get(sk, 0) < v - 16 and wl.get(sk, 0) < v - 16:
                            wl[sk] = v - 16
                    for sk, v in wl.items():
                        e.wait_ge(sems[sk], v)
                        waited[sk] = v
                    ins = fn(e)
                    if sig[i] is not None:
                        ins.then_inc(sems[sig[i][0]], 16 if dma else 1)

            @block.tensor
            def _(e):
                run_engine("pe", e)

            @block.scalar
            def _(e):
                run_engine("act", e)

            @block.vector
            def _(e):
                run_engine("dve", e)

            @block.gpsimd
            def _(e):
                run_engine("pool", e)

            @block.sync
            def _(e):
                run_engine("sp", e)


class Alloc:
    def __init__(self, arena, base=0):
        self.a = arena
        self.off = base

    def get(self, shape, dt, parts=128):
        n = int(np.prod(shape))
        nb = n * (4 if dt == F32 else 2)
        start = self.off
        self.off += (nb + 63) // 64 * 64
        assert self.off <= ARENA_BYTES, ("arena overflow", self.off)
        ap = self.a[0:parts, start // 2:(start + nb) // 2]
        if dt == F32:
            ap = ap.bitcast(F32)
        if len(shape) == 2:
            ap = ap.rearrange("p (a b) -> p a b", b=shape[1])
        elif len(shape) == 3:
            ap = ap.rearrange("p (a b c) -> p a b c", b=shape[1], c=shape[2])
        return ap


def build(NSEQ=NSEQ_CORE, NSUB=8):
    NT = NSEQ * SEQ
    nc = bass.Bass("TRN2", target_bir_lowering=False)

    def din(name, shape):
        return nc.dram_tensor(name, list(shape), F32, kind="ExternalInput").ap()

    xT = din("xT", [D, NT])
    gla_w_in = din("gla_w_in", [2, D, 6160])
    gla_w_out = din("gla_w_out", [2, D, D])
    dil_w_in = din("dil_w_in", [2, D, 9216])
    dil_w_out = din("dil_w_out", [2, 1024, D])
    ffn_w_up = din("ffn_w_up", [4, D, 11008])
    ffn_w_down = din("ffn_w_down", [4, DFF, D])
    cvec_d = din("cvec", [128, NCV])
    cmat_d = din("cmat", [128, 1024])
    wg_d = din("wg_aug", [2, 17, 1024])
    outT = nc.dram_tensor("outT", [D, NT], F32, kind="ExternalOutput").ap()
    X32 = nc.dram_tensor("X32", [D, NT], F32).ap()
    XB = [nc.dram_tensor(f"XB{i}", [D, NT], BF16).ap() for i in range(2)]
    WS = nc.dram_tensor("WS", [2, 64, 128, 2048], BF16).ap()

    with contextlib.ExitStack() as st:
        arena = st.enter_context(nc.sbuf_tensor("arena", [128, ARENA_BYTES // 2], BF16))
        ps = [st.enter_context(nc.psum_tensor(f"ps{i}", [128, 512], F32)) for i in range(8)]
        S = Sched(nc)

        cm = Alloc(arena, 0)
        cvec = cm.get([NCV], F32)
        cm32 = cm.get([4, 128], F32)
        tri = cm32[:, 0, :]
        M3 = cm32[:, 1:4, :]
        mcur = cm32[:, 3, :]
        cb16 = cm.get([4, 128], BF16)
        ident, onesD, ones512, ones1 = cb16[:, 0, :], cb16[:, 1, :], cb16[:, 2, :], cb16[:, 3, :]
        wpool = [cm.get([2048], BF16) for _ in range(NWB)]
        XR = [cm.get([512], F32) for _ in range(3)]
        ybf = cm.get([512], BF16)
        ysq = cm.get([512], BF16)
        mean_t = cm.get([512], F32)
        tmp_t = cm.get([512], F32)
        rstd_t = [cm.get([512], F32) for _ in range(2)]
        nmr_t = [cm.get([512], F32) for _ in range(2)]
        YL = [cm.get([512], F32) for _ in range(4)]
        OBF = [cm.get([512], BF16) for _ in range(2)]
        PHASE_BASE = cm.off

        S.dma("sp", lambda e: e.dma_start(out=cvec, in_=cvec_d), writes=["cvec"])
        S.dma("sp", lambda e: e.dma_start(out=cm32, in_=cmat_d[:, 0:512].rearrange("p (a b) -> p a b", b=128)), writes=["cm32"])
        S.dma("pool", lambda e: e.dma_start(out=cb16, in_=cmat_d[:, 512:1024].rearrange("p (a b) -> p a b", b=128)), writes=["cb16"])

        ctr = {"wb": 0, "bank": 0, "xr": 0, "yl": 0, "obf": 0}

        def wload(Wd, k0, nk, col0, ncols):
            i = ctr["wb"] % NWB
            ctr["wb"] += 1
            view = wpool[i][:, 0:nk * ncols].rearrange("p (a b) -> p a b", b=ncols)
            src = Wd[k0 * 128:(k0 + nk) * 128, col0:col0 + ncols].rearrange("(a p) m -> p a m", p=128)
            key = f"wb{i}"
            S.dma("pool", lambda e: e.dma_start(out=view, in_=src), writes=[key])
            return view, key

        def ws_src(j, cid):
            if cid < 48:
                col0 = (cid * 128 if cid < 8 else 1024 + (cid - 8) * 128 if cid < 16 else 4096 + (cid - 16) * 128 if cid < 32 else 2048 + (cid - 32) * 128)
                return gla_w_in[j], col0
            return gla_w_out[j], (cid - 48) * 128

        def gla_conv_ops(j):
            lst = []
            for cid in range(64):
                def f(cid=cid):
                    Wd, col0 = ws_src(j, cid)
                    view, key = wload(Wd, 0, 16, col0, 128)
                    dst = WS[j, cid].rearrange("p (a b) -> p a b", b=128)
                    S.dma("sp", lambda e: e.dma_start(out=dst, in_=view), reads=[key], writes=[K("ws", j, cid)])
                lst.append(f)
            return lst

        def wload_ws(j, cid):
            i = ctr["wb"] % NWB
            ctr["wb"] += 1
            view = wpool[i].rearrange("p (a b) -> p a b", b=128)
            src = WS[j, cid].rearrange("p (a b) -> p a b", b=128)
            key = f"wb{i}"
            S.dma("pool", lambda e: e.dma_start(out=view, in_=src), reads=[K("ws", j, cid)], writes=[key])
            return view, key

        def pk(b):
            return [f"ps{b}", f"ps{b}_0", f"ps{b}_1"]

        pending_work = deque()

        def pop_work(n=1):
            for _ in range(n):
                if pending_work:
                    pending_work.popleft()()

        def drain_work():
            while pending_work:
                pending_work.popleft()()

        def mainbank(nb=4):
            b = ctr["bank"] % nb
            ctr["bank"] += 1
            return b, f"ps{b}"

        def load_xs(xs, src, tok0, T):
            q = "pool" if src.dtype == F32 else "sp"
            for g in range(4):
                v = src[g * 512:(g + 1) * 512, tok0:tok0 + T].rearrange("(a p) t -> p a t", p=128)
                rk = [K("xb", c, t) for c in range(4 * g, 4 * g + 4) for t in range(tok0, tok0 + T, 512)]
                S.dma(q, lambda e, g=g, v=v: e.dma_start(out=xs[:, 4 * g:4 * g + 4, :], in_=v), reads=rk, writes=[f"xs{g}"])

        def outproj_ln(src_fn, KC, Wd, tok0, NTL, xres, x32_out, xb_out, lncol, okey, wfn=None, defer=True):
            pend = [None]
            groups = [(oc, nt) for oc in range(16) for nt in range(NTL)]
            wts = {}
            xrs = {}

            def load_w(oc):
                if wfn is not None:
                    wv, wk = wfn(oc)
                    wts[oc] = [(0, KC, wv, wk)]
                    return
                pieces = []
                for p0 in range(0, KC, 16):
                    n = min(16, KC - p0)
                    pieces.append((p0, n) + wload(Wd, p0, n, oc * 128, 128))
                wts[oc] = pieces

            def load_xr(g):
                oc, nt = groups[g]
                t0 = tok0 + nt * 512
                xi = ctr["xr"] % 3
                ctr["xr"] += 1
                xr, xk = XR[xi], f"xr{xi}"
                xsrc = xres[oc * 128:(oc + 1) * 128, t0:t0 + 512]
                S.dma("sp", lambda e, xr=xr, xsrc=xsrc: e.dma_start(out=xr, in_=xsrc), reads=[K("x32", oc, t0)], writes=[xk])
                xrs[g] = (xr, xk)

            load_w(0)
            for g in range(min(2, len(groups))):
                load_xr(g)
            for gi, (oc, nt) in enumerate(groups):
                if True:
                    if nt == 0 and oc + 1 < 16:
                        load_w(oc + 1)
                    if gi + 2 < len(groups):
                        load_xr(gi + 2)
                    pieces = wts[oc]
                    t0 = tok0 + nt * 512
                    b, bk = mainbank()
                    xr, xk = xrs.pop(gi)
                    for (p0, n, wv, wk) in pieces:
                        for kk in range(n):
                            kc = p0 + kk
                            rhs, rk = src_fn(kc, nt)
                            S.pe(lambda e, b=b, wv=wv, kk=kk, rhs=rhs, kc=kc: e.matmul(ps[b][:], wv[:, kk, :], rhs, start=(kc == 0), stop=(kc == KC - 1)),
                                 reads=[wk, rk], writes=[bk])
                    if pend[0]:
                        pend[0]()
                        pend[0] = None
                    S.dve(lambda e, xr=xr, b=b: e.scalar_tensor_tensor(out=xr, in0=xr, scalar=ALPHA, in1=ps[b][:], op0=ALU.mult, op1=ALU.add),
                          reads=[xk, bk], writes=[xk])
                    S.act(lambda e, xr=xr: e.activation(out=ybf, in_=xr, func=AF.Copy), reads=[xk], writes=["ybf"])
                    S.act(lambda e, xr=xr: e.activation(out=ysq, in_=xr, func=AF.Square), reads=[xk], writes=["ysq"])
                    ydst = X32[oc * 128:(oc + 1) * 128, t0:t0 + 512]
                    S.dma("act", lambda e, xr=xr, ydst=ydst: e.dma_start(out=ydst, in_=xr), reads=[xk], writes=[K("x32", oc, t0)])

                    def mk(oc=oc, nt=nt):
                        pm, pq = 4 + 2 * nt, 5 + 2 * nt
                        S.pe(lambda e: e.matmul(ps[pm][:], onesD, ybf, start=(oc == 0), stop=(oc == 15)), reads=["ybf", "cb16"], writes=pk(pm))
                        S.pe(lambda e: e.matmul(ps[pq][:], onesD, ysq, start=(oc == 0), stop=(oc == 15)), reads=["ysq", "cb16"], writes=pk(pq))
                    pend[0] = mk
            pend[0]()
            units = []
            for nt in range(NTL):
                t0 = tok0 + nt * 512
                pm, pq = 4 + 2 * nt, 5 + 2 * nt
                rstd, nmr = rstd_t[nt], nmr_t[nt]
                S.act(lambda e, pm=pm: e.activation(out=mean_t, in_=ps[pm][:], func=AF.Copy), reads=pk(pm), writes=["mean"])
                S.dve(lambda e: e.tensor_tensor(out=tmp_t, in0=mean_t, in1=mean_t, op=ALU.mult), reads=["mean"], writes=["tmp"])
                S.dve(lambda e, pq=pq: e.tensor_tensor(out=tmp_t, in0=ps[pq][:], in1=tmp_t, op=ALU.subtract), reads=pk(pq) + ["tmp"], writes=["tmp"])
                S.act(lambda e: e.activation(out=tmp_t, in_=tmp_t, func=AF.Sqrt, bias=1e-5, scale=1.0), reads=["tmp"], writes=["tmp"])
                S.dve(lambda e, rstd=rstd: e.reciprocal(out=rstd, in_=tmp_t), reads=["tmp"], writes=[f"rstd{nt}"])
                S.dve(lambda e, rstd=rstd, nmr=nmr: e.scalar_tensor_tensor(out=nmr, in0=mean_t, scalar=-1.0, in1=rstd, op0=ALU.mult, op1=ALU.mult),
                      reads=["mean", f"rstd{nt}"], writes=[f"nmr{nt}"])
                for oc in range(16):
                    def ld(oc=oc, nt=nt, t0=t0):
                        yi = ctr["yl"] % 4
                        ctr["yl"] += 1
                        yl, yk = YL[yi], f"yl{yi}"
                        ysrc = X32[oc * 128:(oc + 1) * 128, t0:t0 + 512]
                        S.dma("sp", lambda e, yl=yl, ysrc=ysrc: e.dma_start(out=yl, in_=ysrc), reads=[K("x32", oc, t0)], writes=[yk])
                        return yl, yk

                    def cpd(yl, yk, oc=oc, nt=nt, t0=t0, rstd=rstd, nmr=nmr):
                        S.dve(lambda e, yl=yl, rstd=rstd: e.tensor_tensor(out=yl, in0=yl, in1=rstd, op=ALU.mult), reads=[yk, f"rstd{nt}"], writes=[yk])
                        S.dve(lambda e, yl=yl, nmr=nmr: e.tensor_tensor(out=yl, in0=yl, in1=nmr, op=ALU.add), reads=[yk, f"nmr{nt}"], writes=[yk])

                    def cp(yl, yk, oc=oc, nt=nt, t0=t0, rstd=rstd, nmr=nmr):
                        oi = ctr["obf"] % 2
                        ctr["obf"] += 1
                        ob, obk = OBF[oi], f"obf{oi}"
                        gcol = cvec[:, C_LNG + lncol + oc:C_LNG + lncol + oc + 1]
                        bcol = cvec[:, C_LNB + lncol + oc:C_LNB + lncol + oc + 1]
                        S.act(lambda e, yl=yl, ob=ob, gcol=gcol, bcol=bcol: e.activation(out=ob, in_=yl, func=AF.Identity, bias=bcol, scale=gcol),
                              reads=[yk, "cvec"], writes=[obk])
                        S.act(lambda e, yl=yl, gcol=gcol, bcol=bcol: e.activation(out=yl, in_=yl, func=AF.Identity, bias=bcol, scale=gcol),
                              reads=[yk, "cvec"], writes=[yk])
                        odst = x32_out[oc * 128:(oc + 1) * 128, t0:t0 + 512]
                        S.dma("act", lambda e, yl=yl, odst=odst: e.dma_start(out=odst, in_=yl), reads=[yk], writes=[K(okey, oc, t0)])
                        bdst = xb_out[oc * 128:(oc + 1) * 128, t0:t0 + 512]
                        S.dma("act", lambda e, ob=ob, bdst=bdst: e.dma_start(out=bdst, in_=ob), reads=[obk], writes=[K("xb", oc, t0)])
                    units.append((ld, cp, cpd))
            nun = len(units)
            st_ = {}
            for i in range(nun + 3):
                def item(i=i):
                    if 0 <= i - 3 < nun:
                        units[i - 3][1](*st_.pop(i - 3))
                    if 0 <= i - 2 < nun:
                        units[i - 2][2](*st_[i - 2])
                    if i < nun:
                        st_[i] = units[i][0]()
                if defer:
                    pending_work.append(item)
                else:
                    item()

        def ffn_phase(l, xb_in, xres, x32_out, xb_out, okey):
            S.barrier()
            al = Alloc(arena, PHASE_BASE)
            xs = al.get([16, 1024], BF16)
            actT = al.get([NFC, 1024], BF16)
            hraw = [[al.get([514], F32) for _ in range(2)] for _ in range(2)]
            tt = [[al.get([512], F32) for _ in range(2)] for _ in range(2)]
            sg = [al.get([512], F32) for _ in range(2)]
            halo = al.get([86, 2], F32)
            Wup = ffn_w_up[l]
            Wdn = ffn_w_down[l]
            load_xs(xs, xb_in, 0, 1024)
            for tile in range(NT // 1024):
                tok0 = tile * 1024
                seq_start = (tok0 % SEQ == 0)
                for j in range(NFC):
                    for half in range(2):
                        c = j + NFC * half
                        if half == 0:
                            pop_work(1)
                        wv, wk = wload(Wup, 0, 16, c * 128, 128)
                        cw = [cvec[:, C_CW + (l * 3 + tap) * 86 + c:C_CW + (l * 3 + tap) * 86 + c + 1] for tap in range(3)]
                        cbc = cvec[:, C_CB + l * 86 + c:C_CB + l * 86 + c + 1]
                        for nt in range(2):
                            b, bk = mainbank()
                            for kc in range(16):
                                S.pe(lambda e, b=b, wv=wv, kc=kc, nt=nt: e.matmul(ps[b][:], wv[:, kc, :], xs[:, kc, nt * 512:(nt + 1) * 512],
                                                                                  start=(kc == 0), stop=(kc == 15)),
                                     reads=[wk, f"xs{kc // 4}"], writes=[bk])
                            hr, hk = hraw[half][nt], K("hr", half, nt)
                            t_, tk = tt[half][nt], K("tt", half, nt)
                            if nt == 0 and seq_start:
                                S.act(lambda e, hr=hr: e.memzero(hr[:, 0:2]), writes=[hk])
                            else:
                                S.act(lambda e, hr=hr, c=c: e.copy(out=hr[:, 0:2], in_=halo[:, c, :]), reads=[K("halo", c)], writes=[hk])
                            S.act(lambda e, hr=hr, b=b: e.copy(out=hr[:, 2:514], in_=ps[b][:]), reads=[bk], writes=[hk])
                            S.act(lambda e, hr=hr, c=c: e.copy(out=halo[:, c, :], in_=hr[:, 512:514]), reads=[hk], writes=[K("halo", c)])
                            S.act(lambda e, t_=t_, b=b, cw=cw, cbc=cbc: e.activation(out=t_, in_=ps[b][:], func=AF.Identity, bias=cbc, scale=cw[2]),
                                  reads=[bk, "cvec"], writes=[tk])
                            S.dve(lambda e, t_=t_, hr=hr, cw=cw: e.scalar_tensor_tensor(out=t_, in0=hr[:, 1:513], scalar=cw[1], in1=t_, op0=ALU.mult, op1=ALU.add),
                                  reads=[hk, tk, "cvec"], writes=[tk])
                            S.dve(lambda e, t_=t_, hr=hr, cw=cw: e.scalar_tensor_tensor(out=t_, in0=hr[:, 0:512], scalar=cw[0], in1=t_, op0=ALU.mult, op1=ALU.add),
                                  reads=[hk, tk, "cvec"], writes=[tk])
                            if half == 0:
                                S.act(lambda e, t_=t_, nt=nt: e.activation(out=sg[nt], in_=t_, func=AF.Silu), reads=[tk], writes=[K("sg", nt)])
                            else:
                                S.dve(lambda e, t_=t_, nt=nt, j=j: e.tensor_tensor(out=actT[:, j, nt * 512:(nt + 1) * 512], in0=t_, in1=sg[nt], op=ALU.mult),
                                      reads=[tk, K("sg", nt)], writes=[K("act", j, nt)])
                if tok0 + 1024 < NT:
                    load_xs(xs, xb_in, tok0 + 1024, 1024)
                outproj_ln(lambda kc, nt: (actT[:, kc, nt * 512:(nt + 1) * 512], K("act", kc, nt)),
                           NFC, Wdn, tok0, 2, xres, x32_out, xb_out, (l * 2 + 1) * 16, okey)
            drain_work()

        def gla_phase(l, xb_in, xres, x32_out, xb_out, okey):
            j = l // 2
            S.barrier()
            al = Alloc(arena, PHASE_BASE)
            xs = al.get([16, 512], BF16)
            og = xs
            qT = al.get([8, 512], BF16)
            kT = al.get([8, 512], BF16)
            vsb = al.get([4, 2048], BF16)
            rs = al.get([16, 512], BF16)
            glow = al.get([512], F32, parts=17)
            wg = al.get([1024], F32, parts=17)
            S32 = al.get([8, 512], F32)
            Sbf = al.get([8, 512], BF16)
            u = al.get([1024], F32)
            EB = al.get([8, 128], F32)
            ENB = al.get([8, 128], F32)
            qs2 = [al.get([8, 128], BF16) for _ in range(2)]
            ks = al.get([8, 128], BF16)
            kh = al.get([8, 128], BF16)
            khat2 = [al.get([1024], BF16) for _ in range(2)]
            PT2 = [al.get([4, 128], BF16) for _ in range(2)]
            EL2 = [al.get([8], F32) for _ in range(2)]
            sq = al.get([16, 128], BF16)
            rsq = al.get([4, 128], F32)
            t1 = al.get([16, 128], F32)
            Win = gla_w_in[j]
            Wout = gla_w_out[j]
            S.dma("sp", lambda e: e.dma_start(out=wg, in_=wg_d[j]), writes=["wg"])
            S.dve(lambda e: e.memset(glow, 1.0), writes=["glow"])
            ps5bf = ps[5][:].bitcast(BF16)
            XS_KEYS = [f"xs{g}" for g in range(4)]
            for tile in range(NT // 512):
                tok0 = tile * 512
                seq_start = (tok0 % SEQ == 0)
                load_xs(xs, xb_in, tok0, 512)
                if seq_start:
                    S.dve(lambda e: e.memset(S32, 0.0), writes=[K("S32", h) for h in range(8)])
                    S.dve(lambda e: e.memset(Sbf, 0.0), writes=[K("Sbf", h) for h in range(8)])

                def fm_chunk(cid, evac):
                    pop_work(1)
                    wv, wk = wload_ws(j, cid)
                    b, bk = mainbank()
                    for kc in range(16):
                        S.pe(lambda e, b=b, wv=wv, kc=kc: e.matmul(ps[b][:], wv[:, kc, :], xs[:, kc, :], start=(kc == 0), stop=(kc == 15)),
                             reads=[wk, f"xs{kc // 4}"], writes=[bk])
                    evac(b, bk)
                for c in range(8):
                    fm_chunk(c, lambda b, bk, c=c: S.act(lambda e: e.activation(out=qT[:, c, :], in_=ps[b][:], func=AF.Identity, scale=0.0625),
                                                                reads=[bk], writes=[K("qT", c)]))
                for c in range(8):
                    fm_chunk(8 + c, lambda b, bk, c=c: S.dve(lambda e: e.tensor_copy(out=kT[:, c, :], in_=ps[b][:]),
                                                                       reads=[bk], writes=[K("kT", c)]))
                for c in range(16):
                    fm_chunk(16 + c, lambda b, bk, c=c: S.act(lambda e: e.activation(out=rs[:, c, :], in_=ps[b][:], func=AF.Silu),
                                                                       reads=[bk], writes=[K("rs", c)]))
                wv, wk = wload(Win, 0, 16, 6144, 16)
                b, bk = mainbank()
                for kc in range(16):
                    S.pe(lambda e, b=b, wv=wv, kc=kc: e.matmul(ps[b][0:16, :], wv[:, kc, :], xs[:, kc, :], start=(kc == 0), stop=(kc == 15)),
                         reads=[wk, f"xs{kc // 4}"], writes=[bk])
                S.act(lambda e, b=b: e.copy(out=glow[0:16, :], in_=ps[b][0:16, :]), reads=[bk], writes=["glow"])
                for cg in range(16):
                    wv, wk = wload_ws(j, 32 + cg)
                    b, bk = mainbank()
                    for tc in range(4):
                        for kc in range(16):
                            S.pe(lambda e, b=b, wv=wv, kc=kc, tc=tc: e.matmul(ps[b][:, tc * 128:(tc + 1) * 128], xs[:, kc, tc * 128:(tc + 1) * 128], wv[:, kc, :],
                                                                              start=(kc == 0), stop=(kc == 15)),
                                 reads=[wk, f"xs{kc // 4}"], writes=[bk])
                    S.act(lambda e, b=b, cg=cg: e.copy(out=vsb[:, :, cg * 128:(cg + 1) * 128], in_=ps[b][:].rearrange("p (a b) -> p a b", b=128)),
                          reads=[bk], writes=[K("v", cg)])
                VK = [K("v", cg) for cg in range(16)]
                def G(tc):
                    bsel = tc % 2
                    tsl = slice(tc * 128, (tc + 1) * 128)
                    qs, PT, khat, EL = qs2[bsel], PT2[bsel], khat2[bsel], EL2[bsel]
                    for hf in range(2):
                        zb = 6 + hf
                        S.pe(lambda e, hf=hf, zb=zb: e.matmul(ps[zb][:], glow[0:17, tsl], wg[0:17, hf * 512:(hf + 1) * 512], start=True, stop=True),
                             reads=["glow", "wg"], writes=[f"ps{zb}"])
                        S.act(lambda e, hf=hf, zb=zb: e.activation(out=u[:, hf * 512:(hf + 1) * 512], in_=ps[zb][:], func=AF.Exp, scale=-1.0),
                              reads=[f"ps{zb}"], writes=[K("u", hf)])
                        S.act(lambda e, hf=hf: e.activation(out=u[:, hf * 512:(hf + 1) * 512], in_=u[:, hf * 512:(hf + 1) * 512], func=AF.Ln, bias=1.0, scale=1.0),
                              reads=[K("u", hf)], writes=[K("u", hf)])
                    for dc in range(8):
                        pb = 6 + dc // 4
                        S.pe(lambda e, dc=dc, pb=pb: e.matmul(ps[pb][:, (dc % 4) * 128:(dc % 4 + 1) * 128], u[:, dc * 128:(dc + 1) * 128], tri, start=True, stop=True),
                             reads=[K("u", dc // 4), "cm32"], writes=[f"ps{pb}"])
                    for hf in range(2):
                        pv = ps[6 + hf][:].rearrange("p (a b) -> p a b", b=128)
                        S.act(lambda e, hf=hf, pv=pv: e.activation(out=EB[:, hf * 4:(hf + 1) * 4, :], in_=pv, func=AF.Exp), reads=[f"ps{6 + hf}"], writes=[K("EB", hf)])
                        S.act(lambda e, hf=hf, pv=pv: e.activation(out=ENB[:, hf * 4:(hf + 1) * 4, :], in_=pv, func=AF.Exp, scale=-1.0), reads=[f"ps{6 + hf}"], writes=[K("ENB", hf)])
                    EBK = [K("EB", 0), K("EB", 1)]
                    ENBK = [K("ENB", 0), K("ENB", 1)]
                    S.act(lambda e, EL=EL: e.copy(out=EL, in_=EB[:, :, 127]), reads=EBK, writes=[K("EL", bsel)])
                    S.dve(lambda e, qs=qs: e.tensor_tensor(out=qs, in0=qT[:, :, tsl], in1=EB, op=ALU.mult), reads=[K("qT", c) for c in range(8)] + EBK, writes=[K("qs", bsel)])
                    S.dve(lambda e: e.tensor_tensor(out=ks, in0=kT[:, :, tsl], in1=ENB, op=ALU.mult), reads=[K("kT", c) for c in range(8)] + ENBK, writes=["ks"])
                    S.dve(lambda e: e.tensor_tensor(out=kh, in0=ks, in1=EB[:, :, 127:128].broadcast_to([128, 8, 128]), op=ALU.mult), reads=["ks"] + EBK, writes=["kh"])
                    for h in range(4):
                        for dl in range(2):
                            S.pe(lambda e, h=h, dl=dl, qs=qs: e.matmul(ps[4][:, h * 128:(h + 1) * 128], ks[:, 2 * h + dl, :], qs[:, 2 * h + dl, :], start=(dl == 0), stop=(dl == 1)),
                                 reads=["ks", K("qs", bsel)], writes=["ps4"])
                    S.dve(lambda e, PT=PT: e.tensor_tensor(out=PT, in0=ps[4][:].rearrange("p (a b) -> p a b", b=128),
                                                           in1=mcur.unsqueeze(1).broadcast_to([128, 4, 128]), op=ALU.mult), reads=["ps4", "cm32"], writes=[K("PT", bsel)])
                    for dc in range(8):
                        S.pe(lambda e, dc=dc: e.transpose(ps5bf[:, dc * 128:(dc + 1) * 128], kh[:, dc, :], ident), reads=["kh", "cb16"], writes=["ps5"])
                    S.act(lambda e, khat=khat: e.copy(out=khat, in_=ps5bf), reads=["ps5"], writes=[K("khat", bsel)])

                def B(tc):
                    bsel = tc % 2
                    tsl = slice(tc * 128, (tc + 1) * 128)
                    qs, PT, khat, EL = qs2[bsel], PT2[bsel], khat2[bsel], EL2[bsel]
                    for h in range(4):
                        for ec in range(4):
                            o_ = ps[h][:, ec * 128:(ec + 1) * 128]
                            S.pe(lambda e, h=h, ec=ec, o_=o_, PT=PT: e.matmul(o_, vsb[:, tc, h * 512 + ec * 128:h * 512 + (ec + 1) * 128], PT[:, h, :], start=True, stop=False),
                                 reads=VK + [K("PT", bsel)], writes=[f"ps{h}"])
                            for dl in range(2):
                                S.pe(lambda e, h=h, ec=ec, o_=o_, dl=dl, qs=qs: e.matmul(o_, Sbf[:, 2 * h + dl, ec * 128:(ec + 1) * 128], qs[:, 2 * h + dl, :], start=False, stop=(dl == 1)),
                                     reads=[K("Sbf", 2 * h + dl), K("qs", bsel)], writes=[f"ps{h}"])
                    for hd in range(8):
                        h = hd // 2
                        pb = (5, 4)[hd % 2]
                        S.pe(lambda e, hd=hd, h=h, pb=pb, khat=khat: e.matmul(ps[pb][:], khat[:, hd * 128:(hd + 1) * 128], vsb[:, tc, h * 512:(h + 1) * 512], start=True, stop=True),
                             reads=[K("khat", bsel)] + VK, writes=[f"ps{pb}"])
                        S.dve(lambda e, hd=hd, pb=pb, EL=EL: e.scalar_tensor_tensor(out=S32[:, hd, :], in0=S32[:, hd, :], scalar=EL[:, hd:hd + 1], in1=ps[pb][:],
                                                                                    op0=ALU.mult, op1=ALU.add),
                              reads=[K("S32", hd), f"ps{pb}", K("EL", bsel)], writes=[K("S32", hd)])
                        S.act(lambda e, hd=hd: e.copy(out=Sbf[:, hd, :], in_=S32[:, hd, :]), reads=[K("S32", hd)], writes=[K("Sbf", hd)])
                    for h in range(4):
                        S.act(lambda e, h=h: e.activation(out=sq[:, h * 4:(h + 1) * 4, :], in_=ps[h][:].rearrange("p (a b) -> p a b", b=128), func=AF.Square),
                              reads=[f"ps{h}"], writes=[K("sq", h)])
                    for h in range(4):
                        for ec in range(4):
                            S.pe(lambda e, h=h, ec=ec: e.matmul(ps[4][:, h * 128:(h + 1) * 128], ones512, sq[:, h * 4 + ec, :], start=(ec == 0), stop=(ec == 3)),
                                 reads=[K("sq", h), "cb16"], writes=["ps4"])
                    S.act(lambda e: e.activation(out=rsq, in_=ps[4][:].rearrange("p (a b) -> p a b", b=128), func=AF.Sqrt, bias=1e-6, scale=1.0), reads=["ps4"], writes=["rsq"])
                    S.dve(lambda e: e.reciprocal(out=rsq, in_=rsq), reads=["rsq"], writes=["rsq"])
                    for h in range(4):
                        S.dve(lambda e, h=h: e.tensor_tensor(out=t1[:, h * 4:(h + 1) * 4, :], in0=ps[h][:].rearrange("p (a b) -> p a b", b=128),
                                                             in1=rsq[:, h:h + 1, :].broadcast_to([128, 4, 128]), op=ALU.mult),
                              reads=[f"ps{h}", "rsq"], writes=[K("t1", h)])
                    for ec in range(4):
                        ngc = cvec[:, C_NG + j * 4 + ec:C_NG + j * 4 + ec + 1]
                        S.dve(lambda e, ec=ec, ngc=ngc: e.scalar_tensor_tensor(out=og[:, ec::4, tsl], in0=t1[:, ec::4, :], scalar=ngc, in1=rs[:, ec::4, tsl],
                                                                              op0=ALU.mult, op1=ALU.mult),
                              reads=[K("t1", h) for h in range(4)] + [K("rs", c) for c in range(16)] + ["cvec"], writes=XS_KEYS + [K("og", tc)])

                G(0)
                for tc in range(4):
                    if tc < 3:
                        G(tc + 1)
                    B(tc)
                outproj_ln(lambda kc, nt: (og[:, kc, :], f"xs{kc // 4}"), 16, Wout, tok0, 1, xres, x32_out, xb_out, (l * 2) * 16, okey, wfn=lambda oc: wload_ws(j, 48 + oc))
            drain_work()

        def dil_phase(l, xb_in, xres, x32_out, xb_out, okey):
            j = l // 2
            S.barrier()
            al = Alloc(arena, PHASE_BASE)
            xs = al.get([16, 2048], BF16)
            og = al.get([8, 2048], BF16)
            acc2 = al.get([2, 2048], F32)
            qTb = [al.get([2048], BF16) for _ in range(2)]
            kTb = [al.get([2048], BF16) for _ in range(2)]
            vT = al.get([2048], BF16)
            vtokb = [al.get([16, 128], BF16) for _ in range(2)]
            pexp = [al.get([2, 2, 128], BF16) for _ in range(2)]
            PTb = [al.get([2, 2, 128], BF16) for _ in range(2)]
            Win = dil_w_in[j]
            Wout = dil_w_out[j]
            ps3bf = ps[3][:].bitcast(BF16)
            cnt = {"hg": 0, "blk": 0}
            load_xs(xs, xb_in, 0, 2048)
            for s in range(NSEQ):
                tokS = s * SEQ
                def make_iter(h, gi, dd, hg):
                    qT, kT, vtok = qTb[hg], kTb[hg], vtokb[hg]
                    qk_, kk_, vk_, vtk = K("q", hg), K("k", hg), "vT", K("vtok", hg)
                    nb = 16 // dd
                    wst = {}

                    def tsl(blk):
                        r, bb = blk // nb, blk % nb
                        return slice(r + dd * 128 * bb, r + dd * 128 * bb + dd * 127 + 1, dd)

                    def grp(t, nt):
                        if t == 0 and nt == 0:
                            for tt_ in range(3):
                                wst[tt_] = wload(Win, 0, 16, ((gi * 3 + tt_) * 8 + h) * 128, 128)
                        wv, wk = wst[t]
                        b, bk = mainbank(3)
                        for kc in range(16):
                            S.pe(lambda e, b=b, wv=wv, kc=kc, nt=nt: e.matmul(ps[b][:], wv[:, kc, :], xs[:, kc, nt * 512:(nt + 1) * 512], start=(kc == 0), stop=(kc == 15)),
                                 reads=[wk, f"xs{kc // 4}"], writes=[bk])
                        nsl = slice(nt * 512, (nt + 1) * 512)
                        if t == 0:
                            S.act(lambda e, b=b, nsl=nsl: e.activation(out=qT[:, nsl], in_=ps[b][:], func=AF.Identity, scale=float(128 ** -0.5)),
                                  reads=[bk], writes=[qk_])
                        elif t == 1:
                            S.dve(lambda e, b=b, nsl=nsl: e.tensor_copy(out=kT[:, nsl], in_=ps[b][:]), reads=[bk], writes=[kk_])
                        else:
                            S.act(lambda e, b=b, nsl=nsl: e.copy(out=vT[:, nsl], in_=ps[b][:]), reads=[bk], writes=[vk_])
                    groups = [(lambda t=t, nt=nt: grp(t, nt)) for t in range(3) for nt in range(4)]

                    def trans():
                        for half in range(2):
                            for bi in range(8):
                                vsl = tsl(half * 8 + bi)
                                S.pe(lambda e, bi=bi, vsl=vsl: e.transpose(ps3bf[:, bi * 128:(bi + 1) * 128], vT[:, vsl], ident), reads=[vk_, "cb16"], writes=["ps3"])
                            S.dve(lambda e, half=half: e.tensor_copy(out=vtok[:, half * 8:(half + 1) * 8, :], in_=ps3bf.rearrange("p (a b) -> p a b", b=128)),
                                  reads=["ps3"], writes=[vtk])

                    def att_scores(pr):
                        ci = cnt["blk"] % 2
                        cnt["blk"] += 1
                        sb = 4 + ci
                        psS = ps[sb][:].rearrange("p (k a b) -> p k a b", a=2, b=128)
                        infos = []
                        for kq in range(2):
                            blk = 2 * pr + kq
                            bb = blk % nb
                            cs = tsl(blk)
                            psl = tsl(blk - 1) if bb > 0 else cs
                            S.pe(lambda e, psS=psS, psl=psl, cs=cs, kq=kq: e.matmul(psS[:, kq, 0, :], kT[:, psl], qT[:, cs], start=True, stop=True),
                                 reads=[kk_, qk_], writes=[f"ps{sb}"])
                            S.pe(lambda e, psS=psS, cs=cs, kq=kq: e.matmul(psS[:, kq, 1, :], kT[:, cs], qT[:, cs], start=True, stop=True),
                                 reads=[kk_, qk_], writes=[f"ps{sb}"])
                            infos.append((blk, bb, cs))
                        return (ci, infos)

                    def att_mid(ctx):
                        ci, infos = ctx
                        sb, ob_ = 4 + ci, 6 + ci
                        psS = ps[sb][:].rearrange("p (k a b) -> p k a b", a=2, b=128)
                        pe_, PT = pexp[ci], PTb[ci]
                        S.act(lambda e, psS=psS, pe_=pe_: e.activation(out=pe_, in_=psS, func=AF.Exp), reads=[f"ps{sb}"], writes=[K("pexp", ci)])
                        for kq, (blk, bb, cs) in enumerate(infos):
                            msk = M3[:, 1:3, :] if bb > 0 else M3[:, 0:3:2, :]
                            S.dve(lambda e, pe_=pe_, PT=PT, kq=kq, msk=msk: e.tensor_tensor(out=PT[:, kq, :, :], in0=pe_[:, kq, :, :], in1=msk, op=ALU.mult),
                                  reads=[K("pexp", ci), "cm32"], writes=[K("PT", ci, kq)])

                    def att_rest(ctx):
                        ci, infos = ctx
                        sb, ob_ = 4 + ci, 6 + ci
                        psO = ps[ob_][:].rearrange("p (k a b) -> p k a b", a=2, b=128)
                        pe_, PT = pexp[ci], PTb[ci]
                        for kq, (blk, bb, cs) in enumerate(infos):
                            for jj in range(2):
                                bk_ = blk - 1 if (jj == 0 and bb > 0) else blk
                                S.pe(lambda e, psO=psO, PT=PT, jj=jj, bk_=bk_, kq=kq: e.matmul(psO[:, kq, 0, :], vtok[:, bk_, :], PT[:, kq, jj, :], start=(jj == 0), stop=(jj == 1)),
                                     reads=[vtk, K("PT", ci, kq)], writes=[f"ps{ob_}"])
                            for jj in range(2):
                                S.pe(lambda e, psO=psO, PT=PT, jj=jj, kq=kq: e.matmul(psO[:, kq, 1, :], ones1, PT[:, kq, jj, :], start=(jj == 0), stop=(jj == 1)),
                                     reads=["cb16", K("PT", ci, kq)], writes=[f"ps{ob_}"])
                        for kq, (blk, bb, cs) in enumerate(infos):
                            if gi == 0:
                                S.dve(lambda e, psO=psO, cs=cs, kq=kq: e.tensor_copy(out=acc2[:, :, cs], in_=psO[:, kq, :, :]), reads=[f"ps{ob_}"], writes=["acc2"])
                            else:
                                S.dve(lambda e, psO=psO, cs=cs, kq=kq: e.tensor_tensor(out=acc2[:, :, cs], in0=acc2[:, :, cs], in1=psO[:, kq, :, :], op=ALU.add),
                                      reads=[f"ps{ob_}", "acc2"], writes=["acc2"])
                    actx = deque()

                    def att_step(pr):
                        if pr < 8:
                            actx.append(att_scores(pr))
                            att_mid(actx[-1])
                        if pr >= 1:
                            att_rest(actx.popleft())
                    att = [(lambda pr=pr: att_step(pr)) for pr in range(9)]

                    def fin():
                        if gi == 2:
                            S.dve(lambda e: e.reciprocal(out=acc2[:, 1, :], in_=acc2[:, 1, :]), reads=["acc2"], writes=["acc2"])
                            S.dve(lambda e: e.tensor_tensor(out=og[:, h, :], in0=acc2[:, 0, :], in1=acc2[:, 1, :], op=ALU.mult), reads=["acc2"], writes=[K("og", h)])
                    return groups, trans, att, fin

                iters = [(h, gi, dd) for h in range(8) for gi, dd in enumerate((1, 4, 16))]
                prev = None
                for it in range(len(iters) + 1):
                    cur_it = None
                    if it < len(iters):
                        cur_it = make_iter(*iters[it], cnt["hg"] % 2)
                        cnt["hg"] += 1
                    pop_work(4)
                    glist = cur_it[0] if cur_it else []
                    alist = prev[2] if prev else []
                    for i in range(max(len(glist), len(alist))):
                        if i < len(glist):
                            glist[i]()
                        if i < len(alist):
                            alist[i]()
                    if prev:
                        prev[3]()
                    if cur_it:
                        cur_it[1]()
                    prev = cur_it
                if s + 1 < NSEQ:
                    load_xs(xs, xb_in, tokS + SEQ, 2048)
                for half in range(2):
                    outproj_ln(lambda kc, nt, half=half: (og[:, kc, half * 1024 + nt * 512:half * 1024 + (nt + 1) * 512], K("og", kc)),
                               8, Wout, tokS + half * 1024, 2, xres, x32_out, xb_out, (l * 2) * 16, okey)
            drain_work()

        cur = None
        for f in gla_conv_ops(0):
            f()
        for sidx in range(NSUB):
            l = sidx // 2
            if sidx == 2 and NSUB > 4:
                pending_work.extend(gla_conv_ops(1))
            if sidx == 3:
                while pending_work:
                    pending_work.popleft()()
            last = (sidx == NSUB - 1)
            xres = xT if sidx == 0 else X32
            x32_out = outT if last else X32
            okey = "out" if last else "x32"
            xb_in = xT if cur is None else XB[cur]
            nxt = 0 if cur is None else 1 - cur
            xb_out = XB[nxt]
            if sidx % 2 == 1:
                ffn_phase(l, xb_in, xres, x32_out, xb_out, okey)
            elif l % 2 == 0:
                gla_phase(l, xb_in, xres, x32_out, xb_out, okey)
            else:
                dil_phase(l, xb_in, xres, x32_out, xb_out, okey)
            cur = nxt
        S.barrier()
        S.op("sp", lambda e: e.nop())
        S.emit()
    return nc


def host_consts(inputs):
    f = np.float32
    cvec = np.zeros((128, NCV), f)
    cvec[:, C_LNG:C_LNG + 128] = np.asarray(inputs["ln_g"], f).reshape(8, 16, 128).transpose(2, 0, 1).reshape(128, 128)
    cvec[:, C_LNB:C_LNB + 128] = np.asarray(inputs["ln_b"], f).reshape(8, 16, 128).transpose(2, 0, 1).reshape(128, 128)
    cvec[:, C_CW:C_CW + 1032] = np.asarray(inputs["ffn_conv_w"], f).reshape(4, 3, 86, 128).transpose(3, 0, 1, 2).reshape(128, 1032)
    cvec[:, C_CB:C_CB + 344] = np.asarray(inputs["ffn_conv_b"], f).reshape(4, 86, 128).transpose(2, 0, 1).reshape(128, 344)
    cvec[:, C_NG:C_NG + 8] = np.asarray(inputs["gla_norm_g"], f).reshape(2, 4, 128).transpose(2, 0, 1).reshape(128, 8)
    cmat = np.zeros((128, 1024), f)
    ii = np.arange(128)
    le = (ii[:, None] <= ii[None, :]).astype(f)
    ge = (ii[:, None] >= ii[None, :]).astype(f)
    cmat[:, 0:128] = le * f(-1.0 / 16.0)
    cmat[:, 256:384] = ge
    cmat[:, 384:512] = le
    cmat[:, 512:640] = np.eye(128, dtype=f)
    cmat[:, 640:768] = f(1.0 / 2048.0)
    cmat[:, 768:896] = f(1.0 / 512.0)
    cmat[:, 896:1024] = f(1.0)
    wg = np.concatenate([np.asarray(inputs["gla_w_gate_up"], f), np.asarray(inputs["gla_gate_bias"], f)[:, None, :]], axis=1)
    return cvec, cmat, np.ascontiguousarray(wg)


_PROG = {}


def run(inputs, ncores=NCORES, nseq=NSEQ_CORE, nsub=8, trace=False):
    key = (nseq, nsub)
    if key not in _PROG:
        _PROG[key] = build(nseq, nsub)
    nc = _PROG[key]
    cvec, cmat, wg = host_consts(inputs)
    x = np.asarray(inputs["x"], np.float32)
    shared = {k: np.ascontiguousarray(np.asarray(inputs[k], np.float32)) for k in
              ("gla_w_in", "gla_w_out", "dil_w_in", "dil_w_out", "ffn_w_up", "ffn_w_down")}
    in_maps = []
    for c in range(ncores):
        xc = x[c * nseq:(c + 1) * nseq].reshape(nseq * SEQ, D)
        m = dict(shared)
        m.update(xT=np.ascontiguousarray(xc.T), cvec=cvec, cmat=cmat, wg_aug=wg)
        in_maps.append(m)
    res = run_bass_kernel_spmd(nc, in_maps, core_ids=list(range(ncores)), **({"trace": True} if trace else {}))
    out = np.empty((ncores * nseq, SEQ, D), np.float32)
    for c in range(ncores):
        oT = np.asarray(res.results[c]["outT"])
        out[c * nseq:(c + 1) * nseq] = oT.T.reshape(nseq, SEQ, D)
    return out, res


def kernel(**inputs):
    out, _ = run(inputs)
    return out
```

```python
import contextlib
from collections import deque
import numpy as np
import concourse.bass as bass
import concourse.mybir as mybir
from concourse.bass_utils import run_bass_kernel_spmd

F32 = mybir.dt.float32
BF16 = mybir.dt.bfloat16
AF = mybir.ActivationFunctionType
ALU = mybir.AluOpType

D = 2048
SEQ = 2048
DFF = 5504
NFC = 43
ALPHA = float(8 ** 0.25)
NCORES = 8
NSEQ_CORE = 2

ENGS = ["pe", "act", "dve", "pool", "sp"]
NDMASEM = 16
EPOCH = 30000
NWB = 6
ARENA_BYTES = 212800

NCV = 1640
C_LNG, C_LNB, C_CW, C_CB, C_NG = 0, 128, 256, 1288, 1632


def K(*a):
    return "_".join(map(str, a))


class Sched:
    def __init__(self, nc):
        self.nc = nc
        self.ops = []
        self.last_w = {}
        self.readers = {}
        self.last_comp = {}
        self.recent_dma = {e: deque(maxlen=NDMASEM) for e in ENGS}
        self.bar_deps = set()
        self.bar_pending = set()

    def barrier(self):
        deps = set()
        for e in ENGS:
            if e in self.last_comp:
                deps.add(self.last_comp[e])
            deps.update(self.recent_dma[e])
        self.bar_deps = deps
        self.bar_pending = set(ENGS)
        self.last_w = {}
        self.readers = {}

    def op(self, eng, fn, reads=(), writes=(), dma=False):
        i = len(self.ops)
        deps = set()
        for k in reads:
            w = self.last_w.get(k)
            if w is not None:
                deps.add(w)
        for k in writes:
            w = self.last_w.get(k)
            if w is not None:
                deps.add(w)
            rs = self.readers.get(k)
            if rs:
                deps.update(rs)
        for k in reads:
            self.readers.setdefault(k, []).append(i)
        for k in writes:
            self.last_w[k] = i
            self.readers[k] = []
        if eng in self.bar_pending:
            deps |= self.bar_deps
            self.bar_pending.discard(eng)
        deps.discard(i)
        self.ops.append((eng, fn, deps, dma))
        if dma:
            self.recent_dma[eng].append(i)
        else:
            self.last_comp[eng] = i
        return i

    def pe(self, fn, reads=(), writes=()):
        return self.op("pe", fn, reads, writes)

    def act(self, fn, reads=(), writes=()):
        return self.op("act", fn, reads, writes)

    def dve(self, fn, reads=(), writes=()):
        return self.op("dve", fn, reads, writes)

    def dma(self, q, fn, reads=(), writes=()):
        return self.op(q, fn, reads, writes, dma=True)

    def emit(self):
        nc = self.nc
        ops = self.ops
        n = len(ops)
        need_inc = [False] * n
        for i, (eng, fn, deps, dma) in enumerate(ops):
            for d in deps:
                deng, _, _, ddma = ops[d]
                if ddma:
                    continue
                if deng == "pe" and eng == "pe" and not dma:
                    continue
                need_inc[d] = True
        cnt = {e: 0 for e in ENGS}
        dcnt = {e: 0 for e in ENGS}
        sig = [None] * n
        for i, (eng, fn, deps, dma) in enumerate(ops):
            if dma:
                k = dcnt[eng]
                dcnt[eng] += 1
                sig[i] = (("d", eng, k % NDMASEM), 16 * (k // NDMASEM + 1))
            elif need_inc[i]:
                c = cnt[eng]
                cnt[eng] += 1
                sig[i] = (("c", eng, c // EPOCH), (c % EPOCH) + 1)
        semkeys = sorted({s[0] for s in sig if s is not None})
        with contextlib.ExitStack() as st:
            sems = {}
            for k in semkeys:
                sems[k] = st.enter_context(nc.semaphore("s_" + "_".join(map(str, k))))
            block = st.enter_context(nc.Block())

            def run_engine(ename, e):
                waited = {}
                for i, (eng, fn, deps, dma) in enumerate(ops):
                    if eng != ename:
                        continue
                    wl = {}
                    for d in deps:
                        deng, _, _, ddma = ops[d]
                        if (not ddma) and deng == "pe" and eng == "pe" and not dma:
                            continue
                        sk, v = sig[d]
                        if waited.get(sk, 0) >= v:
                            continue
                        if wl.get(sk, 0) < v:
                            wl[sk] = v
                    if dma:
                        sk, v = sig[i]
                        if v > 16 and waited.get(sk, 0) < v - 16 and wl.get(sk, 0) < v - 16:
                            wl[sk] = v - 16
                    for sk, v in wl.items():
                        e.wait_ge(sems[sk], v)
                        waited[sk] = v
                    ins = fn(e)
                    if sig[i] is not None:
                        ins.then_inc(sems[sig[i][0]], 16 if dma else 1)

            @block.tensor
            def _(e):
                run_engine("pe", e)

            @block.scalar
            def _(e):
                run_engine("act", e)

            @block.vector
            def _(e):
                run_engine("dve", e)

            @block.gpsimd
            def _(e):
                run_engine("pool", e)

            @block.sync
            def _(e):
                run_engine("sp", e)


class Alloc:
    def __init__(self, arena, base=0):
        self.a = arena
        self.off = base

    def get(self, shape, dt, parts=128):
        n = int(np.prod(shape))
        nb = n * (4 if dt == F32 else 2)
        start = self.off
        self.off += (nb + 63) // 64 * 64
        assert self.off <= ARENA_BYTES, ("arena overflow", self.off)
        ap = self.a[0:parts, start // 2:(start + nb) // 2]
        if dt == F32:
            ap = ap.bitcast(F32)
        if len(shape) == 2:
            ap = ap.rearrange("p (a b) -> p a b", b=shape[1])
        elif len(shape) == 3:
            ap = ap.rearrange("p (a b c) -> p a b c", b=shape[1], c=shape[2])
        return ap


def build(NSEQ=NSEQ_CORE, NSUB=8):
    NT = NSEQ * SEQ
    nc = bass.Bass("TRN2", target_bir_lowering=False)

    def din(name, shape):
        return nc.dram_tensor(name, list(shape), F32, kind="ExternalInput").ap()

    xT = din("xT", [D, NT])
    gla_w_in = din("gla_w_in", [2, D, 6160])
    gla_w_out = din("gla_w_out", [2, D, D])
    dil_w_in = din("dil_w_in", [2, D, 9216])
    dil_w_out = din("dil_w_out", [2, 1024, D])
    ffn_w_up = din("ffn_w_up", [4, D, 11008])
    ffn_w_down = din("ffn_w_down", [4, DFF, D])
    cvec_d = din("cvec", [128, NCV])
    cmat_d = din("cmat", [128, 1024])
    wg_d = din("wg_aug", [2, 17, 1024])
    outT = nc.dram_tensor("outT", [D, NT], F32, kind="ExternalOutput").ap()
    X32 = nc.dram_tensor("X32", [D, NT], F32).ap()
    XB = [nc.dram_tensor(f"XB{i}", [D, NT], BF16).ap() for i in range(2)]
    WS = nc.dram_tensor("WS", [2, 64, 128, 2048], BF16).ap()

    with contextlib.ExitStack() as st:
        arena = st.enter_context(nc.sbuf_tensor("arena", [128, ARENA_BYTES // 2], BF16))
        ps = [st.enter_context(nc.psum_tensor(f"ps{i}", [128, 512], F32)) for i in range(8)]
        S = Sched(nc)

        cm = Alloc(arena, 0)
        cvec = cm.get([NCV], F32)
        cm32 = cm.get([4, 128], F32)
        tri = cm32[:, 0, :]
        M3 = cm32[:, 1:4, :]
        mcur = cm32[:, 3, :]
        cb16 = cm.get([4, 128], BF16)
        ident, onesD, ones512, ones1 = cb16[:, 0, :], cb16[:, 1, :], cb16[:, 2, :], cb16[:, 3, :]
        wpool = [cm.get([2048], BF16) for _ in range(NWB)]
        XR = [cm.get([512], F32) for _ in range(3)]
        ybf = cm.get([512], BF16)
        ysq = cm.get([512], BF16)
        mean_t = cm.get([512], F32)
        tmp_t = cm.get([512], F32)
        rstd_t = [cm.get([512], F32) for _ in range(2)]
        nmr_t = [cm.get([512], F32) for _ in range(2)]
        YL = [cm.get([512], F32) for _ in range(4)]
        OBF = [cm.get([512], BF16) for _ in range(2)]
        PHASE_BASE = cm.off

        S.dma("sp", lambda e: e.dma_start(out=cvec, in_=cvec_d), writes=["cvec"])
        S.dma("sp", lambda e: e.dma_start(out=cm32, in_=cmat_d[:, 0:512].rearrange("p (a b) -> p a b", b=128)), writes=["cm32"])
        S.dma("pool", lambda e: e.dma_start(out=cb16, in_=cmat_d[:, 512:1024].rearrange("p (a b) -> p a b", b=128)), writes=["cb16"])

        ctr = {"wb": 0, "bank": 0, "xr": 0, "yl": 0, "obf": 0, "conv": 0}

        def wload(Wd, k0, nk, col0, ncols):
            i = ctr["wb"] % NWB
            ctr["wb"] += 1
            view = wpool[i][:, 0:nk * ncols].rearrange("p (a b) -> p a b", b=ncols)
            src = Wd[k0 * 128:(k0 + nk) * 128, col0:col0 + ncols].rearrange("(a p) m -> p a m", p=128)
            key = f"wb{i}"
            S.dma("pool", lambda e: e.dma_start(out=view, in_=src), writes=[key])
            return view, key

        def ws_src(j, cid):
            if cid < 48:
                col0 = (cid * 128 if cid < 8 else 1024 + (cid - 8) * 128 if cid < 16 else 4096 + (cid - 16) * 128 if cid < 32 else 2048 + (cid - 32) * 128)
                return gla_w_in[j], col0
            return gla_w_out[j], (cid - 48) * 128

        def gla_conv_ops(j):
            lst = []
            for cid in range(64):
                def f(cid=cid):
                    ctr["conv"] -= 1
                    Wd, col0 = ws_src(j, cid)
                    view, key = wload(Wd, 0, 16, col0, 128)
                    dst = WS[j, cid].rearrange("p (a b) -> p a b", b=128)
                    S.dma("sp", lambda e: e.dma_start(out=dst, in_=view), reads=[key], writes=[K("ws", j, cid)])
                lst.append(f)
            ctr["conv"] += len(lst)
            return lst

        def wload_ws(j, cid):
            i = ctr["wb"] % NWB
            ctr["wb"] += 1
            view = wpool[i].rearrange("p (a b) -> p a b", b=128)
            src = WS[j, cid].rearrange("p (a b) -> p a b", b=128)
            key = f"wb{i}"
            S.dma("pool", lambda e: e.dma_start(out=view, in_=src), reads=[K("ws", j, cid)], writes=[key])
            return view, key

        def pk(b):
            return [f"ps{b}", f"ps{b}_0", f"ps{b}_1"]

        pending_work = deque()
        active_ranges = []

        def pop_work(n=1):
            for _ in range(n):
                if pending_work:
                    pending_work.popleft()()

        def drain_work():
            while pending_work:
                pending_work.popleft()()

        def mainbank(nb=4):
            b = ctr["bank"] % nb
            ctr["bank"] += 1
            return b, f"ps{b}"

        def load_xs(xs, src, tok0, T):
            if any(a < tok0 + T and tok0 < b for (a, b) in active_ranges):
                drain_work()
            q = "pool" if src.dtype == F32 else "sp"
            for g in range(4):
                v = src[g * 512:(g + 1) * 512, tok0:tok0 + T].rearrange("(a p) t -> p a t", p=128)
                rk = [K("xb", c, t) for c in range(4 * g, 4 * g + 4) for t in range(tok0, tok0 + T, 512)]
                S.dma(q, lambda e, g=g, v=v: e.dma_start(out=xs[:, 4 * g:4 * g + 4, :], in_=v), reads=rk, writes=[f"xs{g}"])

        def outproj_ln(src_fn, KC, Wd, tok0, NTL, xres, x32_out, xb_out, lncol, okey, wfn=None, defer=True):
            pend = [None]
            groups = [(oc, nt) for oc in range(16) for nt in range(NTL)]
            wts = {}
            xrs = {}

            def load_w(oc):
                if wfn is not None:
                    wv, wk = wfn(oc)
                    wts[oc] = [(0, KC, wv, wk)]
                    return
                pieces = []
                for p0 in range(0, KC, 16):
                    n = min(16, KC - p0)
                    pieces.append((p0, n) + wload(Wd, p0, n, oc * 128, 128))
                wts[oc] = pieces

            def load_xr(g):
                oc, nt = groups[g]
                t0 = tok0 + nt * 512
                xi = ctr["xr"] % 3
                ctr["xr"] += 1
                xr, xk = XR[xi], f"xr{xi}"
                xsrc = xres[oc * 128:(oc + 1) * 128, t0:t0 + 512]
                S.dma("sp", lambda e, xr=xr, xsrc=xsrc: e.dma_start(out=xr, in_=xsrc), reads=[K("x32", oc, t0)], writes=[xk])
                xrs[g] = (xr, xk)

            load_w(0)
            for g in range(min(2, len(groups))):
                load_xr(g)
            for gi, (oc, nt) in enumerate(groups):
                if True:
                    pop_work(3)
                    if nt == 0 and oc + 1 < 16:
                        load_w(oc + 1)
                    if gi + 2 < len(groups):
                        load_xr(gi + 2)
                    pieces = wts[oc]
                    t0 = tok0 + nt * 512
                    b, bk = mainbank()
                    xr, xk = xrs.pop(gi)
                    for (p0, n, wv, wk) in pieces:
                        for kk in range(n):
                            kc = p0 + kk
                            rhs, rk = src_fn(kc, nt)
                            S.pe(lambda e, b=b, wv=wv, kk=kk, rhs=rhs, kc=kc: e.matmul(ps[b][:], wv[:, kk, :], rhs, start=(kc == 0), stop=(kc == KC - 1)),
                                 reads=[wk, rk], writes=[bk])
                    if pend[0]:
                        pend[0]()
                        pend[0] = None
                    S.dve(lambda e, xr=xr, b=b: e.scalar_tensor_tensor(out=xr, in0=xr, scalar=ALPHA, in1=ps[b][:], op0=ALU.mult, op1=ALU.add),
                          reads=[xk, bk], writes=[xk])
                    S.act(lambda e, xr=xr: e.activation(out=ybf, in_=xr, func=AF.Copy), reads=[xk], writes=["ybf"])
                    S.act(lambda e, xr=xr: e.activation(out=ysq, in_=xr, func=AF.Square), reads=[xk], writes=["ysq"])
                    ydst = X32[oc * 128:(oc + 1) * 128, t0:t0 + 512]
                    S.dma("act", lambda e, xr=xr, ydst=ydst: e.dma_start(out=ydst, in_=xr), reads=[xk], writes=[K("x32", oc, t0)])

                    def mk(oc=oc, nt=nt):
                        pm, pq = 4 + 2 * nt, 5 + 2 * nt
                        S.pe(lambda e: e.matmul(ps[pm][:], onesD, ybf, start=(oc == 0), stop=(oc == 15)), reads=["ybf", "cb16"], writes=pk(pm))
                        S.pe(lambda e: e.matmul(ps[pq][:], onesD, ysq, start=(oc == 0), stop=(oc == 15)), reads=["ysq", "cb16"], writes=pk(pq))
                    pend[0] = mk
            pend[0]()
            drain_work()
            units = []
            for nt in range(NTL):
                t0 = tok0 + nt * 512
                pm, pq = 4 + 2 * nt, 5 + 2 * nt
                rstd, nmr = rstd_t[nt], nmr_t[nt]
                S.act(lambda e, pm=pm: e.activation(out=mean_t, in_=ps[pm][:], func=AF.Copy), reads=pk(pm), writes=["mean"])
                S.dve(lambda e: e.tensor_tensor(out=tmp_t, in0=mean_t, in1=mean_t, op=ALU.mult), reads=["mean"], writes=["tmp"])
                S.dve(lambda e, pq=pq: e.tensor_tensor(out=tmp_t, in0=ps[pq][:], in1=tmp_t, op=ALU.subtract), reads=pk(pq) + ["tmp"], writes=["tmp"])
                S.act(lambda e: e.activation(out=tmp_t, in_=tmp_t, func=AF.Sqrt, bias=1e-5, scale=1.0), reads=["tmp"], writes=["tmp"])
                S.dve(lambda e, rstd=rstd: e.reciprocal(out=rstd, in_=tmp_t), reads=["tmp"], writes=[f"rstd{nt}"])
                S.dve(lambda e, rstd=rstd, nmr=nmr: e.scalar_tensor_tensor(out=nmr, in0=mean_t, scalar=-1.0, in1=rstd, op0=ALU.mult, op1=ALU.mult),
                      reads=["mean", f"rstd{nt}"], writes=[f"nmr{nt}"])
                for oc in range(16):
                    def ld(oc=oc, nt=nt, t0=t0):
                        yi = ctr["yl"] % 4
                        ctr["yl"] += 1
                        yl, yk = YL[yi], f"yl{yi}"
                        ysrc = X32[oc * 128:(oc + 1) * 128, t0:t0 + 512]
                        S.dma("sp", lambda e, yl=yl, ysrc=ysrc: e.dma_start(out=yl, in_=ysrc), reads=[K("x32", oc, t0)], writes=[yk])
                        return yl, yk

                    def cpd(yl, yk, oc=oc, nt=nt, t0=t0, rstd=rstd, nmr=nmr):
                        S.dve(lambda e, yl=yl, rstd=rstd: e.tensor_tensor(out=yl, in0=yl, in1=rstd, op=ALU.mult), reads=[yk, f"rstd{nt}"], writes=[yk])
                        S.dve(lambda e, yl=yl, nmr=nmr: e.tensor_tensor(out=yl, in0=yl, in1=nmr, op=ALU.add), reads=[yk, f"nmr{nt}"], writes=[yk])

                    def cp(yl, yk, oc=oc, nt=nt, t0=t0, rstd=rstd, nmr=nmr):
                        oi = ctr["obf"] % 2
                        ctr["obf"] += 1
                        ob, obk = OBF[oi], f"obf{oi}"
                        gcol = cvec[:, C_LNG + lncol + oc:C_LNG + lncol + oc + 1]
                        bcol = cvec[:, C_LNB + lncol + oc:C_LNB + lncol + oc + 1]
                        S.act(lambda e, yl=yl, ob=ob, gcol=gcol, bcol=bcol: e.activation(out=ob, in_=yl, func=AF.Identity, bias=bcol, scale=gcol),
                              reads=[yk, "cvec"], writes=[obk])
                        S.act(lambda e, yl=yl, gcol=gcol, bcol=bcol: e.activation(out=yl, in_=yl, func=AF.Identity, bias=bcol, scale=gcol),
                              reads=[yk, "cvec"], writes=[yk])
                        odst = x32_out[oc * 128:(oc + 1) * 128, t0:t0 + 512]
                        S.dma("act", lambda e, yl=yl, odst=odst: e.dma_start(out=odst, in_=yl), reads=[yk], writes=[K(okey, oc, t0)])
                        bdst = xb_out[oc * 128:(oc + 1) * 128, t0:t0 + 512]
                        S.dma("act", lambda e, ob=ob, bdst=bdst: e.dma_start(out=bdst, in_=ob), reads=[obk], writes=[K("xb", oc, t0)])
                    units.append((ld, cp, cpd))
            nun = len(units)
            st_ = {}
            rng_ = (tok0, tok0 + NTL * 512)
            active_ranges.append(rng_)
            for i in range(nun + 3):
                def item(i=i):
                    if i == nun + 2:
                        active_ranges.remove(rng_)
                    if 0 <= i - 3 < nun:
                        units[i - 3][1](*st_.pop(i - 3))
                    if 0 <= i - 2 < nun:
                        units[i - 2][2](*st_[i - 2])
                    if i < nun:
                        st_[i] = units[i][0]()
                if defer:
                    pending_work.append(item)
                else:
                    item()

        def ffn_phase(l, xb_in, xres, x32_out, xb_out, okey):
            S.barrier()
            al = Alloc(arena, PHASE_BASE)
            xs = al.get([16, 1024], BF16)
            actT = al.get([NFC, 1024], BF16)
            hraw = [[al.get([514], F32) for _ in range(2)] for _ in range(2)]
            tt = [[al.get([512], F32) for _ in range(2)] for _ in range(2)]
            sg = [al.get([512], F32) for _ in range(2)]
            halo = al.get([86, 2], F32)
            Wup = ffn_w_up[l]
            Wdn = ffn_w_down[l]
            load_xs(xs, xb_in, 0, 1024)
            for tile in range(NT // 1024):
                tok0 = tile * 1024
                seq_start = (tok0 % SEQ == 0)
                for j in range(NFC):
                    for half in range(2):
                        c = j + NFC * half
                        if half == 0:
                            pop_work(1)
                        wv, wk = wload(Wup, 0, 16, c * 128, 128)
                        cw = [cvec[:, C_CW + (l * 3 + tap) * 86 + c:C_CW + (l * 3 + tap) * 86 + c + 1] for tap in range(3)]
                        cbc = cvec[:, C_CB + l * 86 + c:C_CB + l * 86 + c + 1]
                        for nt in range(2):
                            b, bk = mainbank()
                            for kc in range(16):
                                S.pe(lambda e, b=b, wv=wv, kc=kc, nt=nt: e.matmul(ps[b][:], wv[:, kc, :], xs[:, kc, nt * 512:(nt + 1) * 512],
                                                                                  start=(kc == 0), stop=(kc == 15)),
                                     reads=[wk, f"xs{kc // 4}"], writes=[bk])
                            hr, hk = hraw[half][nt], K("hr", half, nt)
                            t_, tk = tt[half][nt], K("tt", half, nt)
                            if nt == 0 and seq_start:
                                S.act(lambda e, hr=hr: e.memzero(hr[:, 0:2]), writes=[hk])
                            else:
                                S.act(lambda e, hr=hr, c=c: e.copy(out=hr[:, 0:2], in_=halo[:, c, :]), reads=[K("halo", c)], writes=[hk])
                            S.act(lambda e, hr=hr, b=b: e.copy(out=hr[:, 2:514], in_=ps[b][:]), reads=[bk], writes=[hk])
                            S.act(lambda e, hr=hr, c=c: e.copy(out=halo[:, c, :], in_=hr[:, 512:514]), reads=[hk], writes=[K("halo", c)])
                            S.act(lambda e, t_=t_, b=b, cw=cw, cbc=cbc: e.activation(out=t_, in_=ps[b][:], func=AF.Identity, bias=cbc, scale=cw[2]),
                                  reads=[bk, "cvec"], writes=[tk])
                            S.dve(lambda e, t_=t_, hr=hr, cw=cw: e.scalar_tensor_tensor(out=t_, in0=hr[:, 1:513], scalar=cw[1], in1=t_, op0=ALU.mult, op1=ALU.add),
                                  reads=[hk, tk, "cvec"], writes=[tk])
                            S.dve(lambda e, t_=t_, hr=hr, cw=cw: e.scalar_tensor_tensor(out=t_, in0=hr[:, 0:512], scalar=cw[0], in1=t_, op0=ALU.mult, op1=ALU.add),
                                  reads=[hk, tk, "cvec"], writes=[tk])
                            if half == 0:
                                S.act(lambda e, t_=t_, nt=nt: e.activation(out=sg[nt], in_=t_, func=AF.Silu), reads=[tk], writes=[K("sg", nt)])
                            else:
                                S.dve(lambda e, t_=t_, nt=nt, j=j: e.tensor_tensor(out=actT[:, j, nt * 512:(nt + 1) * 512], in0=t_, in1=sg[nt], op=ALU.mult),
                                      reads=[tk, K("sg", nt)], writes=[K("act", j, nt)])
                if tok0 + 1024 < NT:
                    load_xs(xs, xb_in, tok0 + 1024, 1024)
                outproj_ln(lambda kc, nt: (actT[:, kc, nt * 512:(nt + 1) * 512], K("act", kc, nt)),
                           NFC, Wdn, tok0, 2, xres, x32_out, xb_out, (l * 2 + 1) * 16, okey)

        def gla_phase(l, xb_in, xres, x32_out, xb_out, okey):
            j = l // 2
            if ctr["conv"] > 0:
                drain_work()
            S.barrier()
            al = Alloc(arena, PHASE_BASE)
            xs = al.get([16, 512], BF16)
            og = xs
            qT = al.get([8, 512], BF16)
            kT = al.get([8, 512], BF16)
            vsb = al.get([4, 2048], BF16)
            rs = al.get([16, 512], BF16)
            glow = al.get([512], F32, parts=17)
            wg = al.get([1024], F32, parts=17)
            S32 = al.get([8, 512], F32)
            Sbf = al.get([8, 512], BF16)
            u = al.get([1024], F32)
            EB = al.get([8, 128], F32)
            ENB = al.get([8, 128], F32)
            qs2 = [al.get([8, 128], BF16) for _ in range(2)]
            ks = al.get([8, 128], BF16)
            kh = al.get([8, 128], BF16)
            khat2 = [al.get([1024], BF16) for _ in range(2)]
            PT2 = [al.get([4, 128], BF16) for _ in range(2)]
            EL2 = [al.get([8], F32) for _ in range(2)]
            sq = al.get([16, 128], BF16)
            rsq = al.get([4, 128], F32)
            t1 = al.get([16, 128], F32)
            Win = gla_w_in[j]
            Wout = gla_w_out[j]
            S.dma("sp", lambda e: e.dma_start(out=wg, in_=wg_d[j]), writes=["wg"])
            S.dve(lambda e: e.memset(glow, 1.0), writes=["glow"])
            ps5bf = ps[5][:].bitcast(BF16)
            XS_KEYS = [f"xs{g}" for g in range(4)]
            for tile in range(NT // 512):
                tok0 = tile * 512
                seq_start = (tok0 % SEQ == 0)
                load_xs(xs, xb_in, tok0, 512)
                if seq_start:
                    S.dve(lambda e: e.memset(S32, 0.0), writes=[K("S32", h) for h in range(8)])
                    S.dve(lambda e: e.memset(Sbf, 0.0), writes=[K("Sbf", h) for h in range(8)])

                def fm_chunk(cid, evac):
                    pop_work(1)
                    wv, wk = wload_ws(j, cid)
                    b, bk = mainbank()
                    for kc in range(16):
                        S.pe(lambda e, b=b, wv=wv, kc=kc: e.matmul(ps[b][:], wv[:, kc, :], xs[:, kc, :], start=(kc == 0), stop=(kc == 15)),
                             reads=[wk, f"xs{kc // 4}"], writes=[bk])
                    evac(b, bk)
                for c in range(8):
                    fm_chunk(c, lambda b, bk, c=c: S.act(lambda e: e.activation(out=qT[:, c, :], in_=ps[b][:], func=AF.Identity, scale=0.0625),
                                                                reads=[bk], writes=[K("qT", c)]))
                for c in range(8):
                    fm_chunk(8 + c, lambda b, bk, c=c: S.dve(lambda e: e.tensor_copy(out=kT[:, c, :], in_=ps[b][:]),
                                                                       reads=[bk], writes=[K("kT", c)]))
                for c in range(16):
                    fm_chunk(16 + c, lambda b, bk, c=c: S.act(lambda e: e.activation(out=rs[:, c, :], in_=ps[b][:], func=AF.Silu),
                                                                       reads=[bk], writes=[K("rs", c)]))
                wv, wk = wload(Win, 0, 16, 6144, 16)
                b, bk = mainbank()
                for kc in range(16):
                    S.pe(lambda e, b=b, wv=wv, kc=kc: e.matmul(ps[b][0:16, :], wv[:, kc, :], xs[:, kc, :], start=(kc == 0), stop=(kc == 15)),
                         reads=[wk, f"xs{kc // 4}"], writes=[bk])
                S.act(lambda e, b=b: e.copy(out=glow[0:16, :], in_=ps[b][0:16, :]), reads=[bk], writes=["glow"])
                for cg in range(16):
                    wv, wk = wload_ws(j, 32 + cg)
                    b, bk = mainbank()
                    for tc in range(4):
                        for kc in range(16):
                            S.pe(lambda e, b=b, wv=wv, kc=kc, tc=tc: e.matmul(ps[b][:, tc * 128:(tc + 1) * 128], xs[:, kc, tc * 128:(tc + 1) * 128], wv[:, kc, :],
                                                                              start=(kc == 0), stop=(kc == 15)),
                                 reads=[wk, f"xs{kc // 4}"], writes=[bk])
                    S.act(lambda e, b=b, cg=cg: e.copy(out=vsb[:, :, cg * 128:(cg + 1) * 128], in_=ps[b][:].rearrange("p (a b) -> p a b", b=128)),
                          reads=[bk], writes=[K("v", cg)])
                VK = [K("v", cg) for cg in range(16)]
                def G(tc):
                    bsel = tc % 2
                    tsl = slice(tc * 128, (tc + 1) * 128)
                    qs, PT, khat, EL = qs2[bsel], PT2[bsel], khat2[bsel], EL2[bsel]
                    for hf in range(2):
                        zb = 6 + hf
                        S.pe(lambda e, hf=hf, zb=zb: e.matmul(ps[zb][:], glow[0:17, tsl], wg[0:17, hf * 512:(hf + 1) * 512], start=True, stop=True),
                             reads=["glow", "wg"], writes=[f"ps{zb}"])
                        S.act(lambda e, hf=hf, zb=zb: e.activation(out=u[:, hf * 512:(hf + 1) * 512], in_=ps[zb][:], func=AF.Exp, scale=-1.0),
                              reads=[f"ps{zb}"], writes=[K("u", hf)])
                        S.act(lambda e, hf=hf: e.activation(out=u[:, hf * 512:(hf + 1) * 512], in_=u[:, hf * 512:(hf + 1) * 512], func=AF.Ln, bias=1.0, scale=1.0),
                              reads=[K("u", hf)], writes=[K("u", hf)])
                    for dc in range(8):
                        pb = 6 + dc // 4
                        S.pe(lambda e, dc=dc, pb=pb: e.matmul(ps[pb][:, (dc % 4) * 128:(dc % 4 + 1) * 128], u[:, dc * 128:(dc + 1) * 128], tri, start=True, stop=True),
                             reads=[K("u", dc // 4), "cm32"], writes=[f"ps{pb}"])
                    for hf in range(2):
                        pv = ps[6 + hf][:].rearrange("p (a b) -> p a b", b=128)
                        S.act(lambda e, hf=hf, pv=pv: e.activation(out=EB[:, hf * 4:(hf + 1) * 4, :], in_=pv, func=AF.Exp), reads=[f"ps{6 + hf}"], writes=[K("EB", hf)])
                        S.act(lambda e, hf=hf, pv=pv: e.activation(out=ENB[:, hf * 4:(hf + 1) * 4, :], in_=pv, func=AF.Exp, scale=-1.0), reads=[f"ps{6 + hf}"], writes=[K("ENB", hf)])
                    EBK = [K("EB", 0), K("EB", 1)]
                    ENBK = [K("ENB", 0), K("ENB", 1)]
                    S.act(lambda e, EL=EL: e.copy(out=EL, in_=EB[:, :, 127]), reads=EBK, writes=[K("EL", bsel)])
                    S.dve(lambda e, qs=qs: e.tensor_tensor(out=qs, in0=qT[:, :, tsl], in1=EB, op=ALU.mult), reads=[K("qT", c) for c in range(8)] + EBK, writes=[K("qs", bsel)])
                    S.dve(lambda e: e.tensor_tensor(out=ks, in0=kT[:, :, tsl], in1=ENB, op=ALU.mult), reads=[K("kT", c) for c in range(8)] + ENBK, writes=["ks"])
                    S.dve(lambda e: e.tensor_tensor(out=kh, in0=ks, in1=EB[:, :, 127:128].broadcast_to([128, 8, 128]), op=ALU.mult), reads=["ks"] + EBK, writes=["kh"])
                    for h in range(4):
                        for dl in range(2):
                            S.pe(lambda e, h=h, dl=dl, qs=qs: e.matmul(ps[4][:, h * 128:(h + 1) * 128], ks[:, 2 * h + dl, :], qs[:, 2 * h + dl, :], start=(dl == 0), stop=(dl == 1)),
                                 reads=["ks", K("qs", bsel)], writes=["ps4"])
                    S.dve(lambda e, PT=PT: e.tensor_tensor(out=PT, in0=ps[4][:].rearrange("p (a b) -> p a b", b=128),
                                                           in1=mcur.unsqueeze(1).broadcast_to([128, 4, 128]), op=ALU.mult), reads=["ps4", "cm32"], writes=[K("PT", bsel)])
                    for dc in range(8):
                        S.pe(lambda e, dc=dc: e.transpose(ps5bf[:, dc * 128:(dc + 1) * 128], kh[:, dc, :], ident), reads=["kh", "cb16"], writes=["ps5"])
                    S.act(lambda e, khat=khat: e.copy(out=khat, in_=ps5bf), reads=["ps5"], writes=[K("khat", bsel)])

                def B(tc):
                    bsel = tc % 2
                    tsl = slice(tc * 128, (tc + 1) * 128)
                    qs, PT, khat, EL = qs2[bsel], PT2[bsel], khat2[bsel], EL2[bsel]
                    for h in range(4):
                        for ec in range(4):
                            o_ = ps[h][:, ec * 128:(ec + 1) * 128]
                            S.pe(lambda e, h=h, ec=ec, o_=o_, PT=PT: e.matmul(o_, vsb[:, tc, h * 512 + ec * 128:h * 512 + (ec + 1) * 128], PT[:, h, :], start=True, stop=False),
                                 reads=VK + [K("PT", bsel)], writes=[f"ps{h}"])
                            for dl in range(2):
                                S.pe(lambda e, h=h, ec=ec, o_=o_, dl=dl, qs=qs: e.matmul(o_, Sbf[:, 2 * h + dl, ec * 128:(ec + 1) * 128], qs[:, 2 * h + dl, :], start=False, stop=(dl == 1)),
                                     reads=[K("Sbf", 2 * h + dl), K("qs", bsel)], writes=[f"ps{h}"])
                    for hd in range(8):
                        h = hd // 2
                        pb = (5, 4)[hd % 2]
                        S.pe(lambda e, hd=hd, h=h, pb=pb, khat=khat: e.matmul(ps[pb][:], khat[:, hd * 128:(hd + 1) * 128], vsb[:, tc, h * 512:(h + 1) * 512], start=True, stop=True),
                             reads=[K("khat", bsel)] + VK, writes=[f"ps{pb}"])
                        S.dve(lambda e, hd=hd, pb=pb, EL=EL: e.scalar_tensor_tensor(out=S32[:, hd, :], in0=S32[:, hd, :], scalar=EL[:, hd:hd + 1], in1=ps[pb][:],
                                                                                    op0=ALU.mult, op1=ALU.add),
                              reads=[K("S32", hd), f"ps{pb}", K("EL", bsel)], writes=[K("S32", hd)])
                        S.act(lambda e, hd=hd: e.copy(out=Sbf[:, hd, :], in_=S32[:, hd, :]), reads=[K("S32", hd)], writes=[K("Sbf", hd)])
                    for h in range(4):
                        S.act(lambda e, h=h: e.activation(out=sq[:, h * 4:(h + 1) * 4, :], in_=ps[h][:].rearrange("p (a b) -> p a b", b=128), func=AF.Square),
                              reads=[f"ps{h}"], writes=[K("sq", h)])
                    for h in range(4):
                        for ec in range(4):
                            S.pe(lambda e, h=h, ec=ec: e.matmul(ps[4][:, h * 128:(h + 1) * 128], ones512, sq[:, h * 4 + ec, :], start=(ec == 0), stop=(ec == 3)),
                                 reads=[K("sq", h), "cb16"], writes=["ps4"])
                    S.act(lambda e: e.activation(out=rsq, in_=ps[4][:].rearrange("p (a b) -> p a b", b=128), func=AF.Sqrt, bias=1e-6, scale=1.0), reads=["ps4"], writes=["rsq"])
                    S.dve(lambda e: e.reciprocal(out=rsq, in_=rsq), reads=["rsq"], writes=["rsq"])
                    for h in range(4):
                        S.dve(lambda e, h=h: e.tensor_tensor(out=t1[:, h * 4:(h + 1) * 4, :], in0=ps[h][:].rearrange("p (a b) -> p a b", b=128),
                                                             in1=rsq[:, h:h + 1, :].broadcast_to([128, 4, 128]), op=ALU.mult),
                              reads=[f"ps{h}", "rsq"], writes=[K("t1", h)])
                    for ec in range(4):
                        ngc = cvec[:, C_NG + j * 4 + ec:C_NG + j * 4 + ec + 1]
                        S.dve(lambda e, ec=ec, ngc=ngc: e.scalar_tensor_tensor(out=og[:, ec::4, tsl], in0=t1[:, ec::4, :], scalar=ngc, in1=rs[:, ec::4, tsl],
                                                                              op0=ALU.mult, op1=ALU.mult),
                              reads=[K("t1", h) for h in range(4)] + [K("rs", c) for c in range(16)] + ["cvec"], writes=XS_KEYS + [K("og", tc)])

                G(0)
                for tc in range(4):
                    if tc < 3:
                        G(tc + 1)
                    B(tc)
                outproj_ln(lambda kc, nt: (og[:, kc, :], f"xs{kc // 4}"), 16, Wout, tok0, 1, xres, x32_out, xb_out, (l * 2) * 16, okey, wfn=lambda oc: wload_ws(j, 48 + oc))

        def dil_phase(l, xb_in, xres, x32_out, xb_out, okey):
            j = l // 2
            S.barrier()
            al = Alloc(arena, PHASE_BASE)
            xs = al.get([16, 2048], BF16)
            og = al.get([8, 2048], BF16)
            acc2 = al.get([2, 2048], F32)
            qTb = [al.get([2048], BF16) for _ in range(2)]
            kTb = [al.get([2048], BF16) for _ in range(2)]
            vT = al.get([2048], BF16)
            vtokb = [al.get([16, 128], BF16) for _ in range(2)]
            pexp = [al.get([2, 2, 128], BF16) for _ in range(2)]
            PTb = [al.get([2, 2, 128], BF16) for _ in range(2)]
            Win = dil_w_in[j]
            Wout = dil_w_out[j]
            ps3bf = ps[3][:].bitcast(BF16)
            cnt = {"hg": 0, "blk": 0}
            load_xs(xs, xb_in, 0, 2048)
            for s in range(NSEQ):
                tokS = s * SEQ
                def make_iter(h, gi, dd, hg):
                    qT, kT, vtok = qTb[hg], kTb[hg], vtokb[hg]
                    qk_, kk_, vk_, vtk = K("q", hg), K("k", hg), "vT", K("vtok", hg)
                    nb = 16 // dd
                    wst = {}

                    def tsl(blk):
                        r, bb = blk // nb, blk % nb
                        return slice(r + dd * 128 * bb, r + dd * 128 * bb + dd * 127 + 1, dd)

                    def grp(t, nt):
                        if t == 0 and nt == 0:
                            for tt_ in range(3):
                                wst[tt_] = wload(Win, 0, 16, ((gi * 3 + tt_) * 8 + h) * 128, 128)
                        wv, wk = wst[t]
                        b, bk = mainbank(3)
                        for kc in range(16):
                            S.pe(lambda e, b=b, wv=wv, kc=kc, nt=nt: e.matmul(ps[b][:], wv[:, kc, :], xs[:, kc, nt * 512:(nt + 1) * 512], start=(kc == 0), stop=(kc == 15)),
                                 reads=[wk, f"xs{kc // 4}"], writes=[bk])
                        nsl = slice(nt * 512, (nt + 1) * 512)
                        if t == 0:
                            S.act(lambda e, b=b, nsl=nsl: e.activation(out=qT[:, nsl], in_=ps[b][:], func=AF.Identity, scale=float(128 ** -0.5)),
                                  reads=[bk], writes=[qk_])
                        elif t == 1:
                            S.dve(lambda e, b=b, nsl=nsl: e.tensor_copy(out=kT[:, nsl], in_=ps[b][:]), reads=[bk], writes=[kk_])
                        else:
                            S.act(lambda e, b=b, nsl=nsl: e.copy(out=vT[:, nsl], in_=ps[b][:]), reads=[bk], writes=[vk_])
                    groups = [(lambda t=t, nt=nt: grp(t, nt)) for t in range(3) for nt in range(4)]

                    def trans():
                        for half in range(2):
                            for bi in range(8):
                                vsl = tsl(half * 8 + bi)
                                S.pe(lambda e, bi=bi, vsl=vsl: e.transpose(ps3bf[:, bi * 128:(bi + 1) * 128], vT[:, vsl], ident), reads=[vk_, "cb16"], writes=["ps3"])
                            S.dve(lambda e, half=half: e.tensor_copy(out=vtok[:, half * 8:(half + 1) * 8, :], in_=ps3bf.rearrange("p (a b) -> p a b", b=128)),
                                  reads=["ps3"], writes=[vtk])

                    def att_scores(pr):
                        ci = cnt["blk"] % 2
                        cnt["blk"] += 1
                        sb = 4 + ci
                        psS = ps[sb][:].rearrange("p (k a b) -> p k a b", a=2, b=128)
                        infos = []
                        for kq in range(2):
                            blk = 2 * pr + kq
                            bb = blk % nb
                            cs = tsl(blk)
                            psl = tsl(blk - 1) if bb > 0 else cs
                            S.pe(lambda e, psS=psS, psl=psl, cs=cs, kq=kq: e.matmul(psS[:, kq, 0, :], kT[:, psl], qT[:, cs], start=True, stop=True),
                                 reads=[kk_, qk_], writes=[f"ps{sb}"])
                            S.pe(lambda e, psS=psS, cs=cs, kq=kq: e.matmul(psS[:, kq, 1, :], kT[:, cs], qT[:, cs], start=True, stop=True),
                                 reads=[kk_, qk_], writes=[f"ps{sb}"])
                            infos.append((blk, bb, cs))
                        return (ci, infos)

                    def att_mid(ctx):
                        ci, infos = ctx
                        sb, ob_ = 4 + ci, 6 + ci
                        psS = ps[sb][:].rearrange("p (k a b) -> p k a b", a=2, b=128)
                        pe_, PT = pexp[ci], PTb[ci]
                        S.act(lambda e, psS=psS, pe_=pe_: e.activation(out=pe_, in_=psS, func=AF.Exp), reads=[f"ps{sb}"], writes=[K("pexp", ci)])
                        for kq, (blk, bb, cs) in enumerate(infos):
                            msk = M3[:, 1:3, :] if bb > 0 else M3[:, 0:3:2, :]
                            S.dve(lambda e, pe_=pe_, PT=PT, kq=kq, msk=msk: e.tensor_tensor(out=PT[:, kq, :, :], in0=pe_[:, kq, :, :], in1=msk, op=ALU.mult),
                                  reads=[K("pexp", ci), "cm32"], writes=[K("PT", ci, kq)])

                    def att_rest(ctx):
                        ci, infos = ctx
                        sb, ob_ = 4 + ci, 6 + ci
                        psO = ps[ob_][:].rearrange("p (k a b) -> p k a b", a=2, b=128)
                        pe_, PT = pexp[ci], PTb[ci]
                        for kq, (blk, bb, cs) in enumerate(infos):
                            for jj in range(2):
                                bk_ = blk - 1 if (jj == 0 and bb > 0) else blk
                                S.pe(lambda e, psO=psO, PT=PT, jj=jj, bk_=bk_, kq=kq: e.matmul(psO[:, kq, 0, :], vtok[:, bk_, :], PT[:, kq, jj, :], start=(jj == 0), stop=(jj == 1)),
                                     reads=[vtk, K("PT", ci, kq)], writes=[f"ps{ob_}"])
                            for jj in range(2):
                                S.pe(lambda e, psO=psO, PT=PT, jj=jj, kq=kq: e.matmul(psO[:, kq, 1, :], ones1, PT[:, kq, jj, :], start=(jj == 0), stop=(jj == 1)),
                                     reads=["cb16", K("PT", ci, kq)], writes=[f"ps{ob_}"])
                        for kq, (blk, bb, cs) in enumerate(infos):
                            if gi == 0:
                                S.dve(lambda e, psO=psO, cs=cs, kq=kq: e.tensor_copy(out=acc2[:, :, cs], in_=psO[:, kq, :, :]), reads=[f"ps{ob_}"], writes=["acc2"])
                            else:
                                S.dve(lambda e, psO=psO, cs=cs, kq=kq: e.tensor_tensor(out=acc2[:, :, cs], in0=acc2[:, :, cs], in1=psO[:, kq, :, :], op=ALU.add),
                                      reads=[f"ps{ob_}", "acc2"], writes=["acc2"])
                    actx = deque()

                    def att_step(pr):
                        if pr < 8:
                            actx.append(att_scores(pr))
                            att_mid(actx[-1])
                        if pr >= 1:
                            att_rest(actx.popleft())
                    att = [(lambda pr=pr: att_step(pr)) for pr in range(9)]

                    def fin():
                        if gi == 2:
                            S.dve(lambda e: e.reciprocal(out=acc2[:, 1, :], in_=acc2[:, 1, :]), reads=["acc2"], writes=["acc2"])
                            S.dve(lambda e: e.tensor_tensor(out=og[:, h, :], in0=acc2[:, 0, :], in1=acc2[:, 1, :], op=ALU.mult), reads=["acc2"], writes=[K("og", h)])
                    return groups, trans, att, fin

                iters = [(h, gi, dd) for h in range(8) for gi, dd in enumerate((1, 4, 16))]
                prev = None
                for it in range(len(iters) + 1):
                    cur_it = None
                    if it < len(iters):
                        cur_it = make_iter(*iters[it], cnt["hg"] % 2)
                        cnt["hg"] += 1
                    pop_work(4)
                    glist = cur_it[0] if cur_it else []
                    alist = prev[2] if prev else []
                    for i in range(max(len(glist), len(alist))):
                        if i < len(glist):
                            glist[i]()
                        if i < len(alist):
                            alist[i]()
                    if prev:
                        prev[3]()
                    if cur_it:
                        cur_it[1]()
                    prev = cur_it
                if s + 1 < NSEQ:
                    load_xs(xs, xb_in, tokS + SEQ, 2048)
                for half in range(2):
                    outproj_ln(lambda kc, nt, half=half: (og[:, kc, half * 1024 + nt * 512:half * 1024 + (nt + 1) * 512], K("og", kc)),
                               8, Wout, tokS + half * 1024, 2, xres, x32_out, xb_out, (l * 2) * 16, okey)

        cur = None
        for f in gla_conv_ops(0):
            f()
        for sidx in range(NSUB):
            l = sidx // 2
            if sidx == 2 and NSUB > 4:
                pending_work.extend(gla_conv_ops(1))
            last = (sidx == NSUB - 1)
            xres = xT if sidx == 0 else X32
            x32_out = outT if last else X32
            okey = "out" if last else "x32"
            xb_in = xT if cur is None else XB[cur]
            nxt = 0 if cur is None else 1 - cur
            xb_out = XB[nxt]
            if sidx % 2 == 1:
                ffn_phase(l, xb_in, xres, x32_out, xb_out, okey)
            elif l % 2 == 0:
                gla_phase(l, xb_in, xres, x32_out, xb_out, okey)
            else:
                dil_phase(l, xb_in, xres, x32_out, xb_out, okey)
            cur = nxt
        drain_work()
        S.barrier()
        S.op("sp", lambda e: e.nop())
        S.emit()
    return nc


def host_consts(inputs):
    f = np.float32
    cvec = np.zeros((128, NCV), f)
    cvec[:, C_LNG:C_LNG + 128] = np.asarray(inputs["ln_g"], f).reshape(8, 16, 128).transpose(2, 0, 1).reshape(128, 128)
    cvec[:, C_LNB:C_LNB + 128] = np.asarray(inputs["ln_b"], f).reshape(8, 16, 128).transpose(2, 0, 1).reshape(128, 128)
    cvec[:, C_CW:C_CW + 1032] = np.asarray(inputs["ffn_conv_w"], f).reshape(4, 3, 86, 128).transpose(3, 0, 1, 2).reshape(128, 1032)
    cvec[:, C_CB:C_CB + 344] = np.asarray(inputs["ffn_conv_b"], f).reshape(4, 86, 128).transpose(2, 0, 1).reshape(128, 344)
    cvec[:, C_NG:C_NG + 8] = np.asarray(inputs["gla_norm_g"], f).reshape(2, 4, 128).transpose(2, 0, 1).reshape(128, 8)
    cmat = np.zeros((128, 1024), f)
    ii = np.arange(128)
    le = (ii[:, None] <= ii[None, :]).astype(f)
    ge = (ii[:, None] >= ii[None, :]).astype(f)
    cmat[:, 0:128] = le * f(-1.0 / 16.0)
    cmat[:, 256:384] = ge
    cmat[:, 384:512] = le
    cmat[:, 512:640] = np.eye(128, dtype=f)
    cmat[:, 640:768] = f(1.0 / 2048.0)
    cmat[:, 768:896] = f(1.0 / 512.0)
    cmat[:, 896:1024] = f(1.0)
    wg = np.concatenate([np.asarray(inputs["gla_w_gate_up"], f), np.asarray(inputs["gla_gate_bias"], f)[:, None, :]], axis=1)
    return cvec, cmat, np.ascontiguousarray(wg)


_PROG = {}


def run(inputs, ncores=NCORES, nseq=NSEQ_CORE, nsub=8, trace=False):
    key = (nseq, nsub)
    if key not in _PROG:
        _PROG[key] = build(nseq, nsub)
    nc = _PROG[key]
    cvec, cmat, wg = host_consts(inputs)
    x = np.asarray(inputs["x"], np.float32)
    shared = {k: np.ascontiguousarray(np.asarray(inputs[k], np.float32)) for k in
              ("gla_w_in", "gla_w_out", "dil_w_in", "dil_w_out", "ffn_w_up", "ffn_w_down")}
    in_maps = []
    for c in range(ncores):
        xc = x[c * nseq:(c + 1) * nseq].reshape(nseq * SEQ, D)
        m = dict(shared)
        m.update(xT=np.ascontiguousarray(xc.T), cvec=cvec, cmat=cmat, wg_aug=wg)
        in_maps.append(m)
    res = run_bass_kernel_spmd(nc, in_maps, core_ids=list(range(ncores)), **({"trace": True} if trace else {}))
    out = np.empty((ncores * nseq, SEQ, D), np.float32)
    for c in range(ncores):
        oT = np.asarray(res.results[c]["outT"])
        out[c * nseq:(c + 1) * nseq] = oT.T.reshape(nseq, SEQ, D)
    return out, res


def kernel(**inputs):
    out, _ = run(inputs)
    return out
```

```python
import contextlib
from collections import deque
import numpy as np
import concourse.bass as bass
import concourse.mybir as mybir
from concourse.bass_utils import run_bass_kernel_spmd

F32 = mybir.dt.float32
BF16 = mybir.dt.bfloat16
AF = mybir.ActivationFunctionType
ALU = mybir.AluOpType

D = 2048
SEQ = 2048
DFF = 5504
NFC = 43
ALPHA = float(8 ** 0.25)
NCORES = 8
NSEQ_CORE = 2

ENGS = ["pe", "act", "dve", "pool", "sp"]
NDMASEM = 16
EPOCH = 30000
NWB = 6
ARENA_BYTES = 212800

NCV = 1640
C_LNG, C_LNB, C_CW, C_CB, C_NG = 0, 128, 256, 1288, 1632


def K(*a):
    return "_".join(map(str, a))


class Sched:
    def __init__(self, nc):
        self.nc = nc
        self.ops = []
        self.last_w = {}
        self.readers = {}
        self.last_comp = {}
        self.recent_dma = {e: deque(maxlen=NDMASEM) for e in ENGS}
        self.bar_deps = set()
        self.bar_pending = set()

    def barrier(self):
        deps = set()
        for e in ENGS:
            if e in self.last_comp:
                deps.add(self.last_comp[e])
            deps.update(self.recent_dma[e])
        self.bar_deps = deps
        self.bar_pending = set(ENGS)
        self.last_w = {}
        self.readers = {}

    def op(self, eng, fn, reads=(), writes=(), dma=False):
        i = len(self.ops)
        deps = set()
        for k in reads:
            w = self.last_w.get(k)
            if w is not None:
                deps.add(w)
        for k in writes:
            w = self.last_w.get(k)
            if w is not None:
                deps.add(w)
            rs = self.readers.get(k)
            if rs:
                deps.update(rs)
        for k in reads:
            self.readers.setdefault(k, []).append(i)
        for k in writes:
            self.last_w[k] = i
            self.readers[k] = []
        if eng in self.bar_pending:
            deps |= self.bar_deps
            self.bar_pending.discard(eng)
        deps.discard(i)
        self.ops.append((eng, fn, deps, dma))
        if dma:
            self.recent_dma[eng].append(i)
        else:
            self.last_comp[eng] = i
        return i

    def pe(self, fn, reads=(), writes=()):
        return self.op("pe", fn, reads, writes)

    def act(self, fn, reads=(), writes=()):
        return self.op("act", fn, reads, writes)

    def dve(self, fn, reads=(), writes=()):
        return self.op("dve", fn, reads, writes)

    def dma(self, q, fn, reads=(), writes=()):
        return self.op(q, fn, reads, writes, dma=True)

    def emit(self):
        nc = self.nc
        ops = self.ops
        n = len(ops)
        need_inc = [False] * n
        for i, (eng, fn, deps, dma) in enumerate(ops):
            for d in deps:
                deng, _, _, ddma = ops[d]
                if ddma:
                    continue
                if deng == "pe" and eng == "pe" and not dma:
                    continue
                need_inc[d] = True
        cnt = {e: 0 for e in ENGS}
        dcnt = {e: 0 for e in ENGS}
        sig = [None] * n
        for i, (eng, fn, deps, dma) in enumerate(ops):
            if dma:
                k = dcnt[eng]
                dcnt[eng] += 1
                sig[i] = (("d", eng, k % NDMASEM), 16 * (k // NDMASEM + 1))
            elif need_inc[i]:
                c = cnt[eng]
                cnt[eng] += 1
                sig[i] = (("c", eng, c // EPOCH), (c % EPOCH) + 1)
        semkeys = sorted({s[0] for s in sig if s is not None})
        with contextlib.ExitStack() as st:
            sems = {}
            for k in semkeys:
                sems[k] = st.enter_context(nc.semaphore("s_" + "_".join(map(str, k))))
            block = st.enter_context(nc.Block())

            def run_engine(ename, e):
                waited = {}
                for i, (eng, fn, deps, dma) in enumerate(ops):
                    if eng != ename:
                        continue
                    wl = {}
                    for d in deps:
                        deng, _, _, ddma = ops[d]
                        if (not ddma) and deng == "pe" and eng == "pe" and not dma:
                            continue
                        sk, v = sig[d]
                        if waited.get(sk, 0) >= v:
                            continue
                        if wl.get(sk, 0) < v:
                            wl[sk] = v
                    if dma:
                        sk, v = sig[i]
                        if v > 16 and waited.get(sk, 0) < v - 16 and wl.get(sk, 0) < v - 16:
                            wl[sk] = v - 16
                    for sk, v in wl.items():
                        e.wait_ge(sems[sk], v)
                        waited[sk] = v
                    ins = fn(e)
                    if sig[i] is not None:
                        ins.then_inc(sems[sig[i][0]], 16 if dma else 1)

            @block.tensor
            def _(e):
                run_engine("pe", e)

            @block.scalar
            def _(e):
                run_engine("act", e)

            @block.vector
            def _(e):
                run_engine("dve", e)

            @block.gpsimd
            def _(e):
                run_engine("pool", e)

            @block.sync
            def _(e):
                run_engine("sp", e)


class Alloc:
    def __init__(self, arena, base=0):
        self.a = arena
        self.off = base

    def get(self, shape, dt, parts=128):
        n = int(np.prod(shape))
        nb = n * (4 if dt == F32 else 2)
        start = self.off
        self.off += (nb + 63) // 64 * 64
        assert self.off <= ARENA_BYTES, ("arena overflow", self.off)
        ap = self.a[0:parts, start // 2:(start + nb) // 2]
        if dt == F32:
            ap = ap.bitcast(F32)
        if len(shape) == 2:
            ap = ap.rearrange("p (a b) -> p a b", b=shape[1])
        elif len(shape) == 3:
            ap = ap.rearrange("p (a b c) -> p a b c", b=shape[1], c=shape[2])
        return ap


def build(NSEQ=NSEQ_CORE, NSUB=8):
    NT = NSEQ * SEQ
    nc = bass.Bass("TRN2", target_bir_lowering=False)

    def din(name, shape):
        return nc.dram_tensor(name, list(shape), F32, kind="ExternalInput").ap()

    xT = din("xT", [D, NT])
    gla_w_in = din("gla_w_in", [2, D, 6160])
    gla_w_out = din("gla_w_out", [2, D, D])
    dil_w_in = din("dil_w_in", [2, D, 9216])
    dil_w_out = din("dil_w_out", [2, 1024, D])
    ffn_w_up = din("ffn_w_up", [4, D, 11008])
    ffn_w_down = din("ffn_w_down", [4, DFF, D])
    cvec_d = din("cvec", [128, NCV])
    cmat_d = din("cmat", [128, 1024])
    wg_d = din("wg_aug", [2, 17, 1024])
    outT = nc.dram_tensor("outT", [D, NT], F32, kind="ExternalOutput").ap()
    X32 = nc.dram_tensor("X32", [D, NT], F32).ap()
    XB = [nc.dram_tensor(f"XB{i}", [D, NT], BF16).ap() for i in range(2)]
    WS = nc.dram_tensor("WS", [2, 64, 128, 2048], BF16).ap()

    with contextlib.ExitStack() as st:
        arena = st.enter_context(nc.sbuf_tensor("arena", [128, ARENA_BYTES // 2], BF16))
        ps = [st.enter_context(nc.psum_tensor(f"ps{i}", [128, 512], F32)) for i in range(8)]
        S = Sched(nc)

        cm = Alloc(arena, 0)
        cvec = cm.get([NCV], F32)
        cm32 = cm.get([4, 128], F32)
        tri = cm32[:, 0, :]
        M3 = cm32[:, 1:4, :]
        mcur = cm32[:, 3, :]
        cb16 = cm.get([4, 128], BF16)
        ident, onesD, ones512, ones1 = cb16[:, 0, :], cb16[:, 1, :], cb16[:, 2, :], cb16[:, 3, :]
        wpool = [cm.get([2048], BF16) for _ in range(NWB)]
        XR = [cm.get([512], F32) for _ in range(3)]
        ybf = cm.get([512], BF16)
        ysq = cm.get([512], BF16)
        mean_t = cm.get([512], F32)
        tmp_t = cm.get([512], F32)
        rstd_t = [cm.get([512], F32) for _ in range(2)]
        nmr_t = [cm.get([512], F32) for _ in range(2)]
        YL = [cm.get([512], F32) for _ in range(4)]
        OBF = [cm.get([512], BF16) for _ in range(2)]
        PHASE_BASE = cm.off

        S.dma("sp", lambda e: e.dma_start(out=cvec, in_=cvec_d), writes=["cvec"])
        S.dma("sp", lambda e: e.dma_start(out=cm32, in_=cmat_d[:, 0:512].rearrange("p (a b) -> p a b", b=128)), writes=["cm32"])
        S.dma("pool", lambda e: e.dma_start(out=cb16, in_=cmat_d[:, 512:1024].rearrange("p (a b) -> p a b", b=128)), writes=["cb16"])

        ctr = {"wb": 0, "bank": 0, "xr": 0, "yl": 0, "obf": 0, "conv": 0}

        def wload(Wd, k0, nk, col0, ncols):
            i = ctr["wb"] % NWB
            ctr["wb"] += 1
            view = wpool[i][:, 0:nk * ncols].rearrange("p (a b) -> p a b", b=ncols)
            src = Wd[k0 * 128:(k0 + nk) * 128, col0:col0 + ncols].rearrange("(a p) m -> p a m", p=128)
            key = f"wb{i}"
            S.dma("pool", lambda e: e.dma_start(out=view, in_=src), writes=[key])
            return view, key

        def ws_src(j, cid):
            if cid < 48:
                col0 = (cid * 128 if cid < 8 else 1024 + (cid - 8) * 128 if cid < 16 else 4096 + (cid - 16) * 128 if cid < 32 else 2048 + (cid - 32) * 128)
                return gla_w_in[j], col0
            return gla_w_out[j], (cid - 48) * 128

        def gla_conv_ops(j):
            lst = []
            for cid in range(64):
                def f(cid=cid):
                    ctr["conv"] -= 1
                    Wd, col0 = ws_src(j, cid)
                    view, key = wload(Wd, 0, 16, col0, 128)
                    dst = WS[j, cid].rearrange("p (a b) -> p a b", b=128)
                    S.dma("sp", lambda e: e.dma_start(out=dst, in_=view), reads=[key], writes=[K("ws", j, cid)])
                lst.append(f)
            ctr["conv"] += len(lst)
            return lst

        def wload_ws(j, cid):
            i = ctr["wb"] % NWB
            ctr["wb"] += 1
            view = wpool[i].rearrange("p (a b) -> p a b", b=128)
            src = WS[j, cid].rearrange("p (a b) -> p a b", b=128)
            key = f"wb{i}"
            S.dma("pool", lambda e: e.dma_start(out=view, in_=src), reads=[K("ws", j, cid)], writes=[key])
            return view, key

        def pk(b):
            return [f"ps{b}", f"ps{b}_0", f"ps{b}_1"]

        pending_work = deque()
        active_ranges = []

        def pop_work(n=1):
            for _ in range(n):
                if pending_work:
                    pending_work.popleft()()

        def drain_work():
            while pending_work:
                pending_work.popleft()()

        def mainbank(nb=4):
            b = ctr["bank"] % nb
            ctr["bank"] += 1
            return b, f"ps{b}"

        def load_xs(xs, src, tok0, T):
            if any(a < tok0 + T and tok0 < b for (a, b) in active_ranges):
                drain_work()
            q = "pool" if src.dtype == F32 else "sp"
            for g in range(4):
                v = src[g * 512:(g + 1) * 512, tok0:tok0 + T].rearrange("(a p) t -> p a t", p=128)
                rk = [K("xb", c, t) for c in range(4 * g, 4 * g + 4) for t in range(tok0, tok0 + T, 512)]
                S.dma(q, lambda e, g=g, v=v: e.dma_start(out=xs[:, 4 * g:4 * g + 4, :], in_=v), reads=rk, writes=[f"xs{g}"])

        def outproj_ln(src_fn, KC, Wd, tok0, NTL, xres, x32_out, xb_out, lncol, okey, wfn=None, defer=True):
            pend = [None]
            groups = [(oc, nt) for oc in range(16) for nt in range(NTL)]
            wts = {}
            xrs = {}

            def load_w(oc):
                if wfn is not None:
                    wv, wk = wfn(oc)
                    wts[oc] = [(0, KC, wv, wk)]
                    return
                pieces = []
                for p0 in range(0, KC, 16):
                    n = min(16, KC - p0)
                    pieces.append((p0, n) + wload(Wd, p0, n, oc * 128, 128))
                wts[oc] = pieces

            def load_xr(g):
                oc, nt = groups[g]
                t0 = tok0 + nt * 512
                xi = ctr["xr"] % 3
                ctr["xr"] += 1
                xr, xk = XR[xi], f"xr{xi}"
                xsrc = xres[oc * 128:(oc + 1) * 128, t0:t0 + 512]
                S.dma("sp", lambda e, xr=xr, xsrc=xsrc: e.dma_start(out=xr, in_=xsrc), reads=[K("x32", oc, t0)], writes=[xk])
                xrs[g] = (xr, xk)

            load_w(0)
            for g in range(min(2, len(groups))):
                load_xr(g)
            for gi, (oc, nt) in enumerate(groups):
                if True:
                    pop_work(3)
                    if nt == 0 and oc + 1 < 16:
                        load_w(oc + 1)
                    if gi + 2 < len(groups):
                        load_xr(gi + 2)
                    pieces = wts[oc]
                    t0 = tok0 + nt * 512
                    b, bk = mainbank()
                    xr, xk = xrs.pop(gi)
                    for (p0, n, wv, wk) in pieces:
                        for kk in range(n):
                            kc = p0 + kk
                            rhs, rk = src_fn(kc, nt)
                            S.pe(lambda e, b=b, wv=wv, kk=kk, rhs=rhs, kc=kc: e.matmul(ps[b][:], wv[:, kk, :], rhs, start=(kc == 0), stop=(kc == KC - 1)),
                                 reads=[wk, rk], writes=[bk])
                    if pend[0]:
                        pend[0]()
                        pend[0] = None
                    S.dve(lambda e, xr=xr, b=b: e.scalar_tensor_tensor(out=xr, in0=xr, scalar=ALPHA, in1=ps[b][:], op0=ALU.mult, op1=ALU.add),
                          reads=[xk, bk], writes=[xk])
                    S.act(lambda e, xr=xr: e.activation(out=ybf, in_=xr, func=AF.Copy), reads=[xk], writes=["ybf"])
                    S.act(lambda e, xr=xr: e.activation(out=ysq, in_=xr, func=AF.Square), reads=[xk], writes=["ysq"])
                    ydst = X32[oc * 128:(oc + 1) * 128, t0:t0 + 512]
                    S.dma("act", lambda e, xr=xr, ydst=ydst: e.dma_start(out=ydst, in_=xr), reads=[xk], writes=[K("x32", oc, t0)])

                    def mk(oc=oc, nt=nt):
                        pm, pq = 4 + 2 * nt, 5 + 2 * nt
                        S.pe(lambda e: e.matmul(ps[pm][:], onesD, ybf, start=(oc == 0), stop=(oc == 15)), reads=["ybf", "cb16"], writes=pk(pm))
                        S.pe(lambda e: e.matmul(ps[pq][:], onesD, ysq, start=(oc == 0), stop=(oc == 15)), reads=["ysq", "cb16"], writes=pk(pq))
                    pend[0] = mk
            pend[0]()
            drain_work()
            units = []
            for nt in range(NTL):
                t0 = tok0 + nt * 512
                pm, pq = 4 + 2 * nt, 5 + 2 * nt
                rstd, nmr = rstd_t[nt], nmr_t[nt]
                S.act(lambda e, pm=pm: e.activation(out=mean_t, in_=ps[pm][:], func=AF.Copy), reads=pk(pm), writes=["mean"])
                S.dve(lambda e: e.tensor_tensor(out=tmp_t, in0=mean_t, in1=mean_t, op=ALU.mult), reads=["mean"], writes=["tmp"])
                S.dve(lambda e, pq=pq: e.tensor_tensor(out=tmp_t, in0=ps[pq][:], in1=tmp_t, op=ALU.subtract), reads=pk(pq) + ["tmp"], writes=["tmp"])
                S.act(lambda e: e.activation(out=tmp_t, in_=tmp_t, func=AF.Sqrt, bias=1e-5, scale=1.0), reads=["tmp"], writes=["tmp"])
                S.dve(lambda e, rstd=rstd: e.reciprocal(out=rstd, in_=tmp_t), reads=["tmp"], writes=[f"rstd{nt}"])
                S.dve(lambda e, rstd=rstd, nmr=nmr: e.scalar_tensor_tensor(out=nmr, in0=mean_t, scalar=-1.0, in1=rstd, op0=ALU.mult, op1=ALU.mult),
                      reads=["mean", f"rstd{nt}"], writes=[f"nmr{nt}"])
                for oc in range(16):
                    def ld(oc=oc, nt=nt, t0=t0):
                        yi = ctr["yl"] % 4
                        ctr["yl"] += 1
                        yl, yk = YL[yi], f"yl{yi}"
                        ysrc = X32[oc * 128:(oc + 1) * 128, t0:t0 + 512]
                        S.dma("sp", lambda e, yl=yl, ysrc=ysrc: e.dma_start(out=yl, in_=ysrc), reads=[K("x32", oc, t0)], writes=[yk])
                        return yl, yk

                    def cpd(yl, yk, oc=oc, nt=nt, t0=t0, rstd=rstd, nmr=nmr):
                        S.dve(lambda e, yl=yl, rstd=rstd: e.tensor_tensor(out=yl, in0=yl, in1=rstd, op=ALU.mult), reads=[yk, f"rstd{nt}"], writes=[yk])
                        S.dve(lambda e, yl=yl, nmr=nmr: e.tensor_tensor(out=yl, in0=yl, in1=nmr, op=ALU.add), reads=[yk, f"nmr{nt}"], writes=[yk])

                    def cp(yl, yk, oc=oc, nt=nt, t0=t0, rstd=rstd, nmr=nmr):
                        oi = ctr["obf"] % 2
                        ctr["obf"] += 1
                        ob, obk = OBF[oi], f"obf{oi}"
                        gcol = cvec[:, C_LNG + lncol + oc:C_LNG + lncol + oc + 1]
                        bcol = cvec[:, C_LNB + lncol + oc:C_LNB + lncol + oc + 1]
                        S.act(lambda e, yl=yl, ob=ob, gcol=gcol, bcol=bcol: e.activation(out=ob, in_=yl, func=AF.Identity, bias=bcol, scale=gcol),
                              reads=[yk, "cvec"], writes=[obk])
                        S.act(lambda e, yl=yl, gcol=gcol, bcol=bcol: e.activation(out=yl, in_=yl, func=AF.Identity, bias=bcol, scale=gcol),
                              reads=[yk, "cvec"], writes=[yk])
                        odst = x32_out[oc * 128:(oc + 1) * 128, t0:t0 + 512]
                        S.dma("act", lambda e, yl=yl, odst=odst: e.dma_start(out=odst, in_=yl), reads=[yk], writes=[K(okey, oc, t0)])
                        bdst = xb_out[oc * 128:(oc + 1) * 128, t0:t0 + 512]
                        S.dma("act", lambda e, ob=ob, bdst=bdst: e.dma_start(out=bdst, in_=ob), reads=[obk], writes=[K("xb", oc, t0)])
                    units.append((ld, cp, cpd))
            nun = len(units)
            st_ = {}
            rng_ = (tok0, tok0 + NTL * 512)
            active_ranges.append(rng_)
            for i in range(nun + 3):
                def item(i=i):
                    if i == nun + 2:
                        active_ranges.remove(rng_)
                    if 0 <= i - 3 < nun:
                        units[i - 3][1](*st_.pop(i - 3))
                    if 0 <= i - 2 < nun:
                        units[i - 2][2](*st_[i - 2])
                    if i < nun:
                        st_[i] = units[i][0]()
                if defer:
                    pending_work.append(item)
                else:
                    item()

        def ffn_phase(l, xb_in, xres, x32_out, xb_out, okey):
            S.barrier()
            al = Alloc(arena, PHASE_BASE)
            xs = al.get([16, 1024], BF16)
            actT = al.get([NFC, 1024], BF16)
            hraw = [[al.get([514], F32) for _ in range(2)] for _ in range(2)]
            tt = [[al.get([512], F32) for _ in range(2)] for _ in range(2)]
            sg = [al.get([512], F32) for _ in range(2)]
            halo = al.get([86, 2], F32)
            Wup = ffn_w_up[l]
            Wdn = ffn_w_down[l]
            load_xs(xs, xb_in, 0, 1024)
            for tile in range(NT // 1024):
                tok0 = tile * 1024
                seq_start = (tok0 % SEQ == 0)
                for j in range(NFC):
                    for half in range(2):
                        c = j + NFC * half
                        if half == 0:
                            pop_work(1)
                        wv, wk = wload(Wup, 0, 16, c * 128, 128)
                        cw = [cvec[:, C_CW + (l * 3 + tap) * 86 + c:C_CW + (l * 3 + tap) * 86 + c + 1] for tap in range(3)]
                        cbc = cvec[:, C_CB + l * 86 + c:C_CB + l * 86 + c + 1]
                        for nt in range(2):
                            b, bk = mainbank()
                            for kc in range(16):
                                S.pe(lambda e, b=b, wv=wv, kc=kc, nt=nt: e.matmul(ps[b][:], wv[:, kc, :], xs[:, kc, nt * 512:(nt + 1) * 512],
                                                                                  start=(kc == 0), stop=(kc == 15)),
                                     reads=[wk, f"xs{kc // 4}"], writes=[bk])
                            hr, hk = hraw[half][nt], K("hr", half, nt)
                            t_, tk = tt[half][nt], K("tt", half, nt)
                            if nt == 0 and seq_start:
                                S.act(lambda e, hr=hr: e.memzero(hr[:, 0:2]), writes=[hk])
                            else:
                                S.act(lambda e, hr=hr, c=c: e.copy(out=hr[:, 0:2], in_=halo[:, c, :]), reads=[K("halo", c)], writes=[hk])
                            S.act(lambda e, hr=hr, b=b: e.copy(out=hr[:, 2:514], in_=ps[b][:]), reads=[bk], writes=[hk])
                            S.act(lambda e, hr=hr, c=c: e.copy(out=halo[:, c, :], in_=hr[:, 512:514]), reads=[hk], writes=[K("halo", c)])
                            S.act(lambda e, t_=t_, b=b, cw=cw, cbc=cbc: e.activation(out=t_, in_=ps[b][:], func=AF.Identity, bias=cbc, scale=cw[2]),
                                  reads=[bk, "cvec"], writes=[tk])
                            S.dve(lambda e, t_=t_, hr=hr, cw=cw: e.scalar_tensor_tensor(out=t_, in0=hr[:, 1:513], scalar=cw[1], in1=t_, op0=ALU.mult, op1=ALU.add),
                                  reads=[hk, tk, "cvec"], writes=[tk])
                            S.dve(lambda e, t_=t_, hr=hr, cw=cw: e.scalar_tensor_tensor(out=t_, in0=hr[:, 0:512], scalar=cw[0], in1=t_, op0=ALU.mult, op1=ALU.add),
                                  reads=[hk, tk, "cvec"], writes=[tk])
                            if half == 0:
                                S.act(lambda e, t_=t_, nt=nt: e.activation(out=sg[nt], in_=t_, func=AF.Silu), reads=[tk], writes=[K("sg", nt)])
                            else:
                                S.dve(lambda e, t_=t_, nt=nt, j=j: e.tensor_tensor(out=actT[:, j, nt * 512:(nt + 1) * 512], in0=t_, in1=sg[nt], op=ALU.mult),
                                      reads=[tk, K("sg", nt)], writes=[K("act", j, nt)])
                if tok0 + 1024 < NT:
                    load_xs(xs, xb_in, tok0 + 1024, 1024)
                outproj_ln(lambda kc, nt: (actT[:, kc, nt * 512:(nt + 1) * 512], K("act", kc, nt)),
                           NFC, Wdn, tok0, 2, xres, x32_out, xb_out, (l * 2 + 1) * 16, okey)

        def gla_phase(l, xb_in, xres, x32_out, xb_out, okey):
            j = l // 2
            if ctr["conv"] > 0:
                drain_work()
            S.barrier()
            al = Alloc(arena, PHASE_BASE)
            xs = al.get([16, 512], BF16)
            og = xs
            qT = al.get([8, 512], BF16)
            kT = al.get([8, 512], BF16)
            vsb = al.get([4, 2048], BF16)
            rs = al.get([16, 512], BF16)
            glow = al.get([512], F32, parts=17)
            wg = al.get([1024], F32, parts=17)
            S32 = al.get([8, 512], F32)
            Sbf = al.get([8, 512], BF16)
            u = al.get([1024], F32)
            EB = al.get([8, 128], F32)
            ENB = al.get([8, 128], F32)
            qs2 = [al.get([8, 128], BF16) for _ in range(2)]
            ks = al.get([8, 128], BF16)
            kh = al.get([8, 128], BF16)
            khat2 = [al.get([1024], BF16) for _ in range(2)]
            PT2 = [al.get([4, 128], BF16) for _ in range(2)]
            EL2 = [al.get([8], F32) for _ in range(2)]
            sq = al.get([16, 128], BF16)
            rsq = al.get([4, 128], F32)
            t1 = al.get([16, 128], F32)
            Win = gla_w_in[j]
            Wout = gla_w_out[j]
            S.dma("sp", lambda e: e.dma_start(out=wg, in_=wg_d[j]), writes=["wg"])
            S.dve(lambda e: e.memset(glow, 1.0), writes=["glow"])
            ps5bf = ps[5][:].bitcast(BF16)
            XS_KEYS = [f"xs{g}" for g in range(4)]
            for tile in range(NT // 512):
                tok0 = tile * 512
                seq_start = (tok0 % SEQ == 0)
                load_xs(xs, xb_in, tok0, 512)
                if seq_start:
                    S.dve(lambda e: e.memset(S32, 0.0), writes=[K("S32", h) for h in range(8)])
                    S.dve(lambda e: e.memset(Sbf, 0.0), writes=[K("Sbf", h) for h in range(8)])

                def getw(cid, tile=tile):
                    if j == 0 and tile == 0:
                        Wd_, col0_ = ws_src(j, cid)
                        view, key = wload(Wd_, 0, 16, col0_, 128)
                        dst = WS[j, cid].rearrange("p (a b) -> p a b", b=128)
                        S.dma("sp", lambda e: e.dma_start(out=dst, in_=view), reads=[key], writes=[K("ws", j, cid)])
                        return view, key
                    return wload_ws(j, cid)

                def fm_chunk(cid, evac):
                    pop_work(1)
                    wv, wk = getw(cid)
                    b, bk = mainbank()
                    for kc in range(16):
                        S.pe(lambda e, b=b, wv=wv, kc=kc: e.matmul(ps[b][:], wv[:, kc, :], xs[:, kc, :], start=(kc == 0), stop=(kc == 15)),
                             reads=[wk, f"xs{kc // 4}"], writes=[bk])
                    evac(b, bk)
                for c in range(8):
                    fm_chunk(c, lambda b, bk, c=c: S.act(lambda e: e.activation(out=qT[:, c, :], in_=ps[b][:], func=AF.Identity, scale=0.0625),
                                                                reads=[bk], writes=[K("qT", c)]))
                for c in range(8):
                    fm_chunk(8 + c, lambda b, bk, c=c: S.dve(lambda e: e.tensor_copy(out=kT[:, c, :], in_=ps[b][:]),
                                                                       reads=[bk], writes=[K("kT", c)]))
                for c in range(16):
                    fm_chunk(16 + c, lambda b, bk, c=c: S.act(lambda e: e.activation(out=rs[:, c, :], in_=ps[b][:], func=AF.Silu),
                                                                       reads=[bk], writes=[K("rs", c)]))
                wv, wk = wload(Win, 0, 16, 6144, 16)
                b, bk = mainbank()
                for kc in range(16):
                    S.pe(lambda e, b=b, wv=wv, kc=kc: e.matmul(ps[b][0:16, :], wv[:, kc, :], xs[:, kc, :], start=(kc == 0), stop=(kc == 15)),
                         reads=[wk, f"xs{kc // 4}"], writes=[bk])
                S.act(lambda e, b=b: e.copy(out=glow[0:16, :], in_=ps[b][0:16, :]), reads=[bk], writes=["glow"])
                for cg in range(16):
                    wv, wk = getw(32 + cg)
                    b, bk = mainbank()
                    for tc in range(4):
                        for kc in range(16):
                            S.pe(lambda e, b=b, wv=wv, kc=kc, tc=tc: e.matmul(ps[b][:, tc * 128:(tc + 1) * 128], xs[:, kc, tc * 128:(tc + 1) * 128], wv[:, kc, :],
                                                                              start=(kc == 0), stop=(kc == 15)),
                                 reads=[wk, f"xs{kc // 4}"], writes=[bk])
                    S.act(lambda e, b=b, cg=cg: e.copy(out=vsb[:, :, cg * 128:(cg + 1) * 128], in_=ps[b][:].rearrange("p (a b) -> p a b", b=128)),
                          reads=[bk], writes=[K("v", cg)])
                VK = [K("v", cg) for cg in range(16)]
                def G(tc):
                    bsel = tc % 2
                    tsl = slice(tc * 128, (tc + 1) * 128)
                    qs, PT, khat, EL = qs2[bsel], PT2[bsel], khat2[bsel], EL2[bsel]
                    for hf in range(2):
                        zb = 6 + hf
                        S.pe(lambda e, hf=hf, zb=zb: e.matmul(ps[zb][:], glow[0:17, tsl], wg[0:17, hf * 512:(hf + 1) * 512], start=True, stop=True),
                             reads=["glow", "wg"], writes=[f"ps{zb}"])
                        S.act(lambda e, hf=hf, zb=zb: e.activation(out=u[:, hf * 512:(hf + 1) * 512], in_=ps[zb][:], func=AF.Exp, scale=-1.0),
                              reads=[f"ps{zb}"], writes=[K("u", hf)])
                        S.act(lambda e, hf=hf: e.activation(out=u[:, hf * 512:(hf + 1) * 512], in_=u[:, hf * 512:(hf + 1) * 512], func=AF.Ln, bias=1.0, scale=1.0),
                              reads=[K("u", hf)], writes=[K("u", hf)])
                    for dc in range(8):
                        pb = 6 + dc // 4
                        S.pe(lambda e, dc=dc, pb=pb: e.matmul(ps[pb][:, (dc % 4) * 128:(dc % 4 + 1) * 128], u[:, dc * 128:(dc + 1) * 128], tri, start=True, stop=True),
                             reads=[K("u", dc // 4), "cm32"], writes=[f"ps{pb}"])
                    for hf in range(2):
                        pv = ps[6 + hf][:].rearrange("p (a b) -> p a b", b=128)
                        S.act(lambda e, hf=hf, pv=pv: e.activation(out=EB[:, hf * 4:(hf + 1) * 4, :], in_=pv, func=AF.Exp), reads=[f"ps{6 + hf}"], writes=[K("EB", hf)])
                        S.act(lambda e, hf=hf, pv=pv: e.activation(out=ENB[:, hf * 4:(hf + 1) * 4, :], in_=pv, func=AF.Exp, scale=-1.0), reads=[f"ps{6 + hf}"], writes=[K("ENB", hf)])
                    EBK = [K("EB", 0), K("EB", 1)]
                    ENBK = [K("ENB", 0), K("ENB", 1)]
                    S.act(lambda e, EL=EL: e.copy(out=EL, in_=EB[:, :, 127]), reads=EBK, writes=[K("EL", bsel)])
                    S.dve(lambda e, qs=qs: e.tensor_tensor(out=qs, in0=qT[:, :, tsl], in1=EB, op=ALU.mult), reads=[K("qT", c) for c in range(8)] + EBK, writes=[K("qs", bsel)])
                    S.dve(lambda e: e.tensor_tensor(out=ks, in0=kT[:, :, tsl], in1=ENB, op=ALU.mult), reads=[K("kT", c) for c in range(8)] + ENBK, writes=["ks"])
                    S.dve(lambda e: e.tensor_tensor(out=kh, in0=ks, in1=EB[:, :, 127:128].broadcast_to([128, 8, 128]), op=ALU.mult), reads=["ks"] + EBK, writes=["kh"])
                    for h in range(4):
                        for dl in range(2):
                            S.pe(lambda e, h=h, dl=dl, qs=qs: e.matmul(ps[4][:, h * 128:(h + 1) * 128], ks[:, 2 * h + dl, :], qs[:, 2 * h + dl, :], start=(dl == 0), stop=(dl == 1)),
                                 reads=["ks", K("qs", bsel)], writes=["ps4"])
                    S.dve(lambda e, PT=PT: e.tensor_tensor(out=PT, in0=ps[4][:].rearrange("p (a b) -> p a b", b=128),
                                                           in1=mcur.unsqueeze(1).broadcast_to([128, 4, 128]), op=ALU.mult), reads=["ps4", "cm32"], writes=[K("PT", bsel)])
                    for dc in range(8):
                        S.pe(lambda e, dc=dc: e.transpose(ps5bf[:, dc * 128:(dc + 1) * 128], kh[:, dc, :], ident), reads=["kh", "cb16"], writes=["ps5"])
                    S.act(lambda e, khat=khat: e.copy(out=khat, in_=ps5bf), reads=["ps5"], writes=[K("khat", bsel)])

                def B(tc):
                    bsel = tc % 2
                    tsl = slice(tc * 128, (tc + 1) * 128)
                    qs, PT, khat, EL = qs2[bsel], PT2[bsel], khat2[bsel], EL2[bsel]
                    for h in range(4):
                        for ec in range(4):
                            o_ = ps[h][:, ec * 128:(ec + 1) * 128]
                            S.pe(lambda e, h=h, ec=ec, o_=o_, PT=PT: e.matmul(o_, vsb[:, tc, h * 512 + ec * 128:h * 512 + (ec + 1) * 128], PT[:, h, :], start=True, stop=False),
                                 reads=VK + [K("PT", bsel)], writes=[f"ps{h}"])
                            for dl in range(2):
                                S.pe(lambda e, h=h, ec=ec, o_=o_, dl=dl, qs=qs: e.matmul(o_, Sbf[:, 2 * h + dl, ec * 128:(ec + 1) * 128], qs[:, 2 * h + dl, :], start=False, stop=(dl == 1)),
                                     reads=[K("Sbf", 2 * h + dl), K("qs", bsel)], writes=[f"ps{h}"])
                    for hd in range(8):
                        h = hd // 2
                        pb = (5, 4)[hd % 2]
                        S.pe(lambda e, hd=hd, h=h, pb=pb, khat=khat: e.matmul(ps[pb][:], khat[:, hd * 128:(hd + 1) * 128], vsb[:, tc, h * 512:(h + 1) * 512], start=True, stop=True),
                             reads=[K("khat", bsel)] + VK, writes=[f"ps{pb}"])
                        S.dve(lambda e, hd=hd, pb=pb, EL=EL: e.scalar_tensor_tensor(out=S32[:, hd, :], in0=S32[:, hd, :], scalar=EL[:, hd:hd + 1], in1=ps[pb][:],
                                                                                    op0=ALU.mult, op1=ALU.add),
                              reads=[K("S32", hd), f"ps{pb}", K("EL", bsel)], writes=[K("S32", hd)])
                        S.act(lambda e, hd=hd: e.copy(out=Sbf[:, hd, :], in_=S32[:, hd, :]), reads=[K("S32", hd)], writes=[K("Sbf", hd)])
                    for h in range(4):
                        S.act(lambda e, h=h: e.activation(out=sq[:, h * 4:(h + 1) * 4, :], in_=ps[h][:].rearrange("p (a b) -> p a b", b=128), func=AF.Square),
                              reads=[f"ps{h}"], writes=[K("sq", h)])
                    for h in range(4):
                        for ec in range(4):
                            S.pe(lambda e, h=h, ec=ec: e.matmul(ps[4][:, h * 128:(h + 1) * 128], ones512, sq[:, h * 4 + ec, :], start=(ec == 0), stop=(ec == 3)),
                                 reads=[K("sq", h), "cb16"], writes=["ps4"])
                    S.act(lambda e: e.activation(out=rsq, in_=ps[4][:].rearrange("p (a b) -> p a b", b=128), func=AF.Sqrt, bias=1e-6, scale=1.0), reads=["ps4"], writes=["rsq"])
                    S.dve(lambda e: e.reciprocal(out=rsq, in_=rsq), reads=["rsq"], writes=["rsq"])
                    for h in range(4):
                        S.dve(lambda e, h=h: e.tensor_tensor(out=t1[:, h * 4:(h + 1) * 4, :], in0=ps[h][:].rearrange("p (a b) -> p a b", b=128),
                                                             in1=rsq[:, h:h + 1, :].broadcast_to([128, 4, 128]), op=ALU.mult),
                              reads=[f"ps{h}", "rsq"], writes=[K("t1", h)])
                    for ec in range(4):
                        ngc = cvec[:, C_NG + j * 4 + ec:C_NG + j * 4 + ec + 1]
                        S.dve(lambda e, ec=ec, ngc=ngc: e.scalar_tensor_tensor(out=og[:, ec::4, tsl], in0=t1[:, ec::4, :], scalar=ngc, in1=rs[:, ec::4, tsl],
                                                                              op0=ALU.mult, op1=ALU.mult),
                              reads=[K("t1", h) for h in range(4)] + [K("rs", c) for c in range(16)] + ["cvec"], writes=XS_KEYS + [K("og", tc)])

                G(0)
                for tc in range(4):
                    if tc < 3:
                        G(tc + 1)
                    B(tc)
                outproj_ln(lambda kc, nt: (og[:, kc, :], f"xs{kc // 4}"), 16, Wout, tok0, 1, xres, x32_out, xb_out, (l * 2) * 16, okey, wfn=lambda oc: getw(48 + oc))

        def dil_phase(l, xb_in, xres, x32_out, xb_out, okey):
            j = l // 2
            S.barrier()
            al = Alloc(arena, PHASE_BASE)
            xs = al.get([16, 2048], BF16)
            og = al.get([8, 2048], BF16)
            acc2 = al.get([2, 2048], F32)
            qTb = [al.get([2048], BF16) for _ in range(2)]
            kTb = [al.get([2048], BF16) for _ in range(2)]
            vT = al.get([2048], BF16)
            vtokb = [al.get([16, 128], BF16) for _ in range(2)]
            pexp = [al.get([2, 2, 128], BF16) for _ in range(2)]
            PTb = [al.get([2, 2, 128], BF16) for _ in range(2)]
            Win = dil_w_in[j]
            Wout = dil_w_out[j]
            ps3bf = ps[3][:].bitcast(BF16)
            cnt = {"hg": 0, "blk": 0}
            load_xs(xs, xb_in, 0, 2048)
            for s in range(NSEQ):
                tokS = s * SEQ
                def make_iter(h, gi, dd, hg):
                    qT, kT, vtok = qTb[hg], kTb[hg], vtokb[hg]
                    qk_, kk_, vk_, vtk = K("q", hg), K("k", hg), "vT", K("vtok", hg)
                    nb = 16 // dd
                    wst = {}

                    def tsl(blk):
                        r, bb = blk // nb, blk % nb
                        return slice(r + dd * 128 * bb, r + dd * 128 * bb + dd * 127 + 1, dd)

                    def grp(t, nt):
                        if t == 0 and nt == 0:
                            for tt_ in range(3):
                                wst[tt_] = wload(Win, 0, 16, ((gi * 3 + tt_) * 8 + h) * 128, 128)
                        wv, wk = wst[t]
                        b, bk = mainbank(3)
                        for kc in range(16):
                            S.pe(lambda e, b=b, wv=wv, kc=kc, nt=nt: e.matmul(ps[b][:], wv[:, kc, :], xs[:, kc, nt * 512:(nt + 1) * 512], start=(kc == 0), stop=(kc == 15)),
                                 reads=[wk, f"xs{kc // 4}"], writes=[bk])
                        nsl = slice(nt * 512, (nt + 1) * 512)
                        if t == 0:
                            S.act(lambda e, b=b, nsl=nsl: e.activation(out=qT[:, nsl], in_=ps[b][:], func=AF.Identity, scale=float(128 ** -0.5)),
                                  reads=[bk], writes=[qk_])
                        elif t == 1:
                            S.dve(lambda e, b=b, nsl=nsl: e.tensor_copy(out=kT[:, nsl], in_=ps[b][:]), reads=[bk], writes=[kk_])
                        else:
                            S.act(lambda e, b=b, nsl=nsl: e.copy(out=vT[:, nsl], in_=ps[b][:]), reads=[bk], writes=[vk_])
                    groups = [(lambda t=t, nt=nt: grp(t, nt)) for t in range(3) for nt in range(4)]

                    def trans():
                        for half in range(2):
                            for bi in range(8):
                                vsl = tsl(half * 8 + bi)
                                S.pe(lambda e, bi=bi, vsl=vsl: e.transpose(ps3bf[:, bi * 128:(bi + 1) * 128], vT[:, vsl], ident), reads=[vk_, "cb16"], writes=["ps3"])
                            S.dve(lambda e, half=half: e.tensor_copy(out=vtok[:, half * 8:(half + 1) * 8, :], in_=ps3bf.rearrange("p (a b) -> p a b", b=128)),
                                  reads=["ps3"], writes=[vtk])

                    def att_scores(pr):
                        ci = cnt["blk"] % 2
                        cnt["blk"] += 1
                        sb = 4 + ci
                        psS = ps[sb][:].rearrange("p (k a b) -> p k a b", a=2, b=128)
                        infos = []
                        for kq in range(2):
                            blk = 2 * pr + kq
                            bb = blk % nb
                            cs = tsl(blk)
                            psl = tsl(blk - 1) if bb > 0 else cs
                            S.pe(lambda e, psS=psS, psl=psl, cs=cs, kq=kq: e.matmul(psS[:, kq, 0, :], kT[:, psl], qT[:, cs], start=True, stop=True),
                                 reads=[kk_, qk_], writes=[f"ps{sb}"])
                            S.pe(lambda e, psS=psS, cs=cs, kq=kq: e.matmul(psS[:, kq, 1, :], kT[:, cs], qT[:, cs], start=True, stop=True),
                                 reads=[kk_, qk_], writes=[f"ps{sb}"])
                            infos.append((blk, bb, cs))
                        return (ci, infos)

                    def att_mid(ctx):
                        ci, infos = ctx
                        sb, ob_ = 4 + ci, 6 + ci
                        psS = ps[sb][:].rearrange("p (k a b) -> p k a b", a=2, b=128)
                        pe_, PT = pexp[ci], PTb[ci]
                        S.act(lambda e, psS=psS, pe_=pe_: e.activation(out=pe_, in_=psS, func=AF.Exp), reads=[f"ps{sb}"], writes=[K("pexp", ci)])
                        for kq, (blk, bb, cs) in enumerate(infos):
                            msk = M3[:, 1:3, :] if bb > 0 else M3[:, 0:3:2, :]
                            S.dve(lambda e, pe_=pe_, PT=PT, kq=kq, msk=msk: e.tensor_tensor(out=PT[:, kq, :, :], in0=pe_[:, kq, :, :], in1=msk, op=ALU.mult),
                                  reads=[K("pexp", ci), "cm32"], writes=[K("PT", ci, kq)])

                    def att_rest(ctx):
                        ci, infos = ctx
                        sb, ob_ = 4 + ci, 6 + ci
                        psO = ps[ob_][:].rearrange("p (k a b) -> p k a b", a=2, b=128)
                        pe_, PT = pexp[ci], PTb[ci]
                        for kq, (blk, bb, cs) in enumerate(infos):
                            for jj in range(2):
                                bk_ = blk - 1 if (jj == 0 and bb > 0) else blk
                                S.pe(lambda e, psO=psO, PT=PT, jj=jj, bk_=bk_, kq=kq: e.matmul(psO[:, kq, 0, :], vtok[:, bk_, :], PT[:, kq, jj, :], start=(jj == 0), stop=(jj == 1)),
                                     reads=[vtk, K("PT", ci, kq)], writes=[f"ps{ob_}"])
                            for jj in range(2):
                                S.pe(lambda e, psO=psO, PT=PT, jj=jj, kq=kq: e.matmul(psO[:, kq, 1, :], ones1, PT[:, kq, jj, :], start=(jj == 0), stop=(jj == 1)),
                                     reads=["cb16", K("PT", ci, kq)], writes=[f"ps{ob_}"])
                        for kq, (blk, bb, cs) in enumerate(infos):
                            if gi == 0:
                                S.dve(lambda e, psO=psO, cs=cs, kq=kq: e.tensor_copy(out=acc2[:, :, cs], in_=psO[:, kq, :, :]), reads=[f"ps{ob_}"], writes=["acc2"])
                            else:
                                S.dve(lambda e, psO=psO, cs=cs, kq=kq: e.tensor_tensor(out=acc2[:, :, cs], in0=acc2[:, :, cs], in1=psO[:, kq, :, :], op=ALU.add),
                                      reads=[f"ps{ob_}", "acc2"], writes=["acc2"])
                    actx = deque()

                    def att_step(pr):
                        if pr < 8:
                            actx.append(att_scores(pr))
                            att_mid(actx[-1])
                        if pr >= 1:
                            att_rest(actx.popleft())
                    att = [(lambda pr=pr: att_step(pr)) for pr in range(9)]

                    def fin():
                        if gi == 2:
                            S.dve(lambda e: e.reciprocal(out=acc2[:, 1, :], in_=acc2[:, 1, :]), reads=["acc2"], writes=["acc2"])
                            S.dve(lambda e: e.tensor_tensor(out=og[:, h, :], in0=acc2[:, 0, :], in1=acc2[:, 1, :], op=ALU.mult), reads=["acc2"], writes=[K("og", h)])
                    return groups, trans, att, fin

                iters = [(h, gi, dd) for h in range(8) for gi, dd in enumerate((1, 4, 16))]
                prev = None
                for it in range(len(iters) + 1):
                    cur_it = None
                    if it < len(iters):
                        cur_it = make_iter(*iters[it], cnt["hg"] % 2)
                        cnt["hg"] += 1
                    pop_work(4)
                    glist = cur_it[0] if cur_it else []
                    alist = prev[2] if prev else []
                    for i in range(max(len(glist), len(alist))):
                        if i < len(glist):
                            glist[i]()
                        if i < len(alist):
                            alist[i]()
                    if prev:
                        prev[3]()
                    if cur_it:
                        cur_it[1]()
                    prev = cur_it
                if s + 1 < NSEQ:
                    load_xs(xs, xb_in, tokS + SEQ, 2048)
                for half in range(2):
                    outproj_ln(lambda kc, nt, half=half: (og[:, kc, half * 1024 + nt * 512:half * 1024 + (nt + 1) * 512], K("og", kc)),
                               8, Wout, tokS + half * 1024, 2, xres, x32_out, xb_out, (l * 2) * 16, okey)

        cur = None
        for sidx in range(NSUB):
            l = sidx // 2
            if sidx == 2 and NSUB > 4:
                pending_work.extend(gla_conv_ops(1))
            last = (sidx == NSUB - 1)
            xres = xT if sidx == 0 else X32
            x32_out = outT if last else X32
            okey = "out" if last else "x32"
            xb_in = xT if cur is None else XB[cur]
            nxt = 0 if cur is None else 1 - cur
            xb_out = XB[nxt]
            if sidx % 2 == 1:
                ffn_phase(l, xb_in, xres, x32_out, xb_out, okey)
            elif l % 2 == 0:
                gla_phase(l, xb_in, xres, x32_out, xb_out, okey)
            else:
                dil_phase(l, xb_in, xres, x32_out, xb_out, okey)
            cur = nxt
        drain_work()
        S.barrier()
        S.op("sp", lambda e: e.nop())
        S.emit()
    return nc


def host_consts(inputs):
    f = np.float32
    cvec = np.zeros((128, NCV), f)
    cvec[:, C_LNG:C_LNG + 128] = np.asarray(inputs["ln_g"], f).reshape(8, 16, 128).transpose(2, 0, 1).reshape(128, 128)
    cvec[:, C_LNB:C_LNB + 128] = np.asarray(inputs["ln_b"], f).reshape(8, 16, 128).transpose(2, 0, 1).reshape(128, 128)
    cvec[:, C_CW:C_CW + 1032] = np.asarray(inputs["ffn_conv_w"], f).reshape(4, 3, 86, 128).transpose(3, 0, 1, 2).reshape(128, 1032)
    cvec[:, C_CB:C_CB + 344] = np.asarray(inputs["ffn_conv_b"], f).reshape(4, 86, 128).transpose(2, 0, 1).reshape(128, 344)
    cvec[:, C_NG:C_NG + 8] = np.asarray(inputs["gla_norm_g"], f).reshape(2, 4, 128).transpose(2, 0, 1).reshape(128, 8)
    cmat = np.zeros((128, 1024), f)
    ii = np.arange(128)
    le = (ii[:, None] <= ii[None, :]).astype(f)
    ge = (ii[:, None] >= ii[None, :]).astype(f)
    cmat[:, 0:128] = le * f(-1.0 / 16.0)
    cmat[:, 256:384] = ge
    cmat[:, 384:512] = le
    cmat[:, 512:640] = np.eye(128, dtype=f)
    cmat[:, 640:768] = f(1.0 / 2048.0)
    cmat[:, 768:896] = f(1.0 / 512.0)
    cmat[:, 896:1024] = f(1.0)
    wg = np.concatenate([np.asarray(inputs["gla_w_gate_up"], f), np.asarray(inputs["gla_gate_bias"], f)[:, None, :]], axis=1)
    return cvec, cmat, np.ascontiguousarray(wg)


_PROG = {}


def run(inputs, ncores=NCORES, nseq=NSEQ_CORE, nsub=8, trace=False):
    key = (nseq, nsub)
    if key not in _PROG:
        _PROG[key] = build(nseq, nsub)
    nc = _PROG[key]
    cvec, cmat, wg = host_consts(inputs)
    x = np.asarray(inputs["x"], np.float32)
    shared = {k: np.ascontiguousarray(np.asarray(inputs[k], np.float32)) for k in
              ("gla_w_in", "gla_w_out", "dil_w_in", "dil_w_out", "ffn_w_up", "ffn_w_down")}
    in_maps = []
    for c in range(ncores):
        xc = x[c * nseq:(c + 1) * nseq].reshape(nseq * SEQ, D)
        m = dict(shared)
        m.update(xT=np.ascontiguousarray(xc.T), cvec=cvec, cmat=cmat, wg_aug=wg)
        in_maps.append(m)
    res = run_bass_kernel_spmd(nc, in_maps, core_ids=list(range(ncores)), **({"trace": True} if trace else {}))
    out = np.empty((ncores * nseq, SEQ, D), np.float32)
    for c in range(ncores):
        oT = np.asarray(res.results[c]["outT"])
        out[c * nseq:(c + 1) * nseq] = oT.T.reshape(nseq, SEQ, D)
    return out, res


def kernel(**inputs):
    out, _ = run(inputs)
    return out
```
